# Optimizing a Trainium2 kernel written in Bass

```python
import math
import jax, jax.numpy as jnp
from jax import lax
import numpy as np

D_MODEL = 1024
BATCH = 8
SEQ = 8192
DEPTH = 2

RMS_EPS = 1e-6
N_BRANCH = 3

S5_WIDTH = D_MODEL // 2
S5_GROUP = 16
S5_GROUPS = S5_WIDTH // S5_GROUP
S5_STATE = 64
S5_STEP_MIN = 1e-3
S5_STEP_MAX = 1e-1

ATT_HEAD_DIM = 64
ATT_PAIRS = ((128, 1), (512, 4), (2048, 16))
ATT_HEADS_PER_GROUP = 4
ATT_HEADS = len(ATT_PAIRS) * ATT_HEADS_PER_GROUP
ATT_WIDTH = ATT_HEADS * ATT_HEAD_DIM
ATT_OUT_WIDTH = ATT_HEADS_PER_GROUP * ATT_HEAD_DIM
ATT_BLOCK = 128

SSD_HEAD_DIM = 64
SSD_WIDTH = 3 * D_MODEL // 4
SSD_HEADS = SSD_WIDTH // SSD_HEAD_DIM
SSD_GROUPS = 2
SSD_STATE = 128
SSD_CONV = 4
SSD_CHUNK = 128
SSD_CONV_DIM = SSD_WIDTH + 2 * SSD_GROUPS * SSD_STATE
SSD_DT_MIN = 1e-3
SSD_DT_MAX = 1e-1

IN_SPLITS = (S5_WIDTH, S5_WIDTH,
             ATT_WIDTH, ATT_WIDTH, ATT_WIDTH, ATT_OUT_WIDTH,
             SSD_CONV_DIM, SSD_HEADS, SSD_WIDTH,
             N_BRANCH * D_MODEL)
IN_WIDTH = sum(IN_SPLITS)

kernel_name = "hybrid_s5_dilated_attn_ssd_gated_merge"


def rms_norm(x, w):
    xf = x.astype(jnp.float32)
    y = xf * lax.rsqrt(jnp.mean(xf * xf, axis=-1, keepdims=True) + RMS_EPS)
    return y * w.astype(jnp.float32)


def s5_mixer(u, a_re, a_im, log_step, b_re, b_im, c_re, c_im, d, glu_w, glu_b):
    f32 = jnp.float32
    a_re, a_im = a_re.astype(f32), a_im.astype(f32)
    b_re, b_im = b_re.astype(f32), b_im.astype(f32)
    c_re, c_im = c_re.astype(f32), c_im.astype(f32)
    Bt, S, _ = u.shape
    ug = u.reshape(Bt, S, S5_GROUPS, S5_GROUP)
    step = jnp.exp(log_step.astype(f32))[:, None]
    mag = jnp.exp(a_re * step)
    ang = a_im * step
    lam_re, lam_im = mag * jnp.cos(ang), mag * jnp.sin(ang)
    num_re, num_im = lam_re - 1.0, lam_im
    den = a_re * a_re + a_im * a_im
    f_re = (num_re * a_re + num_im * a_im) / den
    f_im = (num_im * a_re - num_re * a_im) / den
    bb_re = f_re[..., None] * b_re - f_im[..., None] * b_im
    bb_im = f_re[..., None] * b_im + f_im[..., None] * b_re
    bu_re = jnp.einsum('gpi,bsgi->bsgp', bb_re, ug)
    bu_im = jnp.einsum('gpi,bsgi->bsgp', bb_im, ug)
    lam_re_t = jnp.broadcast_to(lam_re, (1, S) + lam_re.shape)
    lam_im_t = jnp.broadcast_to(lam_im, (1, S) + lam_im.shape)

    def combine(left, right):
        ar_l, ai_l, br_l, bi_l = left
        ar_r, ai_r, br_r, bi_r = right
        return (ar_r * ar_l - ai_r * ai_l,
                ar_r * ai_l + ai_r * ar_l,
                ar_r * br_l - ai_r * bi_l + br_r,
                ar_r * bi_l + ai_r * br_l + bi_r)

    _, _, h_re, h_im = lax.associative_scan(combine, (lam_re_t, lam_im_t, bu_re, bu_im), axis=1)
    y = jnp.einsum('gip,bsgp->bsgi', c_re, h_re) - jnp.einsum('gip,bsgp->bsgi', c_im, h_im)
    y = y.reshape(Bt, S, S5_WIDTH) + d.astype(f32) * u
    g = jax.nn.gelu(y)
    return g * jax.nn.sigmoid(g @ glu_w + glu_b)


def dilated_window_attention(q, k, v, window, dilation):
    Bt, S, H, Dh = q.shape
    span = window // dilation
    seg = dilation * ATT_BLOCK
    S_pad = -(-S // seg) * seg
    L = S_pad // dilation
    nb = L // ATT_BLOCK

    def to_strided(t):
        t = jnp.pad(t, ((0, 0), (0, S_pad - S), (0, 0), (0, 0)))
        t = t.reshape(Bt, L, dilation, H, Dh).transpose(0, 2, 1, 3, 4)
        return t.reshape(Bt, dilation, nb, ATT_BLOCK, H, Dh)

    def with_prev(t):
        prev = jnp.pad(t[:, :, :-1], ((0, 0), (0, 0), (1, 0), (0, 0), (0, 0), (0, 0)))
        return jnp.concatenate([prev, t], axis=3)

    qb = to_strided(q)
    kb = with_prev(to_strided(k))
    vb = with_prev(to_strided(v))
    s = jnp.einsum('brnqhd,brnkhd->brnhqk', qb, kb) * (Dh ** -0.5)
    qi = jnp.arange(ATT_BLOCK)[:, None] + ATT_BLOCK
    kj = jnp.arange(2 * ATT_BLOCK)[None, :]
    band = (qi - kj >= 0) & (qi - kj <= span)
    has_prev = (jnp.arange(nb) > 0)[:, None, None] | (kj >= ATT_BLOCK)[None]
    mask = band[None] & has_prev
    s = jnp.where(mask[None, None, :, None], s, -jnp.inf)
    m = jnp.max(s, axis=-1, keepdims=True)
    p = jnp.exp(s - m)
    l = jnp.sum(p, axis=-1, keepdims=True)
    o = jnp.einsum('brnhqk,brnkhd->brnqhd', p / l, vb)
    lse = (m + jnp.log(l))[..., 0]
    o = o.reshape(Bt, dilation, L, H, Dh).transpose(0, 2, 1, 3, 4).reshape(Bt, S_pad, H, Dh)[:, :S]
    lse = lse.transpose(0, 1, 2, 4, 3).reshape(Bt, dilation, L, H)
    lse = lse.transpose(0, 2, 1, 3).reshape(Bt, S_pad, H)[:, :S]
    return o, lse


def attention_mixer(q, k, v, q_norm_w, k_norm_w):
    Bt, S = q.shape[:2]
    q = rms_norm(q, q_norm_w)
    k = rms_norm(k, k_norm_w)
    v = v.astype(jnp.float32)
    outs, lses = [], []
    for g, (window, dilation) in enumerate(ATT_PAIRS):
        sl = slice(g * ATT_HEADS_PER_GROUP, (g + 1) * ATT_HEADS_PER_GROUP)
        o, l = dilated_window_attention(q[:, :, sl], k[:, :, sl], v[:, :, sl], window, dilation)
        outs.append(o)
        lses.append(l)
    o = jnp.stack(outs, axis=0)
    alpha = jax.nn.softmax(jnp.stack(lses, axis=0), axis=0)
    y = jnp.sum(alpha[..., None] * o, axis=0)
    return y.reshape(Bt, S, ATT_OUT_WIDTH)


def segsum(a):
    T = a.shape[-1]
    cs = jnp.cumsum(a, axis=-1)
    diff = cs[..., :, None] - cs[..., None, :]
    return jnp.where(jnp.tril(jnp.ones((T, T), dtype=bool)), diff, -jnp.inf)


def causal_depthwise_conv(x, w, b):
    y = lax.conv_general_dilated(x, w.astype(x.dtype)[:, None, :], window_strides=(1,),
                                 padding=((SSD_CONV - 1, 0),),
                                 dimension_numbers=('NWC', 'WIO', 'NWC'),
                                 feature_group_count=x.shape[-1])
    return y + b


def ssd_mixer(xbc, dt, z, conv_w, conv_b, dt_bias, a_log, d, norm_w):
    f32 = jnp.float32
    Bt, S, _ = xbc.shape
    E = SSD_HEADS // SSD_GROUPS
    nc = S // SSD_CHUNK
    xbc = jax.nn.silu(causal_depthwise_conv(xbc, conv_w, conv_b))
    xs, bm, cm = jnp.split(xbc, [SSD_WIDTH, SSD_WIDTH + SSD_GROUPS * SSD_STATE], axis=-1)
    xs = xs.reshape(Bt, nc, SSD_CHUNK, SSD_GROUPS, E, SSD_HEAD_DIM)
    bm = bm.reshape(Bt, nc, SSD_CHUNK, SSD_GROUPS, SSD_STATE)
    cm = cm.reshape(Bt, nc, SSD_CHUNK, SSD_GROUPS, SSD_STATE)
    dt = jax.nn.softplus(dt + dt_bias.astype(f32))
    a = -jnp.exp(a_log.astype(f32))
    dt_c = dt.reshape(Bt, nc, SSD_CHUNK, SSD_GROUPS, E)
    a_dt = (dt_c * a.reshape(SSD_GROUPS, E)).transpose(0, 3, 4, 1, 2)
    xdt = xs * dt_c[..., None]
    a_cs = jnp.cumsum(a_dt, axis=-1)
    decay_in = jnp.exp(segsum(a_dt))
    cb = jnp.einsum('bclgn,bcsgn->bgcls', cm, bm)
    y_diag = jnp.einsum('bgcls,bgecls,bcsgep->bclgep', cb, decay_in, xdt)
    decay_st = jnp.exp(a_cs[..., -1:] - a_cs)
    states = jnp.einsum('bclgn,bgecl,bclgep->bcgepn', bm, decay_st, xdt)
    states = jnp.concatenate([jnp.zeros_like(states[:, :1]), states], axis=1)
    chunk_a = jnp.pad(a_cs[..., -1], ((0, 0), (0, 0), (0, 0), (1, 0)))
    decay_chunk = jnp.exp(segsum(chunk_a))
    states = jnp.einsum('bgezc,bcgepn->bzgepn', decay_chunk, states)[:, :-1]
    y_off = jnp.einsum('bclgn,bcgepn,bgecl->bclgep', cm, states, jnp.exp(a_cs))
    y = y_diag + y_off + xs * d.astype(f32).reshape(SSD_GROUPS, E)[:, :, None]
    y = y.reshape(Bt, S, SSD_WIDTH)
    return rms_norm(y * jax.nn.silu(z), norm_w)


def hybrid_layer(x, norm_w, w_in, s5_a_re, s5_a_im, s5_log_step, s5_b_re, s5_b_im, s5_c_re,
                 s5_c_im, s5_d, s5_glu_w, s5_glu_b, q_norm_w, k_norm_w, conv_w, conv_b,
                 dt_bias, ssd_a_log, ssd_d, ssd_norm_w, proj_a, proj_b, proj_c, w_out):
    Bt, S, _ = x.shape
    h = rms_norm(x, norm_w)
    proj = h @ w_in
    (u_a, z_a, q, k, v, z_b, xbc, dt, z_c, gate_logits) = jnp.split(
        proj, np.cumsum(IN_SPLITS)[:-1].tolist(), axis=-1)
    y_a = s5_mixer(u_a, s5_a_re, s5_a_im, s5_log_step, s5_b_re, s5_b_im, s5_c_re, s5_c_im,
                   s5_d, s5_glu_w, s5_glu_b) * jax.nn.silu(z_a)
    hd = (Bt, S, ATT_HEADS, ATT_HEAD_DIM)
    y_b = attention_mixer(q.reshape(hd), k.reshape(hd), v.reshape(hd),
                          q_norm_w, k_norm_w) * jax.nn.silu(z_b)
    y_c = ssd_mixer(xbc, dt, z_c, conv_w, conv_b, dt_bias, ssd_a_log, ssd_d, ssd_norm_w)
    gates = jax.nn.sigmoid(gate_logits).reshape(Bt, S, N_BRANCH, D_MODEL)
    merged = (gates[:, :, 0] * (y_a @ proj_a)
              + gates[:, :, 1] * (y_b @ proj_b)
              + gates[:, :, 2] * (y_c @ proj_c))
    return x + (merged @ w_out).astype(x.dtype)


def setup_inputs(seed: int = 0) -> dict:
    key = jax.random.key(seed)
    ks = jax.random.split(key, 32)
    L = DEPTH
    nrm = jax.random.normal
    P, I, G = S5_STATE, S5_GROUP, S5_GROUPS
    x = nrm(ks[0], (BATCH, SEQ, D_MODEL), jnp.float32)
    norm_w = 1.0 + 0.02 * nrm(ks[1], (L, D_MODEL))
    w_in = nrm(ks[2], (L, D_MODEL, IN_WIDTH)) * D_MODEL ** -0.5
    s5_a_re = -0.5 + 0.01 * nrm(ks[3], (L, G, P))
    s5_a_im = math.pi * jnp.arange(P, dtype=jnp.float32) + 0.01 * nrm(ks[4], (L, G, P))
    s5_log_step = jax.random.uniform(ks[5], (L, G), minval=math.log(S5_STEP_MIN),
                                     maxval=math.log(S5_STEP_MAX))
    s5_b_re = nrm(ks[6], (L, G, P, I)) * (2 * I) ** -0.5
    s5_b_im = nrm(ks[7], (L, G, P, I)) * (2 * I) ** -0.5
    s5_c_re = nrm(ks[8], (L, G, I, P)) * (2 * P) ** -0.5
    s5_c_im = nrm(ks[9], (L, G, I, P)) * (2 * P) ** -0.5
    s5_d = nrm(ks[10], (L, S5_WIDTH))
    s5_glu_w = nrm(ks[11], (L, S5_WIDTH, S5_WIDTH)) * S5_WIDTH ** -0.5
    s5_glu_b = 0.01 * nrm(ks[12], (L, S5_WIDTH))
    q_norm_w = 1.0 + 0.02 * nrm(ks[13], (L, ATT_HEAD_DIM))
    k_norm_w = 1.0 + 0.02 * nrm(ks[14], (L, ATT_HEAD_DIM))
    conv_w = nrm(ks[15], (L, SSD_CONV, SSD_CONV_DIM)) * SSD_CONV ** -0.5
    conv_b = 0.01 * nrm(ks[16], (L, SSD_CONV_DIM))
    dt0 = jnp.exp(jax.random.uniform(ks[17], (L, SSD_HEADS), minval=math.log(SSD_DT_MIN),
                                     maxval=math.log(SSD_DT_MAX)))
    dt_bias = dt0 + jnp.log(-jnp.expm1(-dt0))
    ssd_a_log = jnp.log(jax.random.uniform(ks[18], (L, SSD_HEADS), minval=1.0, maxval=16.0))
    ssd_d = 1.0 + 0.1 * nrm(ks[19], (L, SSD_HEADS))
    ssd_norm_w = 1.0 + 0.02 * nrm(ks[20], (L, SSD_WIDTH))
    proj_a = nrm(ks[21], (L, S5_WIDTH, D_MODEL)) * S5_WIDTH ** -0.5
    proj_b = nrm(ks[22], (L, ATT_OUT_WIDTH, D_MODEL)) * ATT_OUT_WIDTH ** -0.5
    proj_c = nrm(ks[23], (L, SSD_WIDTH, D_MODEL)) * SSD_WIDTH ** -0.5
    w_out = nrm(ks[24], (L, D_MODEL, D_MODEL)) * (0.5 * D_MODEL ** -0.5)
    return {"x": x, "norm_w": norm_w, "w_in": w_in,
            "s5_a_re": s5_a_re, "s5_a_im": s5_a_im, "s5_log_step": s5_log_step,
            "s5_b_re": s5_b_re, "s5_b_im": s5_b_im, "s5_c_re": s5_c_re, "s5_c_im": s5_c_im,
            "s5_d": s5_d, "s5_glu_w": s5_glu_w, "s5_glu_b": s5_glu_b,
            "q_norm_w": q_norm_w, "k_norm_w": k_norm_w,
            "conv_w": conv_w, "conv_b": conv_b, "dt_bias": dt_bias,
            "ssd_a_log": ssd_a_log, "ssd_d": ssd_d, "ssd_norm_w": ssd_norm_w,
            "proj_a": proj_a, "proj_b": proj_b, "proj_c": proj_c, "w_out": w_out}


def reference(x, norm_w, w_in, s5_a_re, s5_a_im, s5_log_step, s5_b_re, s5_b_im, s5_c_re,
              s5_c_im, s5_d, s5_glu_w, s5_glu_b, q_norm_w, k_norm_w, conv_w, conv_b,
              dt_bias, ssd_a_log, ssd_d, ssd_norm_w, proj_a, proj_b, proj_c, w_out):
    for i in range(DEPTH):
        x = hybrid_layer(x, norm_w[i], w_in[i], s5_a_re[i], s5_a_im[i], s5_log_step[i],
                         s5_b_re[i], s5_b_im[i], s5_c_re[i], s5_c_im[i], s5_d[i],
                         s5_glu_w[i], s5_glu_b[i], q_norm_w[i], k_norm_w[i], conv_w[i],
                         conv_b[i], dt_bias[i], ssd_a_log[i], ssd_d[i], ssd_norm_w[i],
                         proj_a[i], proj_b[i], proj_c[i], w_out[i])
    return x
```

```python
from contextlib import ExitStack
import numpy as np
import concourse.bass as bass
import concourse.mybir as mybir

F32 = mybir.dt.float32
BF16 = mybir.dt.bfloat16
AF = mybir.ActivationFunctionType
ALU = mybir.AluOpType
AX = mybir.AxisListType

ENGS = ["pe", "act", "dve", "pool", "sp"]
NDMA = 40
SAME_SYNC = {"act", "dve", "pool"}


class Buf:
    __slots__ = ("name", "w", "r")

    def __init__(self, name):
        self.name = name
        self.w = None
        self.r = {}


class Tl:
    def __init__(self, t, name):
        self.t = t
        self.name = name
        self.bufs = {}

    def b(self, key=None):
        if key not in self.bufs:
            self.bufs[key] = Buf("%s/%s" % (self.name, key))
        return self.bufs[key]

    def __getitem__(self, idx):
        return self.t[idx]


class Prog:
    def __init__(self, nc):
        self.nc = nc
        self.ops = {e: [] for e in ENGS}
        self.cnt = {e: 0 for e in ENGS}
        self.waited = {e: {} for e in ENGS}
        self.dma_uses = [0] * NDMA
        self.dma_rr = 0
        self.stack = ExitStack()
        self.pstacks = []
        self.n_ops = 0

    def _ctx(self, persistent):
        return self.stack if (persistent or not self.pstacks) else self.pstacks[-1]

    def sb(self, name, shape, dt=F32, persistent=False):
        self.n_ops += 1
        name = "sb%d_%s" % (self.n_ops, name)
        t = self._ctx(persistent).enter_context(self.nc.sbuf_tensor(name, list(shape), dt))
        return Tl(t, name)

    def ps(self, name, shape, dt=F32, persistent=False):
        self.n_ops += 1
        name = "ps%d_%s" % (self.n_ops, name)
        t = self._ctx(persistent).enter_context(self.nc.psum_tensor(name, list(shape), dt))
        return Tl(t, name)

    def dram(self, name, shape, dt, kind="Internal"):
        t = self.nc.dram_tensor(name, list(shape), dt, kind=kind)
        return Tl(t, name)

    def begin_phase(self):
        self.pstacks.append(ExitStack())

    def end_phase(self):
        self.barrier()
        self.pstacks.pop().close()

    def op(self, eng, fn, reads=(), writes=(), dma=False):
        deps = {}
        for b in reads:
            if b.w is not None:
                k, v = b.w
                deps[k] = max(deps.get(k, 0), v)
        for b in writes:
            if b.w is not None:
                k, v = b.w
                deps[k] = max(deps.get(k, 0), v)
            for k, v in b.r.items():
                deps[k] = max(deps.get(k, 0), v)
        waits = []
        wd = self.waited[eng]
        for k, v in deps.items():
            if k == ("eng", eng) and eng not in SAME_SYNC:
                continue
            if wd.get(k, 0) < v:
                wd[k] = v
                waits.append((k, v))
        if dma:
            j = self.dma_rr
            self.dma_rr = (j + 1) % NDMA
            prev = self.dma_uses[j] * 16
            k = ("dma", j)
            if prev and wd.get(k, 0) < prev:
                wd[k] = prev
                waits.append((k, prev))
            self.dma_uses[j] += 1
            tok = (k, self.dma_uses[j] * 16)
        else:
            self.cnt[eng] += 1
            tok = (("eng", eng), self.cnt[eng])
        self.ops[eng].append((waits, fn, tok))
        self.n_ops += 1
        for b in reads:
            k, v = tok
            b.r[k] = max(b.r.get(k, 0), v)
        for b in writes:
            b.w = tok
            b.r = {}
        return tok

    def barrier(self):
        targets = {}
        for e in ENGS:
            if self.cnt[e]:
                targets[("eng", e)] = self.cnt[e]
        for j in range(NDMA):
            if self.dma_uses[j]:
                targets[("dma", j)] = self.dma_uses[j] * 16
        for e in ENGS:
            waits = []
            for k, v in targets.items():
                if k == ("eng", e):
                    continue
                if self.waited[e].get(k, 0) < v:
                    self.waited[e][k] = v
                    waits.append((k, v))
            if waits:
                self.ops[e].append((waits, None, None))

    def emit(self):
        nc = self.nc
        with ExitStack() as es:
            sems = {}
            for e in ENGS:
                sems[("eng", e)] = es.enter_context(nc.semaphore("s_" + e))
            for j in range(NDMA):
                sems[("dma", j)] = es.enter_context(nc.semaphore("d_%d" % j))
            block = es.enter_context(nc.Block())

            def run(engname, eng):
                for waits, fn, tok in self.ops[engname]:
                    for k, v in waits:
                        eng.wait_ge(sems[k], v)
                    if fn is None:
                        continue
                    ins = fn(eng)
                    k, v = tok
                    if k[0] == "dma":
                        ins.then_inc(sems[k], 16)
                    else:
                        ins.then_inc(sems[k], 1)

            @block.tensor
            def _(eng):
                run("pe", eng)

            @block.scalar
            def _(eng):
                run("act", eng)

            @block.vector
            def _(eng):
                run("dve", eng)

            @block.gpsimd
            def _(eng):
                run("pool", eng)

            @block.sync
            def _(eng):
                run("sp", eng)

    def dma(self, out, in_, reads, writes, eng="sp", **kw):
        return self.op(eng, lambda e: e.dma_start(out=out, in_=in_, **kw), reads, writes, dma=True)

    def mm(self, out, lhsT, rhs, reads, writes, start=True, stop=True, **kw):
        return self.op("pe", lambda e: e.matmul(out, lhsT, rhs, start=start, stop=stop, **kw), reads, writes)

    def tr(self, out, in_, ident, reads, writes):
        return self.op("pe", lambda e: e.transpose(out, in_, ident), reads, writes)

    def act(self, out, in_, func, reads, writes, eng="act", **kw):
        return self.op(eng, lambda e: e.activation(out, in_, func, **kw), reads, writes)

    def tt(self, out, in0, in1, op, reads, writes, eng="dve"):
        return self.op(eng, lambda e: e.tensor_tensor(out, in0, in1, op), reads, writes)

    def ts(self, out, in0, s1, s2, op0, op1, reads, writes, eng="dve", **kw):
        if op1 is None:
            return self.op(eng, lambda e: e.tensor_scalar(out, in0, s1, None, op0, **kw), reads, writes)
        return self.op(eng, lambda e: e.tensor_scalar(out, in0, s1, s2, op0, op1, **kw), reads, writes)

    def stt(self, out, in0, scalar, in1, op0, op1, reads, writes, eng="dve"):
        return self.op(eng, lambda e: e.scalar_tensor_tensor(out, in0, scalar, in1, op0, op1), reads, writes)

    def cp(self, out, in_, reads, writes, eng="dve"):
        if eng == "act":
            return self.op(eng, lambda e: e.copy(out, in_), reads, writes)
        return self.op(eng, lambda e: e.tensor_copy(out, in_), reads, writes)

    def memset(self, ap, val, writes, eng="dve"):
        return self.op(eng, lambda e: e.memset(ap, val), (), writes)

    def recip(self, out, in_, reads, writes):
        return self.op("dve", lambda e: e.reciprocal(out, in_), reads, writes)


import math
import numpy as np, ml_dtypes
from concourse.bass_utils import run_bass_kernel_spmd

D = 1024
INW = 8716
C_UA, C_ZA, C_Q, C_K, C_V, C_ZB, C_XBC, C_DT, C_ZC, C_GATE = 0, 512, 1024, 1792, 2560, 3328, 3584, 4864, 4876, 5644
EPS = 1e-6
DIL = (1, 4, 16)
NEGV = -1.0e5


def load_w(P, src_ap, dst, ncols, kt, stg, tag):
    v = src_ap.rearrange("(k p) c -> p k c", p=128)
    i = 0
    for c0 in range(0, ncols, 512):
        c1 = min(ncols, c0 + 512)
        st = stg[i % 2]
        P.dma(st[:, 0:kt, 0:c1 - c0], v[:, :, c0:c1], [], [st.b()])
        P.cp(dst[:, :, c0:c1], st[:, 0:kt, 0:c1 - c0], [st.b()], [dst.b()], eng=("act" if i % 2 else "dve"))
        i += 1


def build(S, L=2, debug=None):
    nc = bass.Bass("TRN2", target_bir_lowering=False)
    P = Prog(nc)
    NT = S // 128
    NB = S // 512
    NSB = S // 2048
    ein = lambda n, sh, dt=F32: P.dram(n, sh, dt, kind="ExternalInput")
    x_d = ein("x", [S, D])
    out_d = P.dram("out", [S, D], F32, kind="ExternalOutput")
    w_in_d = ein("w_in", [L, D, INW])
    nw_d = ein("nw", [L, 128, 8])
    s5par_d = ein("s5par", [L, 64, 3, 32])
    s5b_d = ein("s5b", [L, 64, 2, 32, 16])
    s5c_d = ein("s5c", [L, 64, 2, 32, 16])
    s5d_d = ein("s5d", [L, 128, 32])
    gluw_d = ein("gluw", [L, 512, 512])
    glub_d = ein("glub", [L, 128, 4])
    qkw_d = ein("qkw", [L, 128, 2])
    bd_d = ein("c_bdones", [128, 128], BF16)
    convw_d = ein("convw", [L, 128, 10, 4])
    convb_d = ein("convb", [L, 128, 10])
    ssdv_d = ein("ssdv", [L, 128, 3, 12])
    ssdd_d = ein("ssdd", [L, 128, 6])
    ssdnw_d = ein("ssdnw", [L, 128, 6])
    pa_d = ein("proj_a", [L, 512, D])
    pb_d = ein("proj_b", [L, 256, D])
    pc_d = ein("proj_c", [L, 768, D])
    wo_d = ein("w_out", [L, D, D])
    idb_d = ein("c_identb", [128, 128], BF16)
    idf_d = ein("c_identf", [128, 128])
    onesb_d = ein("c_onesb", [128, 128], BF16)
    onesf_d = ein("c_onesf", [128, 128])
    utri_d = ein("c_utri", [128, 128])
    neg4_d = ein("c_neg4", [128, 4, 128])
    amask_d = ein("c_amask", [128, 2, 128], BF16)
    bmask_d = ein("c_bmask", [128, 128])
    selT_d = ein("c_selT", [128, 8, 8, 128], BF16)
    sel_d = ein("c_sel", [128, 8, 8, 128], BF16)
    xmid_d = P.dram("xmid", [S, D], F32)
    hT_d = P.dram("hT", [8, 128, S], BF16)
    yTa_d = P.dram("yTa", [4, 128, S], BF16)
    yTb_d = P.dram("yTb", [4, 64, S], BF16)
    yTc_d = P.dram("yTc", [6, 128, S], BF16)
    num_d = P.dram("attnum", [3, 4, 64, S], F32)
    den_d = P.dram("attden", [3, 4, 64, S], F32)

    def cload(name, d, shape, dt=F32):
        t = P.sb(name, shape, dt, persistent=True)
        P.dma(t[:], d[:], [], [t.b()])
        return t

    identb = cload("identb", idb_d, [128, 128], BF16)
    identf = cload("identf", idf_d, [128, 128])
    onesb = cload("onesb", onesb_d, [128, 128], BF16)
    onesf = cload("onesf", onesf_d, [128, 128])

    def w_in_cols(l, c0, n):
        return w_in_d.t.ap()[l, :, c0:c0 + n]

    hT_v = hT_d.t.ap().rearrange("c p t -> p c t")

    for l in range(L):
        xin = x_d if l == 0 else xmid_d
        xout = out_d if l == L - 1 else xmid_d
        P.begin_phase()
        nw = P.sb("nw_sb", [128, 8])
        P.dma(nw[:], nw_d[l], [], [nw.b()])
        xts = [P.sb("xt%d" % i, [128, D]) for i in range(2)]
        junk = P.sb("junk", [128, D], BF16)
        xn = [P.sb("xn%d" % i, [128, D], BF16) for i in range(2)]
        ss = P.sb("ss", [128, 4])
        pts = [P.ps("pt%d" % i, [128, 8, 128], BF16) for i in range(2)]
        hts = [P.sb("hts%d" % i, [128, 8, 512], BF16) for i in range(2)]
        for i in range(NT):
            xt = xts[i % 2]
            P.dma(xt[:], xin[i * 128:(i + 1) * 128, :], [xin.b()], [xt.b()])
            P.act(junk[:], xt[:], AF.Square, [xt.b()], [junk.b(), ss.b()], accum_out=ss[:, 0:1])
            P.act(ss[:, 1:2], ss[:, 0:1], AF.Sqrt, [ss.b()], [ss.b()], scale=1.0 / D, bias=EPS)
            P.recip(ss[:, 2:3], ss[:, 1:2], [ss.b()], [ss.b()])
            x_n = xn[i % 2]
            P.act(x_n[:], xt[:], AF.Copy, [xt.b(), ss.b()], [x_n.b()], scale=ss[:, 2:3])
            pt = pts[i % 2]
            for c in range(8):
                P.tr(pt[:, c, :], x_n[:, c * 128:(c + 1) * 128], identb[:], [x_n.b()], [pt.b()])
            ht = hts[(i // 4) % 2]
            q = i % 4
            P.tt(ht[:, :, q * 128:(q + 1) * 128], pt[:], nw[:, :, None].to_broadcast([128, 8, 128]), ALU.mult,
                 [pt.b(), nw.b()], [ht.b()])
            if q == 3:
                t0 = (i // 4) * 512
                P.dma(hT_v[:, :, t0:t0 + 512], ht[:], [ht.b()], [hT_d.b()])
        P.end_phase()

        if debug == "A":
            break
        phase_s5(P, l, S, locals())
        if debug == "B":
            break
        phase_att(P, l, S, locals())
        if debug == "C":
            break
        phase_ssd(P, l, S, locals())
        if debug == "D":
            break
        phase_merge(P, l, S, locals(), xin, xout)

    P.barrier()
    P.emit()
    P.stack.close()
    return nc


def phase_merge(P, l, S, E, xin, xout):
    NB = S // 512
    identb = E["identb"]
    P.begin_phase()
    wg = P.sb("wg", [128, 8, 3072], BF16)
    wpa = P.sb("wpa", [128, 4, D], BF16)
    wpb = P.sb("wpb", [64, 4, D], BF16)
    wpc = P.sb("wpc", [128, 6, D], BF16)
    wo = P.sb("wo", [128, 8, D], BF16)
    P.begin_phase()
    stg = [P.sb("stg%d" % i, [128, 8, 512]) for i in range(2)]
    load_w(P, E["w_in_cols"](l, C_GATE, 3072), wg, 3072, 8, stg, "wg")
    load_w(P, E["pa_d"].t.ap()[l], wpa, D, 4, stg, "pa")
    vpb = E["pb_d"].t.ap()[l].rearrange("(k p) c -> p k c", p=64)
    for i, c0 in enumerate(range(0, D, 512)):
        st = stg[i % 2]
        P.dma(st[0:64, 0:4, :], vpb[:, :, c0:c0 + 512], [], [st.b()])
        P.cp(wpb[:, :, c0:c0 + 512], st[0:64, 0:4, :], [st.b()], [wpb.b()])
    load_w(P, E["pc_d"].t.ap()[l], wpc, D, 6, stg, "pc")
    load_w(P, E["wo_d"].t.ap()[l], wo, D, 8, stg, "wo")
    P.end_phase()
    hT_v = E["hT_v"]
    yTa_v = E["yTa_d"].t.ap().rearrange("c p t -> p c t")
    yTb_v = E["yTb_d"].t.ap().rearrange("c p t -> p c t")
    yTc_v = E["yTc_d"].t.ap().rearrange("c p t -> p c t")
    hts = [P.sb("mh%d" % i, [128, 8, 512], BF16) for i in range(2)]
    yas = [P.sb("mya%d" % i, [128, 4, 512], BF16) for i in range(2)]
    ybs = [P.sb("myb%d" % i, [64, 4, 512], BF16) for i in range(2)]
    ycs = [P.sb("myc%d" % i, [128, 6, 512], BF16) for i in range(2)]
    mg = [P.sb("mg%d" % i, [128, 8, 512], BF16) for i in range(2)]
    gt = [P.sb("gt%d" % i, [128, 512]) for i in range(2)]
    acc = [P.sb("macc%d" % i, [128, 512]) for i in range(2)]
    pg = [P.ps("pg%d" % i, [128, 512]) for i in range(2)]
    pp = [P.ps("pp%d" % i, [128, 512]) for i in range(2)]
    po = [P.ps("po%d" % i, [128, 512]) for i in range(2)]
    xr = [P.sb("xr%d" % i, [128, D]) for i in range(2)]
    xo = [P.sb("xo%d" % i, [128, D]) for i in range(2)]
    cnt = 0
    for b in range(NB):
        t0 = b * 512
        ht, ya, yb, yc, m = hts[b % 2], yas[b % 2], ybs[b % 2], ycs[b % 2], mg[b % 2]
        P.dma(ht[:], hT_v[:, :, t0:t0 + 512], [E["hT_d"].b()], [ht.b()])
        P.dma(ya[:], yTa_v[:, :, t0:t0 + 512], [E["yTa_d"].b()], [ya.b()])
        P.dma(yb[:], yTb_v[:, :, t0:t0 + 512], [E["yTb_d"].b()], [yb.b()])
        P.dma(yc[:], yTc_v[:, :, t0:t0 + 512], [E["yTc_d"].b()], [yc.b()])
        for dm in range(8):
            a = acc[dm % 2]
            for br in range(3):
                g_ps, p_ps, g_sb = pg[cnt % 2], pp[cnt % 2], gt[cnt % 2]
                cnt += 1
                col = br * D + dm * 128
                for k in range(8):
                    P.mm(g_ps[:], wg[:, k, col:col + 128], ht[:, k, :], [wg.b(), ht.b()], [g_ps.b()], start=(k == 0), stop=(k == 7))
                P.act(g_sb[:], g_ps[:], AF.Sigmoid, [g_ps.b()], [g_sb.b()])
                if br == 0:
                    for k in range(4):
                        P.mm(p_ps[:], wpa[:, k, dm * 128:(dm + 1) * 128], ya[:, k, :], [wpa.b(), ya.b()], [p_ps.b()], start=(k == 0), stop=(k == 3))
                elif br == 1:
                    for k in range(4):
                        P.mm(p_ps[:], wpb[:, k, dm * 128:(dm + 1) * 128], yb[:, k, :], [wpb.b(), yb.b()], [p_ps.b()], start=(k == 0), stop=(k == 3))
                else:
                    for k in range(6):
                        P.mm(p_ps[:], wpc[:, k, dm * 128:(dm + 1) * 128], yc[:, k, :], [wpc.b(), yc.b()], [p_ps.b()], start=(k == 0), stop=(k == 5))
                if br == 0:
                    P.tt(a[:], g_sb[:], p_ps[:], ALU.mult, [g_sb.b(), p_ps.b()], [a.b()])
                elif br == 1:
                    P.tt(g_sb[:], g_sb[:], p_ps[:], ALU.mult, [g_sb.b(), p_ps.b()], [g_sb.b()])
                    P.tt(a[:], a[:], g_sb[:], ALU.add, [a.b(), g_sb.b()], [a.b()], eng="pool")
                else:
                    P.tt(g_sb[:], g_sb[:], p_ps[:], ALU.mult, [g_sb.b(), p_ps.b()], [g_sb.b()])
                    P.tt(m[:, dm, :], a[:], g_sb[:], ALU.add, [a.b(), g_sb.b()], [m.b()], eng="pool")
        for tt_ in range(4):
            xi = b * 4 + tt_
            xr_, xo_ = xr[xi % 2], xo[xi % 2]
            P.dma(xr_[:], xin[xi * 128:(xi + 1) * 128, :], [xin.b()], [xr_.b()])
            for hf in range(2):
                o_ps = po[hf]
                for k in range(8):
                    P.mm(o_ps[:], m[:, k, tt_ * 128:(tt_ + 1) * 128], wo[:, k, hf * 512:(hf + 1) * 512], [m.b(), wo.b()], [o_ps.b()], start=(k == 0), stop=(k == 7))
                P.tt(xo_[:, hf * 512:(hf + 1) * 512], xr_[:, hf * 512:(hf + 1) * 512], o_ps[:], ALU.add, [xr_.b(), o_ps.b()], [xo_.b()])
            P.dma(xout[xi * 128:(xi + 1) * 128, :], xo_[:], [xo_.b()], [xout.b()])
    P.end_phase()


def phase_ssd(P, l, S, E):
    NB = S // 512
    NCK = S // 128
    identb, identf, onesf = E["identb"], E["identf"], E["onesf"]
    P.begin_phase()
    wx = P.sb("wx", [128, 8, 1280], BF16)
    wdt = P.sb("wdt", [128, 8, 12], BF16)
    wzc = P.sb("wzc", [128, 8, 768], BF16)
    P.begin_phase()
    stg = [P.sb("stg%d" % i, [128, 8, 512]) for i in range(2)]
    load_w(P, E["w_in_cols"](l, C_XBC, 1280), wx, 1280, 8, stg, "wx")
    load_w(P, E["w_in_cols"](l, C_DT, 12), wdt, 12, 8, stg, "wdt")
    load_w(P, E["w_in_cols"](l, C_ZC, 768), wzc, 768, 8, stg, "wzc")
    P.end_phase()

    def ld(name, src, shape, dt=F32):
        t = P.sb(name, shape, dt)
        P.dma(t[:], src, [], [t.b()])
        return t
    convw = ld("convw", E["convw_d"][l], [128, 10, 4])
    convb = ld("convb", E["convb_d"][l], [128, 10])
    ssdv = ld("ssdv", E["ssdv_d"][l], [128, 3, 12])
    dcol = ld("dcol", E["ssdd_d"][l], [128, 6])
    nwcol = ld("nwcol", E["ssdnw_d"][l], [128, 6])
    utri = ld("utri", E["utri_d"][:], [128, 128])
    neg4 = ld("neg4", E["neg4_d"][:], [128, 4, 128])
    nega = P.sb("nega", [128, 12])
    P.act(nega[:], ssdv[:, 1, :], AF.Exp, [ssdv.b()], [nega.b()])
    P.ts(nega[:], nega[:], -1.0, None, ALU.mult, None, [nega.b()], [nega.b()])
    H = P.sb("H", [128, 12, 64])
    Hpad = P.sb("Hpad", [128, 12, 128], BF16)
    P.memset(H[:], 0.0, [H.b()])
    P.memset(Hpad[:], 0.0, [Hpad.b()])
    rawx = P.sb("rawx", [128, 10, 515])
    P.memset(rawx[:, :, 0:3], 0.0, [rawx.b()])
    caccs = [P.sb("cacc%d" % i, [128, 512]) for i in range(2)]
    xcs = [P.sb("xc%d" % i, [128, 10, 512], BF16) for i in range(2)]
    zss = [P.sb("zs%d" % i, [128, 6, 512], BF16) for i in range(2)]
    ycb = [P.sb("ycb%d" % i, [128, 6, 512], BF16) for i in range(2)]
    hts = [P.sb("sh%d" % i, [128, 8, 512], BF16) for i in range(1)] * 2
    dtbs = [P.sb("dtb%d" % i, [128, 4, 12]) for i in range(2)]
    nacs = P.sb("nacs", [128, 12])
    wdec = P.sb("wdec", [128, 12])
    uadt = P.sb("uadt", [128, 12, 128])
    xtok = P.sb("xtok", [128, 12, 64], BF16)
    Lm = P.sb("Lm", [128, 12, 128], BF16)
    btoks = [P.sb("btok%d" % i, [128, 256], BF16) for i in range(2)]
    Ealls = [P.sb("Eall%d" % i, [128, 12, 128]) for i in range(2)]
    MTs = [P.sb("MT%d" % i, [128, 12, 128], BF16) for i in range(2)]
    CTss = [P.sb("CTs%d" % i, [128, 12, 128], BF16) for i in range(2)]
    xdds = [P.sb("xdd%d" % i, [128, 12, 64], BF16) for i in range(2)]
    xdtpads = [P.sb("xdtpad%d" % i, [128, 12, 128], BF16) for i in range(2)]
    for t in xdtpads:
        P.memset(t[:], 0.0, [t.b()])
    yf = P.sb("yf", [128, 6, 128])
    gg = P.sb("gg", [128, 6, 128])
    rstd = P.sb("rstd", [128, 128])
    sq = P.sb("sq", [128, 6, 128])
    htmp = P.sb("htmp", [128, 6, 64])
    px = P.ps("px", [128, 512])
    pmisc = P.ps("pmisc", [128, 512])
    ptr = P.ps("ptr", [128, 8, 128], BF16)
    prep = [P.ps("prep%d" % i, [128, 4, 128]) for i in range(2)]
    pyA = P.ps("pyA", [128, 4, 128])
    pyB = P.ps("pyB", [128, 4, 128])
    pS = P.ps("pS", [128, 512])
    pgt = pmisc[:, 0:256].rearrange("p (g l) -> p g l", g=2)
    hT_v = E["hT_v"]
    yTc_v = E["yTc_d"].t.ap().rearrange("c p t -> p c t")
    prq = {"n": 0}

    def inproj(b):
        t0 = b * 512
        ht, xc, zs = hts[b % 2], xcs[b % 2], zss[b % 2]
        P.dma(ht[:], hT_v[:, :, t0:t0 + 512], [E["hT_d"].b()], [ht.b()])
        for tile in range(10):
            for k in range(8):
                P.mm(px[:], wx[:, k, tile * 128:(tile + 1) * 128], ht[:, k, :], [wx.b(), ht.b()], [px.b()], start=(k == 0), stop=(k == 7))
            P.cp(rawx[:, tile, 3:515], px[:], [px.b()], [rawx.b(tile)], eng="act")
        for tile in range(10):
            rb = rawx.b(tile)
            cacc = caccs[tile % 2]
            cb = cacc.b()
            P.ts(cacc[:], rawx[:, tile, 0:512], convw[:, tile, 0:1], None, ALU.mult, None, [rb, rawx.b(), convw.b()], [cb])
            for kk in range(1, 4):
                P.stt(cacc[:], rawx[:, tile, kk:kk + 512], convw[:, tile, kk:kk + 1], cacc[:], ALU.mult, ALU.add,
                      [rb, rawx.b(), cb], [cb])
            P.act(xc[:, tile, :], cacc[:], AF.Silu, [cb, convb.b()], [xc.b(tile), xc.b()], bias=convb[:, tile:tile + 1])
            P.cp(rawx[:, tile, 0:3], rawx[:, tile, 512:515], [rb, rawx.b()], [rb, rawx.b()], eng="pool")
        for tile in range(6):
            for k in range(8):
                P.mm(px[:], wzc[:, k, tile * 128:(tile + 1) * 128], ht[:, k, :], [wzc.b(), ht.b()], [px.b()], start=(k == 0), stop=(k == 7))
            P.act(zs[:, tile, :], px[:], AF.Silu, [px.b()], [zs.b()])

    def front(ci):
        b, cc = ci // 4, ci % 4
        o = cc * 128
        ht, xc = hts[b % 2], xcs[b % 2]
        q2 = ci % 2
        dtb, btok, Eall, MT, CTs, xdd, xdtpad = dtbs[q2], btoks[q2], Ealls[q2], MTs[q2], CTss[q2], xdds[q2], xdtpads[q2]
        pd = pmisc[:, 256:268]
        pcs = pmisc[:, 272:284]
        for k in range(8):
            P.mm(pd, ht[:, k, o:o + 128], wdt[:, k, :], [ht.b(), wdt.b()], [pmisc.b("pd")], start=(k == 0), stop=(k == 7))
        P.tt(dtb[:, 0, :], pd, ssdv[:, 0, :], ALU.add, [pmisc.b("pd"), ssdv.b()], [dtb.b()])
        P.act(dtb[:, 1, :], dtb[:, 0, :], AF.Exp, [dtb.b()], [dtb.b()])
        P.act(dtb[:, 2, :], dtb[:, 1, :], AF.Ln, [dtb.b()], [dtb.b()], bias=1.0)
        P.tt(dtb[:, 3, :], dtb[:, 2, :], nega[:], ALU.mult, [dtb.b(), nega.b()], [dtb.b()])
        P.mm(pcs, utri[:], dtb[:, 3, :], [utri.b(), dtb.b()], [pmisc.b("pcs")])
        P.ts(nacs[:], pcs, -1.0, None, ALU.mult, None, [pmisc.b("pcs")], [nacs.b()])
        P.tt(uadt[:], utri[:, None, :].to_broadcast([128, 12, 128]), dtb[:, 3, :, None].to_broadcast([128, 12, 128]), ALU.mult,
             [utri.b(), dtb.b()], [uadt.b()], eng="pool")
        for tile in range(8):
            P.tr(ptr[:, tile, :], xc[:, tile, o:o + 128], identb[:], [xc.b(tile), xc.b()], [ptr.b()])
        P.cp(xtok[:].rearrange("p h d -> p (h d)"), ptr[:, 0:6, :].rearrange("p a b -> p (a b)"), [ptr.b()], [xtok.b()], eng="act")
        P.cp(btok[:], ptr[:, 6:8, :].rearrange("p a b -> p (a b)"), [ptr.b()], [btok.b()], eng="act")
        for g in range(2):
            P.mm(pgt[:, g, :], xc[:, 6 + g, o:o + 128], xc[:, 8 + g, o:o + 128], [xc.b()], [pmisc.b("pgt")])
        for q in range(3):
            pr = prep[prq["n"] % 2]
            prq["n"] += 1
            P.mm(pr[:].rearrange("p a b -> p (a b)"), onesf[:], uadt[:, 4 * q:4 * q + 4, :].rearrange("p a b -> p (a b)"),
                 [uadt.b()], [pr.b()])
            P.act(Eall[:, 4 * q:4 * q + 4, :], pr[:], AF.Exp, [pr.b()], [Eall.b()])
            P.mm(pr[:].rearrange("p a b -> p (a b)"), identf[:], neg4[:].rearrange("p a b -> p (a b)"),
                 [neg4.b(), Eall.b()], [pr.b()], start=False, stop=True)
            for hh in range(4):
                h = 4 * q + hh
                P.act(Lm[:, h, :], pr[:, hh, :], AF.Exp, [pr.b(), nacs.b()], [Lm.b()], bias=nacs[:, h:h + 1])
        for g in range(2):
            P.tt(MT[:, 6 * g:6 * g + 6, :], Lm[:, 6 * g:6 * g + 6, :], pgt[:, g:g + 1, :].to_broadcast([128, 6, 128]), ALU.mult,
                 [Lm.b(), pmisc.b("pgt")], [MT.b()])
            P.tt(CTs[:, 6 * g:6 * g + 6, :], Eall[:, 6 * g:6 * g + 6, :], xc[:, 8 + g:9 + g, o:o + 128].to_broadcast([128, 6, 128]), ALU.mult,
                 [Eall.b(), xc.b()], [CTs.b()], eng="pool")
        for r in range(2):
            P.tt(xdtpad[:, r::2, r * 64:r * 64 + 64], xtok[:, r::2, :], dtb[:, 2, r::2, None].to_broadcast([128, 6, 64]), ALU.mult,
                 [xtok.b(), dtb.b()], [xdtpad.b()])
        P.tt(wdec[:], dtb[:, 2, :], Lm[:, :, 127], ALU.mult, [dtb.b(), Lm.b()], [wdec.b()])
        P.tt(xdd[:], xtok[:], wdec[:, :, None].to_broadcast([128, 12, 64]), ALU.mult, [xtok.b(), wdec.b()], [xdd.b()])

    def back(ci):
        b, cc = ci // 4, ci % 4
        o = cc * 128
        xc, zs, yc = xcs[b % 2], zss[b % 2], ycb[b % 2]
        q2 = ci % 2
        btok, Eall, MT, CTs, xdd, xdtpad = btoks[q2], Ealls[q2], MTs[q2], CTss[q2], xdds[q2], xdtpads[q2]
        for pr6 in range(6):
            pyt = pyA[:, pr6, :] if pr6 < 4 else pyB[:, pr6 - 4, :]
            pyb = pyA.b() if pr6 < 4 else pyB.b("y")
            for r in range(2):
                h = 2 * pr6 + r
                P.mm(pyt, xdtpad[:, h, :], MT[:, h, :], [xdtpad.b(), MT.b()], [pyb], start=(r == 0), stop=False)
                P.mm(pyt, Hpad[:, h, :], CTs[:, h, :], [Hpad.b(), CTs.b()], [pyb], start=False, stop=(r == 1))
            P.stt(yf[:, pr6, :], xc[:, pr6, o:o + 128], dcol[:, pr6:pr6 + 1], pyt, ALU.mult, ALU.add, [xc.b(), dcol.b(), pyb], [yf.b()])
        P.tt(gg[:], yf[:], zs[:, :, o:o + 128], ALU.mult, [yf.b(), zs.b()], [gg.b()])
        P.act(sq[:], gg[:], AF.Square, [gg.b()], [sq.b()])
        pss = pyB[:, 2, :]
        for tile in range(6):
            P.mm(pss, onesf[:], sq[:, tile, :], [sq.b()], [pyB.b("ss")], start=(tile == 0), stop=(tile == 5))
        P.act(rstd[:], pss, AF.Sqrt, [pyB.b("ss")], [rstd.b()], scale=1.0 / 768.0, bias=EPS)
        P.recip(rstd[:], rstd[:], [rstd.b()], [rstd.b()])
        for tile in range(6):
            P.stt(yc[:, tile, o:o + 128], gg[:, tile, :], nwcol[:, tile:tile + 1], rstd[:], ALU.mult, ALU.mult,
                  [gg.b(), nwcol.b(), rstd.b()], [yc.b()])
        for g in range(2):
            P.mm(pS[:, 0:384], btok[:, g * 128:(g + 1) * 128], xdd[:, 6 * g:6 * g + 6, :].rearrange("p a b -> p (a b)"),
                 [btok.b(), xdd.b()], [pS.b()])
            P.tt(htmp[:], H[:, 6 * g:6 * g + 6, :], Eall[:, 6 * g:6 * g + 6, 127:128].to_broadcast([128, 6, 64]), ALU.mult,
                 [H.b(), Eall.b()], [htmp.b()])
            P.tt(H[:, 6 * g:6 * g + 6, :], htmp[:], pS[:, 0:384].rearrange("p (a b) -> p a b", a=6), ALU.add,
                 [htmp.b(), pS.b()], [H.b()])
        for r in range(2):
            P.cp(Hpad[:, r::2, r * 64:r * 64 + 64], H[:, r::2, :], [H.b()], [Hpad.b()], eng="act")
        if cc == 3:
            t0 = b * 512
            P.dma(yTc_v[:, :, t0:t0 + 512], yc[:], [yc.b()], [E["yTc_d"].b()])

    inproj(0)
    front(0)
    for ci in range(NCK):
        if ci + 1 < NCK:
            if (ci + 1) % 4 == 0:
                inproj((ci + 1) // 4)
            front(ci + 1)
        back(ci)
    P.end_phase()


def phase_att(P, l, S, E):
    NSB = S // 2048
    onesb = E["onesb"]
    P.begin_phase()
    stg = [P.sb("stg%d" % i, [128, 8, 512]) for i in range(2)]
    qkw = P.sb("qkw", [128, 2])
    P.dma(qkw[:], E["qkw_d"][l], [], [qkw.b()])
    amask = P.sb("amask", [128, 2, 128], BF16)
    P.dma(amask[:], E["amask_d"][:], [], [amask.b()])
    bdones = P.sb("bdones", [128, 128], BF16)
    P.dma(bdones[:], E["bd_d"][:], [], [bdones.b()])
    wq = P.sb("wq", [128, 8, 256], BF16)
    wk = P.sb("wk", [128, 8, 256], BF16)
    wv = P.sb("wv", [128, 8, 256], BF16)
    hts = [P.sb("ah%d" % i, [128, 8, 2048], BF16) for i in range(1)]
    qT = [P.sb("qT%d" % h, [64, 2048], BF16) for h in range(4)]
    kT = [[P.sb("kT%d_%d" % (h, p), [64, 2048], BF16) for p in range(2)] for h in range(4)]
    V = [P.sb("V%d" % p, [128, 16, 4, 64], BF16) for p in range(2)]
    sqb = [P.sb("sqb%d" % i, [64, 512], BF16) for i in range(2)]
    rt = [P.sb("rt%d" % i, [64, 512]) for i in range(2)]
    pT = [P.sb("pT%d" % i, [128, 2, 128], BF16) for i in range(2)]
    ndS = [P.sb("ndS%d" % i, [64, 2, 2048]) for i in range(1)] * 2
    pq = [P.ps("pq%d" % i, [64, 512]) for i in range(2)]
    pn = P.ps("pn", [64, 512])
    pv = P.ps("pv", [128, 256])
    ps = [P.ps("ps%d" % i, [128, 2, 128]) for i in range(2)]
    pnd = [P.ps("pnd%d" % i, [64, 2, 128]) for i in range(2)]
    hT_v = E["hT_v"]
    num_d, den_d = E["num_d"], E["den_d"]
    bi = 0
    hi = 0
    qi = 0
    for g in range(3):
        d = DIL[g]
        nun = 2048 // (128 * d)
        load_w(P, E["w_in_cols"](l, C_Q + g * 256, 256), wq, 256, 8, stg, "wq")
        load_w(P, E["w_in_cols"](l, C_K + g * 256, 256), wk, 256, 8, stg, "wk")
        load_w(P, E["w_in_cols"](l, C_V + g * 256, 256), wv, 256, 8, stg, "wv")
        for sb_ in range(NSB):
            par = sb_ % 2
            T0 = sb_ * 2048
            ht = hts[0]
            P.dma(ht[:], hT_v[:, :, T0:T0 + 2048], [E["hT_d"].b()], [ht.b()])
            for pp in range(4):
                for (W, dst, wc) in ((wq, qT[pp], 0), (wk, kT[pp][par], 1)):
                    for tb in range(4):
                        p_q, s_q, r_t = pq[qi % 2], sqb[qi % 2], rt[qi % 2]
                        qi += 1
                        for k in range(8):
                            P.mm(p_q[:], W[:, k, pp * 64:(pp + 1) * 64], ht[:, k, tb * 512:(tb + 1) * 512], [W.b(), ht.b()], [p_q.b()],
                                 start=(k == 0), stop=(k == 7))
                        P.act(s_q[:], p_q[:], AF.Square, [p_q.b()], [s_q.b()])
                        P.mm(pn[:], onesb[0:64, 0:64], s_q[:], [s_q.b()], [pn.b()])
                        P.act(r_t[:], pn[:], AF.Ln, [pn.b()], [r_t.b()], scale=1.0 / 64.0, bias=EPS)
                        P.act(r_t[:], r_t[:], AF.Exp, [r_t.b()], [r_t.b()], scale=-0.5)
                        P.stt(dst[:, tb * 512:(tb + 1) * 512], p_q[:], qkw[0:64, wc:wc + 1], r_t[:], ALU.mult, ALU.mult,
                              [p_q.b(), qkw.b(), r_t.b()], [dst.b()])
            Vc = V[par]
            for blk in range(16):
                m, r = blk // d, blk % d
                off = m * 128 * d + r
                for k in range(8):
                    P.mm(pv[:], ht[:, k, off:off + 127 * d + 1:d], wv[:, k, :], [ht.b(), wv.b()], [pv.b()], start=(k == 0), stop=(k == 7))
                P.cp(Vc[:, blk, :, :].rearrange("p a b -> p (a b)"), pv[:], [pv.b()], [Vc.b()], eng="act")
            for hh in range(4):
                pp, ro = hh, 0
                qTh = qT[pp]
                kTc = kT[pp][par]
                nd = ndS[hi % 2]
                hi += 1
                def blkinfo(blk):
                    m, r = blk // d, blk % d
                    off = m * 128 * d + r
                    sl = slice(off, off + 127 * d + 1, d)
                    has_prev = (m > 0) or (sb_ > 0)
                    kprev = vprev = pb = None
                    if m > 0:
                        kprev = kTc[ro:ro + 64, off - 128 * d:off - d + 1:d]
                        vprev = Vc[:, blk - d, hh, :]
                        pb = [kTc.b(), Vc.b()]
                    elif sb_ > 0:
                        offp = (nun - 1) * 128 * d + r
                        kprev = kT[pp][1 - par][ro:ro + 64, offp:offp + 127 * d + 1:d]
                        vprev = V[1 - par][:, (nun - 1) * d + r, hh, :]
                        pb = [kT[pp][1 - par].b(), V[1 - par].b()]
                    return sl, has_prev, kprev, vprev, pb

                def qk(blk, bidx):
                    sl, has_prev, kprev, vprev, pb = blkinfo(blk)
                    p_s = ps[bidx % 2]
                    P.mm(p_s[:, 1, :], kTc[ro:ro + 64, sl], qTh[ro:ro + 64, sl], [kTc.b(), qTh.b()], [p_s.b()])
                    if has_prev:
                        P.mm(p_s[:, 0, :], kprev, qTh[ro:ro + 64, sl], [pb[0], qTh.b()], [p_s.b()])

                def rest(blk, bidx):
                    sl, has_prev, kprev, vprev, pb = blkinfo(blk)
                    p_s, p_T, p_nd = ps[bidx % 2], pT[bidx % 2], pnd[bidx % 2]
                    lo = 0 if has_prev else 1
                    P.act(p_T[:, lo:2, :], p_s[:, lo:2, :], AF.Exp, [p_s.b()], [p_T.b()], scale=0.125)
                    P.tt(p_T[:, lo:2, :], p_T[:, lo:2, :], amask[:, lo:2, :], ALU.mult, [p_T.b(), amask.b()], [p_T.b()], eng="pool")
                    P.mm(p_nd[:, 0, :], Vc[:, blk, hh, :], p_T[:, 1, :], [Vc.b(), p_T.b()], [p_nd.b()], start=True, stop=not has_prev)
                    if has_prev:
                        P.mm(p_nd[:, 0, :], vprev, p_T[:, 0, :], [pb[1], p_T.b()], [p_nd.b()], start=False, stop=True)
                    P.mm(p_nd[:, 1, :], onesb[:, 0:64], p_T[:, 1, :], [p_T.b()], [p_nd.b()], start=True, stop=not has_prev)
                    if has_prev:
                        P.mm(p_nd[:, 1, :], onesb[:, 0:64], p_T[:, 0, :], [p_T.b()], [p_nd.b()], start=False, stop=True)
                    P.cp(nd[:, :, sl], p_nd[:], [p_nd.b()], [nd.b()])
                qk(0, bi)
                for blk in range(16):
                    if blk + 1 < 16:
                        qk(blk + 1, bi + 1)
                    rest(blk, bi)
                    bi += 1
                P.dma(num_d.t.ap()[g, hh, :, T0:T0 + 2048], nd[:, 0, :], [nd.b()], [num_d.b()])
                P.dma(den_d.t.ap()[g, hh, :, T0:T0 + 2048], nd[:, 1, :], [nd.b()], [den_d.b()])
    P.end_phase()
    P.begin_phase()
    stg = [P.sb("stg%d" % i, [128, 8, 512]) for i in range(2)]
    wzb = P.sb("wzb", [128, 8, 256], BF16)
    load_w(P, E["w_in_cols"](l, C_ZB, 256), wzb, 256, 8, stg, "wzb")
    hts = [P.sb("ch%d" % i, [128, 8, 512], BF16) for i in range(2)]
    nn = [P.sb("nn%d" % i, [64, 3, 512]) for i in range(2)]
    dd = [P.sb("dd%d" % i, [64, 3, 512]) for i in range(2)]
    zb = P.sb("zb", [64, 512])
    yb = [P.sb("yb%d" % i, [64, 512], BF16) for i in range(2)]
    pz = [P.ps("pz%d" % i, [64, 512]) for i in range(2)]
    it = 0
    for b in range(S // 512):
        t0 = b * 512
        ht = hts[b % 2]
        P.dma(ht[:], hT_v[:, :, t0:t0 + 512], [E["hT_d"].b()], [ht.b()])
        for hh in range(4):
            n_, d_, y_, p_ = nn[it % 2], dd[it % 2], yb[it % 2], pz[it % 2]
            it += 1
            P.dma(n_[:], num_d.t.ap()[:, hh, :, t0:t0 + 512].rearrange("g p t -> p g t"), [num_d.b()], [n_.b()])
            P.dma(d_[:], den_d.t.ap()[:, hh, :, t0:t0 + 512].rearrange("g p t -> p g t"), [den_d.b()], [d_.b()])
            for k in range(8):
                P.mm(p_[:], wzb[:, k, hh * 64:(hh + 1) * 64], ht[:, k, :], [wzb.b(), ht.b()], [p_.b()], start=(k == 0), stop=(k == 7))
            P.act(zb[:], p_[:], AF.Silu, [p_.b()], [zb.b()])
            P.tt(n_[:, 0, :], n_[:, 0, :], n_[:, 1, :], ALU.add, [n_.b()], [n_.b()])
            P.tt(n_[:, 0, :], n_[:, 0, :], n_[:, 2, :], ALU.add, [n_.b()], [n_.b()])
            P.tt(d_[:, 0, :], d_[:, 0, :], d_[:, 1, :], ALU.add, [d_.b()], [d_.b()], eng="pool")
            P.tt(d_[:, 0, :], d_[:, 0, :], d_[:, 2, :], ALU.add, [d_.b()], [d_.b()], eng="pool")
            P.act(d_[:, 1, :], d_[:, 0, :], AF.Ln, [d_.b()], [d_.b()])
            P.act(d_[:, 1, :], d_[:, 1, :], AF.Exp, [d_.b()], [d_.b()], scale=-1.0)
            P.tt(n_[:, 1, :], n_[:, 0, :], d_[:, 1, :], ALU.mult, [n_.b(), d_.b()], [n_.b()])
            P.tt(y_[:], n_[:, 1, :], zb[:], ALU.mult, [n_.b(), zb.b()], [y_.b()])
            P.dma(E["yTb_d"].t.ap()[hh, :, t0:t0 + 512], y_[:], [y_.b()], [E["yTb_d"].b()])
    P.end_phase()


def phase_s5(P, l, S, E):
    SEG = 512
    NSEG = S // SEG
    NCH = SEG // 8
    identf, onesb = E["identf"], E["onesb"]
    P.begin_phase()
    D0b = P.sb("D0b", [128, 32, 128], BF16)
    Bst = P.sb("Bst", [128, 32, 2, 64], BF16)
    Yb = P.sb("Yb", [64, 32, 2, 128], BF16)
    L8 = P.sb("L8", [64, 2, 32])
    P.begin_phase()
    tab = Buf("tab")
    R, W = [tab], [tab]
    par = P.sb("s5par", [64, 3, 32])
    bb = P.sb("s5bb", [64, 2, 32, 16])
    cc = P.sb("s5cc", [64, 2, 32, 16])
    bmask = P.sb("bmask", [128, 128])
    d0t = P.sb("d0t", [128, 128])
    dS5 = P.sb("dS5", [128, 32])
    P.dma(dS5[:], E["s5d_d"][l], [], W)
    P.dma(par[:], E["s5par_d"][l], [], W)
    P.dma(bb[:], E["s5b_d"][l], [], W)
    P.dma(cc[:], E["s5c_d"][l], [], W)
    P.dma(bmask[:], E["bmask_d"][:], [], W)
    tm = P.sb("s5tm", [64, 12, 32])
    LP = P.sb("LP", [64, 9, 2, 32])
    LN = P.sb("LN", [64, 9, 2, 32])
    BB = P.sb("BBar", [64, 2, 32, 16])
    XX = P.sb("XX", [64, 2, 32, 8, 16])
    YY = P.sb("YY", [64, 2, 32, 8, 16])
    X7 = P.sb("X7", [64, 2, 32, 8, 16])
    t16 = P.sb("t16", [64, 2, 32, 16])
    pD = P.ps("pD", [128, 128])
    pTr = P.ps("pTr", [128, 2, 64])
    are, aim, lst = par[:, 0, :], par[:, 1, :], par[:, 2, :]
    T_ = lambda i: tm[:, i, :]
    mul = lambda o, a, b: P.tt(o, a, b, ALU.mult, R, W)
    add = lambda o, a, b: P.tt(o, a, b, ALU.add, R, W)
    sub = lambda o, a, b: P.tt(o, a, b, ALU.subtract, R, W)
    P.act(T_(0), lst, AF.Exp, R, W)
    mul(T_(1), are, T_(0))
    mul(T_(2), aim, T_(0))
    P.act(T_(3), T_(1), AF.Exp, R, W, scale=1.0 / 16)
    hp = P.sb("halfpi", [64, 1])
    P.memset(hp[:], math.pi / 2, W)
    P.act(T_(4), T_(2), AF.Sin, R, W, scale=1.0 / 16, bias=hp[:, 0:1])
    P.act(T_(5), T_(2), AF.Sin, R, W, scale=1.0 / 16)
    lr, li = LP[:, 1, 0, :], LP[:, 1, 1, :]
    mul(lr, T_(3), T_(4))
    mul(li, T_(3), T_(5))
    for _ in range(4):
        mul(T_(6), lr, lr)
        mul(T_(7), li, li)
        mul(T_(8), lr, li)
        sub(lr, T_(6), T_(7))
        P.ts(li, T_(8), 2.0, None, ALU.mult, None, R, W)
    P.ts(T_(0), lr, -1.0, None, ALU.add, None, R, W)
    mul(T_(1), are, are)
    mul(T_(2), aim, aim)
    add(T_(1), T_(1), T_(2))
    P.recip(T_(1), T_(1), R, W)
    mul(T_(2), T_(0), are)
    mul(T_(3), li, aim)
    add(T_(2), T_(2), T_(3))
    mul(T_(9), T_(2), T_(1))
    mul(T_(2), li, are)
    mul(T_(3), T_(0), aim)
    sub(T_(2), T_(2), T_(3))
    mul(T_(10), T_(2), T_(1))
    mul(T_(0), lr, lr)
    mul(T_(1), li, li)
    add(T_(0), T_(0), T_(1))
    P.recip(T_(0), T_(0), R, W)
    mul(LN[:, 1, 0, :], lr, T_(0))
    mul(T_(1), li, T_(0))
    P.ts(LN[:, 1, 1, :], T_(1), -1.0, None, ALU.mult, None, R, W)
    P.memset(LP[:, 0, 0, :], 1.0, W)
    P.memset(LP[:, 0, 1, :], 0.0, W)

    def cmul(o_re, o_im, a_re, a_im, b_re, b_im, t1, t2):
        mul(t1, a_re, b_re)
        mul(t2, a_im, b_im)
        sub(o_re, t1, t2)
        mul(t1, a_re, b_im)
        mul(t2, a_im, b_re)
        add(o_im, t1, t2)
    for k in range(1, 8):
        cmul(LP[:, k + 1, 0, :], LP[:, k + 1, 1, :], LP[:, k, 0, :], LP[:, k, 1, :], LP[:, 1, 0, :], LP[:, 1, 1, :], T_(6), T_(7))
        cmul(LN[:, k + 1, 0, :], LN[:, k + 1, 1, :], LN[:, k, 0, :], LN[:, k, 1, :], LN[:, 1, 0, :], LN[:, 1, 1, :], T_(6), T_(7))
    bc = lambda ap: ap[:, :, None].to_broadcast([64, 32, 16])
    t1, t2 = t16[:, 0, :, :], t16[:, 1, :, :]
    cmul(BB[:, 0, :, :], BB[:, 1, :, :], bc(T_(9)), bc(T_(10)), bb[:, 0, :, :], bb[:, 1, :, :], t1, t2)
    for j in range(8):
        cmul(XX[:, 0, :, j, :], XX[:, 1, :, j, :], bc(LN[:, j + 1, 0, :]), bc(LN[:, j + 1, 1, :]), BB[:, 0, :, :], BB[:, 1, :, :], t1, t2)
        cmul(YY[:, 0, :, j, :], YY[:, 1, :, j, :], bc(LP[:, j + 1, 0, :]), bc(LP[:, j + 1, 1, :]), cc[:, 0, :, :], cc[:, 1, :, :], t1, t2)
        cmul(X7[:, 0, :, j, :], X7[:, 1, :, j, :], bc(LP[:, 7 - j, 0, :]), bc(LP[:, 7 - j, 1, :]), BB[:, 0, :, :], BB[:, 1, :, :], t1, t2)
    P.ts(YY[:, 1, :, :, :].rearrange("p g j o -> p (g j o)"), YY[:, 1, :, :, :].rearrange("p g j o -> p (g j o)"), -1.0, None, ALU.mult, None, R, W)
    P.cp(Yb[:, :, 0, :], YY[:, 0, :, :, :].rearrange("p g j o -> p g (j o)"), R, W)
    P.cp(Yb[:, :, 1, :], YY[:, 1, :, :, :].rearrange("p g j o -> p g (j o)"), R, W)
    P.cp(L8[:], LP[:, 8, :, :], R, W)
    for g in range(32):
        P.mm(pD[:], XX[:, 0, g, :, :].rearrange("p j i -> p (j i)"), YY[:, 0, g, :, :].rearrange("p j o -> p (j o)"), R, W, start=True, stop=False)
        P.mm(pD[:], XX[:, 1, g, :, :].rearrange("p j i -> p (j i)"), YY[:, 1, g, :, :].rearrange("p j o -> p (j o)"), R, W, start=False, stop=True)
        P.tt(d0t[:], pD[:], bmask[:], ALU.mult, R, W)
        P.stt(D0b[:, g, :], identf[:], dS5[:, g:g + 1], d0t[:], ALU.mult, ALU.add, R, W)
        for c in range(2):
            P.tr(pTr[:, c, :], X7[:, c, g, :, :].rearrange("p j i -> p (j i)"), identf[0:64, 0:64], R, W)
        P.cp(Bst[:, g, :, :], pTr[:], R, W)
    P.end_phase()
    wu = P.sb("wu", [128, 8, 512], BF16)
    wza = P.sb("wza", [128, 8, 512], BF16)
    glw = P.sb("glw", [128, 4, 512], BF16)
    P.begin_phase()
    stg = [P.sb("stg%d" % i, [128, 8, 512]) for i in range(2)]
    load_w(P, E["w_in_cols"](l, C_UA, 512), wu, 512, 8, stg, "wu")
    load_w(P, E["w_in_cols"](l, C_ZA, 512), wza, 512, 8, stg, "wza")
    load_w(P, E["gluw_d"].t.ap()[l], glw, 512, 4, stg, "glw")
    P.end_phase()

    def ld(name, src, shape, dt=F32):
        t = P.sb(name, shape, dt)
        P.dma(t[:], src, [], [t.b()])
        return t
    glub = ld("glub", E["glub_d"][l], [128, 4])
    selT = ld("selT", E["selT_d"][:], [128, 8, 8, 128], BF16)
    sel = ld("sel", E["sel_d"][:], [128, 8, 8, 128], BF16)
    A2 = P.sb("A2", [64, 2, 32])
    B2 = P.sb("B2", [64, 2, 32])
    P.cp(A2[:, 0, :], L8[:, 0, :], [tab], [A2.b()])
    P.cp(A2[:, 1, :], L8[:, 0, :], [tab], [A2.b()])
    P.cp(B2[:, 1, :], L8[:, 1, :], [tab], [B2.b()])
    P.ts(B2[:, 0, :], L8[:, 1, :], -1.0, None, ALU.mult, None, [tab], [B2.b()])
    hts = [P.sb("s5ht%d" % i, [128, 8, SEG], BF16) for i in range(2)]
    uT = P.sb("uT", [128, 4, SEG], BF16)
    Ugs = [P.sb("Ug%d" % i, [128, 32, NCH], BF16) for i in range(2)]
    HHs = [P.sb("HH%d" % i, [64, 2, 32, NCH]) for i in range(2)]
    Hbs = [P.sb("Hb%d" % i, [64, 2, 32, NCH], BF16) for i in range(1)] * 2
    carry = P.sb("carry", [64, 2, 32])
    P.memset(carry[:], 0.0, [carry.b()])
    sct = P.sb("sct", [64, 2, 2, 32])
    glg = P.sb("glg", [128, 32, NCH], BF16)
    gT = P.sb("gT", [128, 4, SEG], BF16)
    y1 = P.sb("y1", [128, 8, NCH])
    y2 = P.sb("y2", [128, 8, NCH])
    sgs = P.sb("sgs", [128, 512])
    zsa = P.sb("zsa", [128, 512])
    yaS = [P.sb("yaS%d" % i, [128, 4, 512], BF16) for i in range(1)] * 2
    pA = [P.ps("pA%d" % i, [128, 512]) for i in range(2)]
    pU = [P.ps("pU%d" % i, [128, 8, NCH]) for i in range(2)]
    pY = [P.ps("pY%d" % i, [128, NCH]) for i in range(2)]
    yp = [P.sb("yp%d" % i, [128, NCH]) for i in range(6)]
    yq = [P.sb("yq%d" % i, [128, NCH]) for i in range(6)]
    pSr = P.ps("pSr", [64, 8, NCH])
    pSi = P.ps("pSi", [64, 8, NCH])
    hT_v = E["hT_v"]
    yTa_v = E["yTa_d"].t.ap().rearrange("c p t -> p c t")
    import os
    SCAN_ENG = os.environ.get("SCAN_ENG", "pool")
    lvl = int(os.environ.get("S5DBG", "9"))
    cnt = {"a": 0, "u": 0}

    def pass1(sg):
        T0 = sg * SEG
        ht, Ug, HH = hts[sg % 2], Ugs[sg % 2], HHs[sg % 2]
        P.dma(ht[:], hT_v[:, :, T0:T0 + SEG], [E["hT_d"].b()], [ht.b()])
        for ct in range(4):
            for tb in range(SEG // 512):
                p_ = pA[cnt["a"] % 2]
                cnt["a"] += 1
                for k in range(8):
                    P.mm(p_[:], wu[:, k, ct * 128:(ct + 1) * 128], ht[:, k, tb * 512:(tb + 1) * 512], [wu.b(), ht.b()], [p_.b()], start=(k == 0), stop=(k == 7))
                P.cp(uT[:, ct, tb * 512:(tb + 1) * 512], p_[:], [p_.b()], [uT.b()], eng=("act" if cnt["a"] % 2 else "dve"))
        for bq in range(4):
            p_u = pU[cnt["u"] % 2]
            cnt["u"] += 1
            for gl in range(8):
                for j in range(8):
                    P.mm(p_u[:, gl, :], selT[:, gl, j, :], uT[:, bq, j:SEG:8], [selT.b(), uT.b()], [p_u.b()], start=(j == 0), stop=(j == 7))
            P.cp(Ug[:, 8 * bq:8 * bq + 8, :], p_u[:], [p_u.b()], [Ug.b(bq)], eng="act")
            for gl in range(8):
                g = 8 * bq + gl
                P.mm(pSr[:, gl, :], Bst[:, g, 0, :], Ug[:, g, :], [tab, Ug.b(bq)], [pSr.b()])
                P.mm(pSi[:, gl, :], Bst[:, g, 1, :], Ug[:, g, :], [tab, Ug.b(bq)], [pSi.b()])
            P.cp(HH[:, 0, 8 * bq:8 * bq + 8, :], pSr[:], [pSr.b()], [HH.b()])
            P.cp(HH[:, 1, 8 * bq:8 * bq + 8, :], pSi[:], [pSi.b()], [HH.b()], eng="act")

    def scan(sg):
        HH, Hb = HHs[sg % 2], Hbs[sg % 2]
        hb = [HH.b()]
        for c in range(NCH):
            if c == 0:
                hp = carry
                idx = lambda pl: carry[:, pl, :]
                rb = [carry.b(), HH.b()]
            else:
                idx = (lambda cc: (lambda pl: HH[:, pl, :, cc]))(c - 1)
                rb = [HH.b()]
            hall = carry[:] if c == 0 else HH[:, :, :, c - 1]
            sb_ = [sct.b()]
            P.tt(sct[:, 0, :, :], A2[:], hall, ALU.mult, rb + [A2.b()], sb_, eng=SCAN_ENG)
            P.tt(sct[:, 1, 0, :], B2[:, 0, :], idx(1), ALU.mult, rb + [B2.b()], sb_, eng=SCAN_ENG)
            P.tt(sct[:, 1, 1, :], B2[:, 1, :], idx(0), ALU.mult, rb + [B2.b()], sb_, eng=SCAN_ENG)
            P.tt(HH[:, :, :, c], HH[:, :, :, c], sct[:, 0, :, :], ALU.add, sb_ + hb, hb, eng=SCAN_ENG)
            P.tt(HH[:, :, :, c], HH[:, :, :, c], sct[:, 1, :, :], ALU.add, sb_ + hb, hb, eng=SCAN_ENG)
        P.cp(Hb[:, :, :, 0], carry[:], [carry.b()], [Hb.b()], eng=SCAN_ENG)
        P.cp(Hb[:, :, :, 1:NCH], HH[:, :, :, 0:NCH - 1], [HH.b()], [Hb.b()], eng=SCAN_ENG)
        P.cp(carry[:], HH[:, :, :, NCH - 1], [HH.b(), Hb.b()], [carry.b()], eng=SCAN_ENG)

    def pass2(sg):
        T0 = sg * SEG
        ht, Ug, Hb = hts[sg % 2], Ugs[sg % 2], Hbs[sg % 2]
        NBUF = 6

        def st1(g):
            bq = g // 8
            p_y = pY[g % 2]
            ya1 = yp[g % NBUF]
            P.mm(p_y[:], D0b[:, g, :], Ug[:, g, :], [tab, Ug.b(bq)], [p_y.b()], start=True, stop=False)
            P.mm(p_y[:], Yb[:, g, 0, :], Hb[:, 0, g, :], [Hb.b()], [p_y.b()], start=False, stop=False)
            P.mm(p_y[:], Yb[:, g, 1, :], Hb[:, 1, g, :], [Hb.b()], [p_y.b()], start=False, stop=True)
            P.cp(ya1[:], p_y[:], [p_y.b()], [ya1.b()])

        def st2(g):
            ya1, ya2 = yp[g % NBUF], yq[g % NBUF]
            P.act(ya2[:], ya1[:], AF.Square, [ya1.b()], [ya2.b()], scale=0.21145921592590868)

        def st3(g):
            ya1, ya2 = yp[g % NBUF], yq[g % NBUF]
            P.stt(ya2[:], ya2[:], 1.0, ya1[:], ALU.add, ALU.mult, [ya2.b(), ya1.b()], [ya2.b()])

        def st4(g):
            ya2 = yq[g % NBUF]
            P.act(ya2[:], ya2[:], AF.Sigmoid, [ya2.b()], [ya2.b()], scale=1.5957691216)

        def st5(g):
            ya1, ya2 = yp[g % NBUF], yq[g % NBUF]
            P.tt(glg[:, g, :], ya2[:], ya1[:], ALU.mult, [ya2.b(), ya1.b()], [glg.b(g // 8)])
        stages = [st1, st2, st3, st4, st5]
        for it in range(32 + len(stages) - 1):
            for si, st in enumerate(stages):
                g = it - si
                if 0 <= g < 32:
                    st(g)
        if lvl < 4:
            return
        for ct in range(4):
            p_f = pU[cnt["u"] % 2]
            cnt["u"] += 1
            for jp in range(8):
                for gl in range(8):
                    P.mm(p_f[:, jp, :], sel[:, gl, jp, :], glg[:, ct * 8 + gl, :], [sel.b(), glg.b(ct)], [p_f.b()], start=(gl == 0), stop=(gl == 7))
            P.cp(gT[:, ct, :].rearrange("p (c j) -> p j c", j=8), p_f[:], [p_f.b()], [gT.b()], eng=("act" if ct % 2 else "dve"))
        if lvl < 5:
            return
        for tb in range(SEG // 512):
            ya = yaS[sg % 2]
            ts_ = slice(tb * 512, (tb + 1) * 512)
            for co in range(4):
                p_ = pA[cnt["a"] % 2]
                pZ = pA[(cnt["a"] + 1) % 2]
                cnt["a"] += 1
                for ct in range(4):
                    P.mm(p_[:], glw[:, ct, co * 128:(co + 1) * 128], gT[:, ct, ts_], [glw.b(), gT.b()], [p_.b()], start=(ct == 0), stop=(ct == 3))
                P.act(sgs[:], p_[:], AF.Sigmoid, [p_.b(), glub.b()], [sgs.b()], bias=glub[:, co:co + 1])
                for k in range(8):
                    P.mm(pZ[:], wza[:, k, co * 128:(co + 1) * 128], ht[:, k, ts_], [wza.b(), ht.b()], [pZ.b()], start=(k == 0), stop=(k == 7))
                P.act(zsa[:], pZ[:], AF.Silu, [pZ.b()], [zsa.b()])
                P.tt(sgs[:], sgs[:], gT[:, co, ts_], ALU.mult, [sgs.b(), gT.b()], [sgs.b()])
                P.tt(ya[:, co, :], sgs[:], zsa[:], ALU.mult, [sgs.b(), zsa.b()], [ya.b()])
            P.dma(yTa_v[:, :, T0 + tb * 512:T0 + (tb + 1) * 512], ya[:], [ya.b()], [E["yTa_d"].b()])

    if lvl >= 1:
        pass1(0)
    for sg in range(NSEG if lvl >= 9 else 1):
        if lvl >= 2:
            scan(sg)
        if sg + 1 < NSEG and lvl >= 9:
            pass1(sg + 1)
        if lvl >= 3:
            pass2(sg)
    P.end_phase()


def host_consts():
    bf = ml_dtypes.bfloat16
    c = {}
    c["c_identb"] = np.eye(128).astype(bf)
    c["c_identf"] = np.eye(128, dtype=np.float32)
    c["c_onesb"] = np.ones((128, 128)).astype(bf)
    c["c_onesf"] = np.ones((128, 128), np.float32)
    k = np.arange(128)
    bd = np.zeros((128, 128), np.float32)
    bd[:64, :64] = 1.0
    bd[64:, 64:] = 1.0
    c["c_bdones"] = bd.astype(bf)
    c["c_utri"] = (k[:, None] <= k[None, :]).astype(np.float32)
    neg = np.where(k[:, None] > k[None, :], NEGV, 0.0).astype(np.float32)
    c["c_neg4"] = np.ascontiguousarray(np.broadcast_to(neg[:, None, :], (128, 4, 128)))
    am = np.zeros((128, 2, 128), np.float32)
    am[:, 0, :] = (k[:, None] >= k[None, :])
    am[:, 1, :] = (k[None, :] >= k[:, None])
    c["c_amask"] = am.astype(bf)
    c["c_bmask"] = ((k[None, :] // 16) >= (k[:, None] // 16)).astype(np.float32)
    selT = np.zeros((128, 8, 8, 128), np.float32)
    sel = np.zeros((128, 8, 8, 128), np.float32)
    for gl in range(8):
        for j in range(8):
            for i in range(16):
                selT[16 * gl + i, gl, j, 16 * j + i] = 1.0
                sel[16 * j + i, gl, j, 16 * gl + i] = 1.0
    c["c_selT"] = selT.astype(bf)
    c["c_sel"] = sel.astype(bf)
    return c


def host_params(inp):
    f = lambda a: np.ascontiguousarray(np.asarray(a, dtype=np.float32))
    L = inp["norm_w"].shape[0]
    o = {}
    o["w_in"] = f(inp["w_in"])
    o["nw"] = f(np.asarray(inp["norm_w"]).reshape(L, 8, 128).transpose(0, 2, 1))
    are = np.asarray(inp["s5_a_re"]).transpose(0, 2, 1)
    aim = np.asarray(inp["s5_a_im"]).transpose(0, 2, 1)
    lst = np.broadcast_to(np.asarray(inp["s5_log_step"])[:, None, :], (L, 64, 32))
    o["s5par"] = f(np.stack([are, aim, lst], axis=2))
    bre = np.asarray(inp["s5_b_re"]).transpose(0, 2, 1, 3)
    bim = np.asarray(inp["s5_b_im"]).transpose(0, 2, 1, 3)
    o["s5b"] = f(np.stack([bre, bim], axis=2))
    cre = np.asarray(inp["s5_c_re"]).transpose(0, 3, 1, 2)
    cim = np.asarray(inp["s5_c_im"]).transpose(0, 3, 1, 2)
    o["s5c"] = f(np.stack([cre, cim], axis=2))
    d = np.asarray(inp["s5_d"]).reshape(L, 32, 16)
    o["s5d"] = f(np.broadcast_to(d.transpose(0, 2, 1)[:, None, :, :], (L, 8, 16, 32)).reshape(L, 128, 32))
    o["gluw"] = f(inp["s5_glu_w"])
    o["glub"] = f(np.asarray(inp["s5_glu_b"]).reshape(L, 4, 128).transpose(0, 2, 1))
    qk = np.stack([np.asarray(inp["q_norm_w"]), np.asarray(inp["k_norm_w"])], axis=2)
    o["qkw"] = f(np.concatenate([qk, qk], axis=1))
    cw = np.asarray(inp["conv_w"]).reshape(L, 4, 10, 128)
    o["convw"] = f(cw.transpose(0, 3, 2, 1))
    o["convb"] = f(np.asarray(inp["conv_b"]).reshape(L, 10, 128).transpose(0, 2, 1))
    sv = np.stack([np.asarray(inp["dt_bias"]), np.asarray(inp["ssd_a_log"]), np.asarray(inp["ssd_d"])], axis=1)
    o["ssdv"] = f(np.broadcast_to(sv[:, None, :, :], (L, 128, 3, 12)))
    dd = np.repeat(np.asarray(inp["ssd_d"]), 64, axis=1)
    o["ssdd"] = f(dd.reshape(L, 6, 128).transpose(0, 2, 1))
    o["ssdnw"] = f(np.asarray(inp["ssd_norm_w"]).reshape(L, 6, 128).transpose(0, 2, 1))
    o["proj_a"] = f(inp["proj_a"])
    o["proj_b"] = f(inp["proj_b"])
    o["proj_c"] = f(inp["proj_c"])
    o["w_out"] = f(inp["w_out"])
    return o


_NC_CACHE = {}


def kernel(**inputs):
    x = np.asarray(inputs["x"], dtype=np.float32)
    B, S, _ = x.shape
    L = np.asarray(inputs["norm_w"]).shape[0]
    key = (S, L)
    if key not in _NC_CACHE:
        _NC_CACHE[key] = build(S, L)
    nc = _NC_CACHE[key]
    shared = host_params(inputs)
    shared.update(host_consts())
    in_maps = []
    for b in range(B):
        m = dict(shared)
        m["x"] = np.ascontiguousarray(x[b])
        in_maps.append(m)
    res = run_bass_kernel_spmd(nc, in_maps, core_ids=list(range(B)))
    out = np.stack([np.asarray(r["out"], dtype=np.float32) for r in res.results], axis=0)
    return out
```

```python
from contextlib import ExitStack
import numpy as np
import concourse.bass as bass
import concourse.mybir as mybir

F32 = mybir.dt.float32
BF16 = mybir.dt.bfloat16
AF = mybir.ActivationFunctionType
ALU = mybir.AluOpType
AX = mybir.AxisListType

ENGS = ["pe", "act", "dve", "pool", "sp"]
NDMA = 40
SAME_SYNC = {"act", "dve", "pool"}


class Buf:
    __slots__ = ("name", "w", "r")

    def __init__(self, name):
        self.name = name
        self.w = None
        self.r = {}


class Tl:
    def __init__(self, t, name):
        self.t = t
        self.name = name
        self.bufs = {}

    def b(self, key=None):
        if key not in self.bufs:
            self.bufs[key] = Buf("%s/%s" % (self.name, key))
        return self.bufs[key]

    def __getitem__(self, idx):
        return self.t[idx]


class Prog:
    def __init__(self, nc):
        self.nc = nc
        self.ops = {e: [] for e in ENGS}
        self.cnt = {e: 0 for e in ENGS}
        self.waited = {e: {} for e in ENGS}
        self.dma_uses = [0] * NDMA
        self.dma_rr = 0
        self.stack = ExitStack()
        self.pstacks = []
        self.n_ops = 0

    def _ctx(self, persistent):
        return self.stack if (persistent or not self.pstacks) else self.pstacks[-1]

    def sb(self, name, shape, dt=F32, persistent=False):
        self.n_ops += 1
        name = "sb%d_%s" % (self.n_ops, name)
        t = self._ctx(persistent).enter_context(self.nc.sbuf_tensor(name, list(shape), dt))
        return Tl(t, name)

    def ps(self, name, shape, dt=F32, persistent=False):
        self.n_ops += 1
        name = "ps%d_%s" % (self.n_ops, name)
        t = self._ctx(persistent).enter_context(self.nc.psum_tensor(name, list(shape), dt))
        return Tl(t, name)

    def dram(self, name, shape, dt, kind="Internal"):
        t = self.nc.dram_tensor(name, list(shape), dt, kind=kind)
        return Tl(t, name)

    def begin_phase(self):
        self.pstacks.append(ExitStack())

    def end_phase(self):
        self.barrier()
        self.pstacks.pop().close()

    def capture(self, f):
        self._cap = []
        try:
            f()
        finally:
            lst, self._cap = self._cap, None
        return lst

    def replay_interleaved(self, a, b):
        na, nb = len(a), len(b)
        ia = ib = 0
        while ia < na or ib < nb:
            if ib >= nb or (ia < na and ia * nb <= ib * na):
                self.op(*a[ia])
                ia += 1
            else:
                self.op(*b[ib])
                ib += 1

    def op(self, eng, fn, reads=(), writes=(), dma=False):
        if getattr(self, "_cap", None) is not None:
            self._cap.append((eng, fn, tuple(reads), tuple(writes), dma))
            return None
        deps = {}
        for b in reads:
            if b.w is not None:
                k, v = b.w
                deps[k] = max(deps.get(k, 0), v)
        for b in writes:
            if b.w is not None:
                k, v = b.w
                deps[k] = max(deps.get(k, 0), v)
            for k, v in b.r.items():
                deps[k] = max(deps.get(k, 0), v)
        waits = []
        wd = self.waited[eng]
        for k, v in deps.items():
            if k == ("eng", eng) and eng not in SAME_SYNC:
                continue
            if wd.get(k, 0) < v:
                wd[k] = v
                waits.append((k, v))
        if dma:
            j = self.dma_rr
            self.dma_rr = (j + 1) % NDMA
            prev = self.dma_uses[j] * 16
            k = ("dma", j)
            if prev and wd.get(k, 0) < prev:
                wd[k] = prev
                waits.append((k, prev))
            self.dma_uses[j] += 1
            tok = (k, self.dma_uses[j] * 16)
        else:
            self.cnt[eng] += 1
            tok = (("eng", eng), self.cnt[eng])
        self.ops[eng].append((waits, fn, tok))
        self.n_ops += 1
        for b in reads:
            k, v = tok
            b.r[k] = max(b.r.get(k, 0), v)
        for b in writes:
            b.w = tok
            b.r = {}
        return tok

    def barrier(self):
        targets = {}
        for e in ENGS:
            if self.cnt[e]:
                targets[("eng", e)] = self.cnt[e]
        for j in range(NDMA):
            if self.dma_uses[j]:
                targets[("dma", j)] = self.dma_uses[j] * 16
        for e in ENGS:
            waits = []
            for k, v in targets.items():
                if k == ("eng", e):
                    continue
                if self.waited[e].get(k, 0) < v:
                    self.waited[e][k] = v
                    waits.append((k, v))
            if waits:
                self.ops[e].append((waits, None, None))

    def emit(self):
        nc = self.nc
        with ExitStack() as es:
            sems = {}
            for e in ENGS:
                sems[("eng", e)] = es.enter_context(nc.semaphore("s_" + e))
            for j in range(NDMA):
                sems[("dma", j)] = es.enter_context(nc.semaphore("d_%d" % j))
            block = es.enter_context(nc.Block())

            def run(engname, eng):
                for waits, fn, tok in self.ops[engname]:
                    for k, v in waits:
                        eng.wait_ge(sems[k], v)
                    if fn is None:
                        continue
                    ins = fn(eng)
                    k, v = tok
                    if k[0] == "dma":
                        ins.then_inc(sems[k], 16)
                    else:
                        ins.then_inc(sems[k], 1)

            @block.tensor
            def _(eng):
                run("pe", eng)

            @block.scalar
            def _(eng):
                run("act", eng)

            @block.vector
            def _(eng):
                run("dve", eng)

            @block.gpsimd
            def _(eng):
                run("pool", eng)

            @block.sync
            def _(eng):
                run("sp", eng)

    def dma(self, out, in_, reads, writes, eng="sp", **kw):
        return self.op(eng, lambda e: e.dma_start(out=out, in_=in_, **kw), reads, writes, dma=True)

    def mm(self, out, lhsT, rhs, reads, writes, start=True, stop=True, **kw):
        return self.op("pe", lambda e: e.matmul(out, lhsT, rhs, start=start, stop=stop, **kw), reads, writes)

    def tr(self, out, in_, ident, reads, writes):
        return self.op("pe", lambda e: e.transpose(out, in_, ident), reads, writes)

    def act(self, out, in_, func, reads, writes, eng="act", **kw):
        return self.op(eng, lambda e: e.activation(out, in_, func, **kw), reads, writes)

    def tt(self, out, in0, in1, op, reads, writes, eng="dve"):
        return self.op(eng, lambda e: e.tensor_tensor(out, in0, in1, op), reads, writes)

    def ts(self, out, in0, s1, s2, op0, op1, reads, writes, eng="dve", **kw):
        if op1 is None:
            return self.op(eng, lambda e: e.tensor_scalar(out, in0, s1, None, op0, **kw), reads, writes)
        return self.op(eng, lambda e: e.tensor_scalar(out, in0, s1, s2, op0, op1, **kw), reads, writes)

    def stt(self, out, in0, scalar, in1, op0, op1, reads, writes, eng="dve"):
        return self.op(eng, lambda e: e.scalar_tensor_tensor(out, in0, scalar, in1, op0, op1), reads, writes)

    def cp(self, out, in_, reads, writes, eng="dve"):
        if eng == "act":
            return self.op(eng, lambda e: e.copy(out, in_), reads, writes)
        return self.op(eng, lambda e: e.tensor_copy(out, in_), reads, writes)

    def memset(self, ap, val, writes, eng="dve"):
        return self.op(eng, lambda e: e.memset(ap, val), (), writes)

    def recip(self, out, in_, reads, writes):
        return self.op("dve", lambda e: e.reciprocal(out, in_), reads, writes)


import math
import numpy as np, ml_dtypes
from concourse.bass_utils import run_bass_kernel_spmd

D = 1024
INW = 8716
C_UA, C_ZA, C_Q, C_K, C_V, C_ZB, C_XBC, C_DT, C_ZC, C_GATE = 0, 512, 1024, 1792, 2560, 3328, 3584, 4864, 4876, 5644
EPS = 1e-6
DIL = (1, 4, 16)
NEGV = -1.0e5


def load_w(P, src_ap, dst, ncols, kt, stg, tag):
    v = src_ap.rearrange("(k p) c -> p k c", p=128)
    i = 0
    for c0 in range(0, ncols, 512):
        c1 = min(ncols, c0 + 512)
        st = stg[i % 2]
        P.dma(st[:, 0:kt, 0:c1 - c0], v[:, :, c0:c1], [], [st.b()])
        P.cp(dst[:, :, c0:c1], st[:, 0:kt, 0:c1 - c0], [st.b()], [dst.b()], eng=("act" if i % 2 else "dve"))
        i += 1


def build(S, L=2, debug=None):
    nc = bass.Bass("TRN2", target_bir_lowering=False)
    P = Prog(nc)
    NT = S // 128
    NB = S // 512
    NSB = S // 2048
    ein = lambda n, sh, dt=F32: P.dram(n, sh, dt, kind="ExternalInput")
    x_d = ein("x", [S, D])
    out_d = P.dram("out", [S, D], F32, kind="ExternalOutput")
    w_in_d = ein("w_in", [L, D, INW])
    nw_d = ein("nw", [L, 128, 8])
    s5par_d = ein("s5par", [L, 64, 3, 32])
    s5b_d = ein("s5b", [L, 64, 2, 32, 16])
    s5c_d = ein("s5c", [L, 64, 2, 32, 16])
    s5d_d = ein("s5d", [L, 128, 32])
    gluw_d = ein("gluw", [L, 512, 512])
    glub_d = ein("glub", [L, 128, 4])
    qkw_d = ein("qkw", [L, 128, 2])
    bd_d = ein("c_bdones", [128, 128], BF16)
    convw_d = ein("convw", [L, 128, 10, 4])
    convb_d = ein("convb", [L, 128, 10])
    ssdv_d = ein("ssdv", [L, 128, 3, 12])
    ssdd_d = ein("ssdd", [L, 128, 6])
    ssdnw_d = ein("ssdnw", [L, 128, 6])
    pa_d = ein("proj_a", [L, 512, D])
    pb_d = ein("proj_b", [L, 256, D])
    pc_d = ein("proj_c", [L, 768, D])
    wo_d = ein("w_out", [L, D, D])
    idb_d = ein("c_identb", [128, 128], BF16)
    idf_d = ein("c_identf", [128, 128])
    onesb_d = ein("c_onesb", [128, 128], BF16)
    onesf_d = ein("c_onesf", [128, 128])
    utri_d = ein("c_utri", [128, 128])
    neg4_d = ein("c_neg4", [128, 4, 128])
    amask_d = ein("c_amask", [128, 2, 128], BF16)
    bmask_d = ein("c_bmask", [128, 128])
    selT_d = ein("c_selT", [128, 8, 8, 128], BF16)
    sel_d = ein("c_sel", [128, 8, 8, 128], BF16)
    xmid_d = P.dram("xmid", [S, D], F32)
    hT_d = P.dram("hT", [8, 128, S], BF16)
    yTa_d = P.dram("yTa", [4, 128, S], BF16)
    yTb_d = P.dram("yTb", [4, 64, S], BF16)
    yTc_d = P.dram("yTc", [6, 128, S], BF16)
    num_d = P.dram("attnum", [3, 4, 64, S], F32)
    den_d = P.dram("attden", [3, 4, 64, S], F32)

    def cload(name, d, shape, dt=F32):
        t = P.sb(name, shape, dt, persistent=True)
        P.dma(t[:], d[:], [], [t.b()])
        return t

    identb = cload("identb", idb_d, [128, 128], BF16)
    identf = cload("identf", idf_d, [128, 128])
    onesb = cload("onesb", onesb_d, [128, 128], BF16)
    onesf = cload("onesf", onesf_d, [128, 128])

    def w_in_cols(l, c0, n):
        return w_in_d.t.ap()[l, :, c0:c0 + n]

    hT_v = hT_d.t.ap().rearrange("c p t -> p c t")

    for l in range(L):
        xin = x_d if l == 0 else xmid_d
        xout = out_d if l == L - 1 else xmid_d
        P.begin_phase()
        nw = P.sb("nw_sb", [128, 8])
        P.dma(nw[:], nw_d[l], [], [nw.b()])
        xts = [P.sb("xt%d" % i, [128, D]) for i in range(2)]
        junk = P.sb("junk", [128, D], BF16)
        xn = [P.sb("xn%d" % i, [128, D], BF16) for i in range(2)]
        ss = P.sb("ss", [128, 4])
        pts = [P.ps("pt%d" % i, [128, 8, 128], BF16) for i in range(2)]
        hts = [P.sb("hts%d" % i, [128, 8, 512], BF16) for i in range(2)]
        P.dma(xts[0][:], xin[0:128, :], [xin.b()], [xts[0].b()])
        for i in range(NT):
            xt = xts[i % 2]
            if i + 1 < NT:
                xn_ = xts[(i + 1) % 2]
                P.dma(xn_[:], xin[(i + 1) * 128:(i + 2) * 128, :], [xin.b()], [xn_.b()])
            P.act(junk[:], xt[:], AF.Square, [xt.b()], [junk.b(), ss.b()], accum_out=ss[:, 0:1])
            P.act(ss[:, 1:2], ss[:, 0:1], AF.Sqrt, [ss.b()], [ss.b()], scale=1.0 / D, bias=EPS)
            P.recip(ss[:, 2:3], ss[:, 1:2], [ss.b()], [ss.b()])
            x_n = xn[i % 2]
            P.act(x_n[:], xt[:], AF.Copy, [xt.b(), ss.b()], [x_n.b()], scale=ss[:, 2:3])
            pt = pts[i % 2]
            for c in range(8):
                P.tr(pt[:, c, :], x_n[:, c * 128:(c + 1) * 128], identb[:], [x_n.b()], [pt.b()])
            ht = hts[(i // 4) % 2]
            q = i % 4
            P.tt(ht[:, :, q * 128:(q + 1) * 128], pt[:], nw[:, :, None].to_broadcast([128, 8, 128]), ALU.mult,
                 [pt.b(), nw.b()], [ht.b()])
            if q == 3:
                t0 = (i // 4) * 512
                P.dma(hT_v[:, :, t0:t0 + 512], ht[:], [ht.b()], [hT_d.b()])
        P.end_phase()

        if debug == "A":
            break
        phase_s5(P, l, S, locals())
        if debug == "B":
            break
        phase_att(P, l, S, locals())
        if debug == "C":
            break
        phase_ssd(P, l, S, locals())
        if debug == "D":
            break
        phase_merge(P, l, S, locals(), xin, xout)

    P.barrier()
    P.emit()
    P.stack.close()
    return nc


def phase_merge(P, l, S, E, xin, xout):
    NB = S // 512
    identb = E["identb"]
    P.begin_phase()
    wg = P.sb("wg", [128, 8, 3072], BF16)
    wpa = P.sb("wpa", [128, 4, D], BF16)
    wpb = P.sb("wpb", [64, 4, D], BF16)
    wpc = P.sb("wpc", [128, 6, D], BF16)
    wo = P.sb("wo", [128, 8, D], BF16)
    P.begin_phase()
    stg = [P.sb("stg%d" % i, [128, 8, 512]) for i in range(2)]
    load_w(P, E["w_in_cols"](l, C_GATE, 3072), wg, 3072, 8, stg, "wg")
    load_w(P, E["pa_d"].t.ap()[l], wpa, D, 4, stg, "pa")
    vpb = E["pb_d"].t.ap()[l].rearrange("(k p) c -> p k c", p=64)
    for i, c0 in enumerate(range(0, D, 512)):
        st = stg[i % 2]
        P.dma(st[0:64, 0:4, :], vpb[:, :, c0:c0 + 512], [], [st.b()])
        P.cp(wpb[:, :, c0:c0 + 512], st[0:64, 0:4, :], [st.b()], [wpb.b()])
    load_w(P, E["pc_d"].t.ap()[l], wpc, D, 6, stg, "pc")
    load_w(P, E["wo_d"].t.ap()[l], wo, D, 8, stg, "wo")
    P.end_phase()
    hT_v = E["hT_v"]
    yTa_v = E["yTa_d"].t.ap().rearrange("c p t -> p c t")
    yTb_v = E["yTb_d"].t.ap().rearrange("c p t -> p c t")
    yTc_v = E["yTc_d"].t.ap().rearrange("c p t -> p c t")
    hts = [P.sb("mh%d" % i, [128, 8, 512], BF16) for i in range(2)]
    yas = [P.sb("mya%d" % i, [128, 4, 512], BF16) for i in range(2)]
    ybs = [P.sb("myb%d" % i, [64, 4, 512], BF16) for i in range(2)]
    ycs = [P.sb("myc%d" % i, [128, 6, 512], BF16) for i in range(2)]
    mg = [P.sb("mg%d" % i, [128, 8, 512], BF16) for i in range(2)]
    gt = [P.sb("gt%d" % i, [128, 512]) for i in range(2)]
    acc = [P.sb("macc%d" % i, [128, 512]) for i in range(2)]
    pg = [P.ps("pg%d" % i, [128, 512]) for i in range(2)]
    pp = [P.ps("pp%d" % i, [128, 512]) for i in range(2)]
    po = [P.ps("po%d" % i, [128, 512]) for i in range(2)]
    xr = [P.sb("xr%d" % i, [128, D]) for i in range(2)]
    xo = [P.sb("xo%d" % i, [128, D]) for i in range(2)]
    cnt = 0
    def mloads(b):
        t0 = b * 512
        ht, ya, yb, yc = hts[b % 2], yas[b % 2], ybs[b % 2], ycs[b % 2]
        P.dma(ht[:], hT_v[:, :, t0:t0 + 512], [E["hT_d"].b()], [ht.b()])
        P.dma(ya[:], yTa_v[:, :, t0:t0 + 512], [E["yTa_d"].b()], [ya.b()])
        P.dma(yb[:], yTb_v[:, :, t0:t0 + 512], [E["yTb_d"].b()], [yb.b()])
        P.dma(yc[:], yTc_v[:, :, t0:t0 + 512], [E["yTc_d"].b()], [yc.b()])

    def xload(xi):
        P.dma(xr[xi % 2][:], xin[xi * 128:(xi + 1) * 128, :], [xin.b()], [xr[xi % 2].b()])
    mloads(0)
    xload(0)
    for b in range(NB):
        t0 = b * 512
        ht, ya, yb, yc, m = hts[b % 2], yas[b % 2], ybs[b % 2], ycs[b % 2], mg[b % 2]
        if b + 1 < NB:
            mloads(b + 1)
        for dm in range(8):
            a = acc[dm % 2]
            for br in range(3):
                g_ps, p_ps, g_sb = pg[cnt % 2], pp[cnt % 2], gt[cnt % 2]
                cnt += 1
                col = br * D + dm * 128
                for k in range(8):
                    P.mm(g_ps[:], wg[:, k, col:col + 128], ht[:, k, :], [wg.b(), ht.b()], [g_ps.b()], start=(k == 0), stop=(k == 7))
                P.act(g_sb[:], g_ps[:], AF.Sigmoid, [g_ps.b()], [g_sb.b()])
                if br == 0:
                    for k in range(4):
                        P.mm(p_ps[:], wpa[:, k, dm * 128:(dm + 1) * 128], ya[:, k, :], [wpa.b(), ya.b()], [p_ps.b()], start=(k == 0), stop=(k == 3))
                elif br == 1:
                    for k in range(4):
                        P.mm(p_ps[:], wpb[:, k, dm * 128:(dm + 1) * 128], yb[:, k, :], [wpb.b(), yb.b()], [p_ps.b()], start=(k == 0), stop=(k == 3))
                else:
                    for k in range(6):
                        P.mm(p_ps[:], wpc[:, k, dm * 128:(dm + 1) * 128], yc[:, k, :], [wpc.b(), yc.b()], [p_ps.b()], start=(k == 0), stop=(k == 5))
                if br == 0:
                    P.tt(a[:], g_sb[:], p_ps[:], ALU.mult, [g_sb.b(), p_ps.b()], [a.b()])
                elif br == 1:
                    P.tt(g_sb[:], g_sb[:], p_ps[:], ALU.mult, [g_sb.b(), p_ps.b()], [g_sb.b()])
                    P.tt(a[:], a[:], g_sb[:], ALU.add, [a.b(), g_sb.b()], [a.b()], eng="pool")
                else:
                    P.tt(g_sb[:], g_sb[:], p_ps[:], ALU.mult, [g_sb.b(), p_ps.b()], [g_sb.b()])
                    P.tt(m[:, dm, :], a[:], g_sb[:], ALU.add, [a.b(), g_sb.b()], [m.b()], eng="pool")
        for tt_ in range(4):
            xi = b * 4 + tt_
            xr_, xo_ = xr[xi % 2], xo[xi % 2]
            if xi + 1 < NB * 4:
                xload(xi + 1)
            for hf in range(2):
                o_ps = po[hf]
                for k in range(8):
                    P.mm(o_ps[:], m[:, k, tt_ * 128:(tt_ + 1) * 128], wo[:, k, hf * 512:(hf + 1) * 512], [m.b(), wo.b()], [o_ps.b()], start=(k == 0), stop=(k == 7))
                P.tt(xo_[:, hf * 512:(hf + 1) * 512], xr_[:, hf * 512:(hf + 1) * 512], o_ps[:], ALU.add, [xr_.b(), o_ps.b()], [xo_.b()])
            P.dma(xout[xi * 128:(xi + 1) * 128, :], xo_[:], [xo_.b()], [xout.b()])
    P.end_phase()


def phase_ssd(P, l, S, E):
    NB = S // 512
    identb, identf, onesf = E["identb"], E["identf"], E["onesf"]
    P.begin_phase()
    wx = P.sb("wx", [128, 8, 1280], BF16)
    wdt = P.sb("wdt", [128, 8, 12], BF16)
    wzc = P.sb("wzc", [128, 8, 768], BF16)
    P.begin_phase()
    stg = [P.sb("stg%d" % i, [128, 8, 512]) for i in range(2)]
    load_w(P, E["w_in_cols"](l, C_XBC, 1280), wx, 1280, 8, stg, "wx")
    load_w(P, E["w_in_cols"](l, C_DT, 12), wdt, 12, 8, stg, "wdt")
    load_w(P, E["w_in_cols"](l, C_ZC, 768), wzc, 768, 8, stg, "wzc")
    P.end_phase()

    def ld(name, src, shape, dt=F32):
        t = P.sb(name, shape, dt)
        P.dma(t[:], src, [], [t.b()])
        return t
    convw = ld("convw", E["convw_d"][l], [128, 10, 4])
    convb = ld("convb", E["convb_d"][l], [128, 10])
    ssdv = ld("ssdv", E["ssdv_d"][l], [128, 3, 12])
    dcol = ld("dcol", E["ssdd_d"][l], [128, 6])
    nwcol = ld("nwcol", E["ssdnw_d"][l], [128, 6])
    utri = ld("utri", E["utri_d"][:], [128, 128])
    neg4 = ld("neg4", E["neg4_d"][:], [128, 4, 128])
    nega = P.sb("nega", [128, 12])
    P.act(nega[:], ssdv[:, 1, :], AF.Exp, [ssdv.b()], [nega.b()])
    P.ts(nega[:], nega[:], -1.0, None, ALU.mult, None, [nega.b()], [nega.b()])
    H = P.sb("H", [128, 12, 64])
    Hpad = P.sb("Hpad", [128, 12, 128], BF16)
    xdtpad = P.sb("xdtpad", [128, 12, 128], BF16)
    P.memset(H[:], 0.0, [H.b()])
    P.memset(Hpad[:], 0.0, [Hpad.b()])
    P.memset(xdtpad[:], 0.0, [xdtpad.b()])
    rawx = P.sb("rawx", [128, 10, 515])
    P.memset(rawx[:, :, 0:3], 0.0, [rawx.b()])
    cacc = P.sb("cacc", [128, 10, 512])
    xc = P.sb("xc", [128, 10, 512], BF16)
    zs = P.sb("zs", [128, 6, 512], BF16)
    ycb = [P.sb("ycb%d" % i, [128, 6, 512], BF16) for i in range(2)]
    hts = [P.sb("sh%d" % i, [128, 8, 512], BF16) for i in range(2)]
    dtb = P.sb("dtb", [128, 4, 12])
    nacs = P.sb("nacs", [128, 12])
    wdec = P.sb("wdec", [128, 12])
    uadt = P.sb("uadt", [128, 12, 128])
    xtok = P.sb("xtok", [128, 12, 64], BF16)
    btok = P.sb("btok", [128, 256], BF16)
    Eall = P.sb("Eall", [128, 12, 128])
    Lm = P.sb("Lm", [128, 12, 128], BF16)
    MT = P.sb("MT", [128, 12, 128], BF16)
    CTs = P.sb("CTs", [128, 12, 128], BF16)
    xdd = P.sb("xdd", [128, 12, 64], BF16)
    yf = P.sb("yf", [128, 6, 128])
    gg = P.sb("gg", [128, 6, 128])
    sq = P.sb("sq", [128, 6, 128])
    rstd = P.sb("rstd", [128, 128])
    htmp = P.sb("htmp", [128, 6, 64])
    px = P.ps("px", [128, 512])
    pmisc = P.ps("pmisc", [128, 512])
    ptr = P.ps("ptr", [128, 8, 128], BF16)
    prep = [P.ps("prep%d" % i, [128, 4, 128]) for i in range(2)]
    pyA = P.ps("pyA", [128, 4, 128])
    pyB = P.ps("pyB", [128, 4, 128])
    pS = P.ps("pS", [128, 512])
    pgt = pmisc[:, 0:256].rearrange("p (g l) -> p g l", g=2)
    hT_v = E["hT_v"]
    yTc_v = E["yTc_d"].t.ap().rearrange("c p t -> p c t")
    ev = 0
    P.dma(hts[0][:], hT_v[:, :, 0:512], [E["hT_d"].b()], [hts[0].b()])
    for b in range(NB):
        t0 = b * 512
        ht = hts[b % 2]
        if b + 1 < NB:
            P.dma(hts[(b + 1) % 2][:], hT_v[:, :, t0 + 512:t0 + 1024], [E["hT_d"].b()], [hts[(b + 1) % 2].b()])
        for tile in range(10):
            for k in range(8):
                P.mm(px[:], wx[:, k, tile * 128:(tile + 1) * 128], ht[:, k, :], [wx.b(), ht.b()], [px.b()], start=(k == 0), stop=(k == 7))
            P.cp(rawx[:, tile, 3:515], px[:], [px.b()], [rawx.b(tile)], eng="act")
        for tile in range(10):
            rb = rawx.b(tile)
            cb = cacc.b(tile)
            P.ts(cacc[:, tile, :], rawx[:, tile, 0:512], convw[:, tile, 0:1], None, ALU.mult, None, [rb, rawx.b(), convw.b()], [cb])
            for kk in range(1, 4):
                P.stt(cacc[:, tile, :], rawx[:, tile, kk:kk + 512], convw[:, tile, kk:kk + 1], cacc[:, tile, :], ALU.mult, ALU.add,
                      [rb, rawx.b(), cb], [cb])
            P.act(xc[:, tile, :], cacc[:, tile, :], AF.Silu, [cb, convb.b()], [xc.b(tile), xc.b()], bias=convb[:, tile:tile + 1])
            P.cp(rawx[:, tile, 0:3], rawx[:, tile, 512:515], [rb, rawx.b()], [rb, rawx.b()], eng="pool")
        for tile in range(6):
            for k in range(8):
                P.mm(px[:], wzc[:, k, tile * 128:(tile + 1) * 128], ht[:, k, :], [wzc.b(), ht.b()], [px.b()], start=(k == 0), stop=(k == 7))
            P.act(zs[:, tile, :], px[:], AF.Silu, [px.b()], [zs.b()])
        yc = ycb[b % 2]
        for cc in range(4):
            o = cc * 128
            pd = pmisc[:, 256:268]
            pcs = pmisc[:, 272:284]
            for k in range(8):
                P.mm(pd, ht[:, k, o:o + 128], wdt[:, k, :], [ht.b(), wdt.b()], [pmisc.b("pd")], start=(k == 0), stop=(k == 7))
            P.tt(dtb[:, 0, :], pd, ssdv[:, 0, :], ALU.add, [pmisc.b("pd"), ssdv.b()], [dtb.b()])
            P.act(dtb[:, 1, :], dtb[:, 0, :], AF.Exp, [dtb.b()], [dtb.b()])
            P.act(dtb[:, 2, :], dtb[:, 1, :], AF.Ln, [dtb.b()], [dtb.b()], bias=1.0)
            P.tt(dtb[:, 3, :], dtb[:, 2, :], nega[:], ALU.mult, [dtb.b(), nega.b()], [dtb.b()])
            P.mm(pcs, utri[:], dtb[:, 3, :], [utri.b(), dtb.b()], [pmisc.b("pcs")])
            P.ts(nacs[:], pcs, -1.0, None, ALU.mult, None, [pmisc.b("pcs")], [nacs.b()])
            P.tt(uadt[:], utri[:, None, :].to_broadcast([128, 12, 128]), dtb[:, 3, :, None].to_broadcast([128, 12, 128]), ALU.mult,
                 [utri.b(), dtb.b()], [uadt.b()], eng="pool")
            for tile in range(8):
                P.tr(ptr[:, tile, :], xc[:, tile, o:o + 128], identb[:], [xc.b(tile), xc.b()], [ptr.b()])
            P.cp(xtok[:].rearrange("p h d -> p (h d)"), ptr[:, 0:6, :].rearrange("p a b -> p (a b)"), [ptr.b()], [xtok.b()], eng="act")
            P.cp(btok[:], ptr[:, 6:8, :].rearrange("p a b -> p (a b)"), [ptr.b()], [btok.b()], eng="act")
            for g in range(2):
                P.mm(pgt[:, g, :], xc[:, 6 + g, o:o + 128], xc[:, 8 + g, o:o + 128], [xc.b()], [pmisc.b("pgt")])
            for q in range(3):
                pr = prep[q % 2]
                P.mm(pr[:].rearrange("p a b -> p (a b)"), onesf[:], uadt[:, 4 * q:4 * q + 4, :].rearrange("p a b -> p (a b)"),
                     [uadt.b()], [pr.b()])
                P.act(Eall[:, 4 * q:4 * q + 4, :], pr[:], AF.Exp, [pr.b()], [Eall.b()])
                P.mm(pr[:].rearrange("p a b -> p (a b)"), identf[:], neg4[:].rearrange("p a b -> p (a b)"),
                     [neg4.b(), Eall.b()], [pr.b()], start=False, stop=True)
                for hh in range(4):
                    h = 4 * q + hh
                    P.act(Lm[:, h, :], pr[:, hh, :], AF.Exp, [pr.b(), nacs.b()], [Lm.b()], bias=nacs[:, h:h + 1])
            for g in range(2):
                P.tt(MT[:, 6 * g:6 * g + 6, :], Lm[:, 6 * g:6 * g + 6, :], pgt[:, g:g + 1, :].to_broadcast([128, 6, 128]), ALU.mult,
                     [Lm.b(), pmisc.b("pgt")], [MT.b()])
                P.tt(CTs[:, 6 * g:6 * g + 6, :], Eall[:, 6 * g:6 * g + 6, :], xc[:, 8 + g:9 + g, o:o + 128].to_broadcast([128, 6, 128]), ALU.mult,
                     [Eall.b(), xc.b()], [CTs.b()], eng="pool")
            for r in range(2):
                P.tt(xdtpad[:, r::2, r * 64:r * 64 + 64], xtok[:, r::2, :], dtb[:, 2, r::2, None].to_broadcast([128, 6, 64]), ALU.mult,
                     [xtok.b(), dtb.b()], [xdtpad.b()])
            for pr6 in range(6):
                pyt = pyA[:, pr6, :] if pr6 < 4 else pyB[:, pr6 - 4, :]
                pyb = pyA.b() if pr6 < 4 else pyB.b("y")
                for r in range(2):
                    h = 2 * pr6 + r
                    P.mm(pyt, xdtpad[:, h, :], MT[:, h, :], [xdtpad.b(), MT.b()], [pyb], start=(r == 0), stop=False)
                    P.mm(pyt, Hpad[:, h, :], CTs[:, h, :], [Hpad.b(), CTs.b()], [pyb], start=False, stop=(r == 1))
                P.stt(yf[:, pr6, :], xc[:, pr6, o:o + 128], dcol[:, pr6:pr6 + 1], pyt, ALU.mult, ALU.add, [xc.b(), dcol.b(), pyb], [yf.b()])
            P.tt(gg[:], yf[:], zs[:, :, o:o + 128], ALU.mult, [yf.b(), zs.b()], [gg.b()])
            P.act(sq[:], gg[:], AF.Square, [gg.b()], [sq.b()])
            pss = pyB[:, 2, :]
            for tile in range(6):
                P.mm(pss, onesf[:], sq[:, tile, :], [sq.b()], [pyB.b("ss")], start=(tile == 0), stop=(tile == 5))
            P.act(rstd[:], pss, AF.Sqrt, [pyB.b("ss")], [rstd.b()], scale=1.0 / 768.0, bias=EPS)
            P.recip(rstd[:], rstd[:], [rstd.b()], [rstd.b()])
            for tile in range(6):
                P.stt(yc[:, tile, o:o + 128], gg[:, tile, :], nwcol[:, tile:tile + 1], rstd[:], ALU.mult, ALU.mult,
                      [gg.b(), nwcol.b(), rstd.b()], [yc.b()])
            P.tt(wdec[:], dtb[:, 2, :], Lm[:, :, 127], ALU.mult, [dtb.b(), Lm.b()], [wdec.b()])
            P.tt(xdd[:], xtok[:], wdec[:, :, None].to_broadcast([128, 12, 64]), ALU.mult, [xtok.b(), wdec.b()], [xdd.b()])
            for g in range(2):
                P.mm(pS[:, 0:384], btok[:, g * 128:(g + 1) * 128], xdd[:, 6 * g:6 * g + 6, :].rearrange("p a b -> p (a b)"),
                     [btok.b(), xdd.b()], [pS.b()])
                P.tt(htmp[:], H[:, 6 * g:6 * g + 6, :], Eall[:, 6 * g:6 * g + 6, 127:128].to_broadcast([128, 6, 64]), ALU.mult,
                     [H.b(), Eall.b()], [htmp.b()])
                P.tt(H[:, 6 * g:6 * g + 6, :], htmp[:], pS[:, 0:384].rearrange("p (a b) -> p a b", a=6), ALU.add,
                     [htmp.b(), pS.b()], [H.b()])
            for r in range(2):
                P.cp(Hpad[:, r::2, r * 64:r * 64 + 64], H[:, r::2, :], [H.b()], [Hpad.b()], eng="act")
        P.dma(yTc_v[:, :, t0:t0 + 512], yc[:], [yc.b()], [E["yTc_d"].b()])
    P.end_phase()


def phase_att(P, l, S, E):
    NSB = S // 2048
    onesb = E["onesb"]
    P.begin_phase()
    stg = [P.sb("stg%d" % i, [128, 8, 512]) for i in range(2)]
    qkw = P.sb("qkw", [128, 2])
    P.dma(qkw[:], E["qkw_d"][l], [], [qkw.b()])
    amask = P.sb("amask", [128, 2, 128], BF16)
    P.dma(amask[:], E["amask_d"][:], [], [amask.b()])
    bdones = P.sb("bdones", [128, 128], BF16)
    P.dma(bdones[:], E["bd_d"][:], [], [bdones.b()])
    wq = P.sb("wq", [128, 8, 256], BF16)
    wk = P.sb("wk", [128, 8, 256], BF16)
    wv = P.sb("wv", [128, 8, 256], BF16)
    hts = [P.sb("ah%d" % i, [128, 8, 2048], BF16) for i in range(1)]
    qT = [P.sb("qT%d" % h, [64, 2048], BF16) for h in range(4)]
    kT = [[P.sb("kT%d_%d" % (h, p), [64, 2048], BF16) for p in range(2)] for h in range(4)]
    V = [P.sb("V%d" % p, [128, 16, 4, 64], BF16) for p in range(2)]
    sqb = [P.sb("sqb%d" % i, [64, 512], BF16) for i in range(2)]
    rt = [P.sb("rt%d" % i, [64, 512]) for i in range(2)]
    pT = [P.sb("pT%d" % i, [128, 2, 128], BF16) for i in range(2)]
    ndS = [P.sb("ndS%d" % i, [64, 2, 2048]) for i in range(1)] * 2
    pq = [P.ps("pq%d" % i, [64, 512]) for i in range(2)]
    pn = P.ps("pn", [64, 512])
    pv = P.ps("pv", [128, 256])
    ps = [P.ps("ps%d" % i, [128, 2, 128]) for i in range(2)]
    pnd = [P.ps("pnd%d" % i, [64, 2, 128]) for i in range(2)]
    hT_v = E["hT_v"]
    num_d, den_d = E["num_d"], E["den_d"]
    bi = 0
    hi = 0
    qi = 0
    for g in range(3):
        d = DIL[g]
        nun = 2048 // (128 * d)
        load_w(P, E["w_in_cols"](l, C_Q + g * 256, 256), wq, 256, 8, stg, "wq")
        load_w(P, E["w_in_cols"](l, C_K + g * 256, 256), wk, 256, 8, stg, "wk")
        load_w(P, E["w_in_cols"](l, C_V + g * 256, 256), wv, 256, 8, stg, "wv")
        for sb_ in range(NSB):
            par = sb_ % 2
            T0 = sb_ * 2048
            ht = hts[0]
            P.dma(ht[:], hT_v[:, :, T0:T0 + 2048], [E["hT_d"].b()], [ht.b()])
            for pp in range(4):
                for (W, dst, wc) in ((wq, qT[pp], 0), (wk, kT[pp][par], 1)):
                    for tb in range(4):
                        p_q, s_q, r_t = pq[qi % 2], sqb[qi % 2], rt[qi % 2]
                        qi += 1
                        for k in range(8):
                            P.mm(p_q[:], W[:, k, pp * 64:(pp + 1) * 64], ht[:, k, tb * 512:(tb + 1) * 512], [W.b(), ht.b()], [p_q.b()],
                                 start=(k == 0), stop=(k == 7))
                        P.act(s_q[:], p_q[:], AF.Square, [p_q.b()], [s_q.b()])
                        P.mm(pn[:], onesb[0:64, 0:64], s_q[:], [s_q.b()], [pn.b()])
                        P.act(r_t[:], pn[:], AF.Ln, [pn.b()], [r_t.b()], scale=1.0 / 64.0, bias=EPS)
                        P.act(r_t[:], r_t[:], AF.Exp, [r_t.b()], [r_t.b()], scale=-0.5)
                        P.stt(dst[:, tb * 512:(tb + 1) * 512], p_q[:], qkw[0:64, wc:wc + 1], r_t[:], ALU.mult, ALU.mult,
                              [p_q.b(), qkw.b(), r_t.b()], [dst.b()])
            Vc = V[par]
            for blk in range(16):
                m, r = blk // d, blk % d
                off = m * 128 * d + r
                for k in range(8):
                    P.mm(pv[:], ht[:, k, off:off + 127 * d + 1:d], wv[:, k, :], [ht.b(), wv.b()], [pv.b()], start=(k == 0), stop=(k == 7))
                P.cp(Vc[:, blk, :, :].rearrange("p a b -> p (a b)"), pv[:], [pv.b()], [Vc.b()], eng="act")
            for hh in range(4):
                pp, ro = hh, 0
                qTh = qT[pp]
                kTc = kT[pp][par]
                nd = ndS[hi % 2]
                hi += 1
                def blkinfo(blk):
                    m, r = blk // d, blk % d
                    off = m * 128 * d + r
                    sl = slice(off, off + 127 * d + 1, d)
                    has_prev = (m > 0) or (sb_ > 0)
                    kprev = vprev = pb = None
                    if m > 0:
                        kprev = kTc[ro:ro + 64, off - 128 * d:off - d + 1:d]
                        vprev = Vc[:, blk - d, hh, :]
                        pb = [kTc.b(), Vc.b()]
                    elif sb_ > 0:
                        offp = (nun - 1) * 128 * d + r
                        kprev = kT[pp][1 - par][ro:ro + 64, offp:offp + 127 * d + 1:d]
                        vprev = V[1 - par][:, (nun - 1) * d + r, hh, :]
                        pb = [kT[pp][1 - par].b(), V[1 - par].b()]
                    return sl, has_prev, kprev, vprev, pb

                def qk(blk, bidx):
                    sl, has_prev, kprev, vprev, pb = blkinfo(blk)
                    p_s = ps[bidx % 2]
                    P.mm(p_s[:, 1, :], kTc[ro:ro + 64, sl], qTh[ro:ro + 64, sl], [kTc.b(), qTh.b()], [p_s.b()])
                    if has_prev:
                        P.mm(p_s[:, 0, :], kprev, qTh[ro:ro + 64, sl], [pb[0], qTh.b()], [p_s.b()])

                def rest(blk, bidx):
                    sl, has_prev, kprev, vprev, pb = blkinfo(blk)
                    p_s, p_T, p_nd = ps[bidx % 2], pT[bidx % 2], pnd[bidx % 2]
                    lo = 0 if has_prev else 1
                    P.act(p_T[:, lo:2, :], p_s[:, lo:2, :], AF.Exp, [p_s.b()], [p_T.b()], scale=0.125)
                    P.tt(p_T[:, lo:2, :], p_T[:, lo:2, :], amask[:, lo:2, :], ALU.mult, [p_T.b(), amask.b()], [p_T.b()], eng="pool")
                    P.mm(p_nd[:, 0, :], Vc[:, blk, hh, :], p_T[:, 1, :], [Vc.b(), p_T.b()], [p_nd.b()], start=True, stop=not has_prev)
                    if has_prev:
                        P.mm(p_nd[:, 0, :], vprev, p_T[:, 0, :], [pb[1], p_T.b()], [p_nd.b()], start=False, stop=True)
                    P.mm(p_nd[:, 1, :], onesb[:, 0:64], p_T[:, 1, :], [p_T.b()], [p_nd.b()], start=True, stop=not has_prev)
                    if has_prev:
                        P.mm(p_nd[:, 1, :], onesb[:, 0:64], p_T[:, 0, :], [p_T.b()], [p_nd.b()], start=False, stop=True)
                    P.cp(nd[:, :, sl], p_nd[:], [p_nd.b()], [nd.b()])
                qk(0, bi)
                for blk in range(16):
                    if blk + 1 < 16:
                        qk(blk + 1, bi + 1)
                    rest(blk, bi)
                    bi += 1
                P.dma(num_d.t.ap()[g, hh, :, T0:T0 + 2048], nd[:, 0, :], [nd.b()], [num_d.b()])
                P.dma(den_d.t.ap()[g, hh, :, T0:T0 + 2048], nd[:, 1, :], [nd.b()], [den_d.b()])
    P.end_phase()
    P.begin_phase()
    stg = [P.sb("stg%d" % i, [128, 8, 512]) for i in range(2)]
    wzb = P.sb("wzb", [128, 8, 256], BF16)
    load_w(P, E["w_in_cols"](l, C_ZB, 256), wzb, 256, 8, stg, "wzb")
    hts = [P.sb("ch%d" % i, [128, 8, 512], BF16) for i in range(2)]
    nn = [P.sb("nn%d" % i, [64, 3, 512]) for i in range(2)]
    dd = [P.sb("dd%d" % i, [64, 3, 512]) for i in range(2)]
    zb = P.sb("zb", [64, 512])
    yb = [P.sb("yb%d" % i, [64, 512], BF16) for i in range(2)]
    pz = [P.ps("pz%d" % i, [64, 512]) for i in range(2)]
    it = 0
    NBC = S // 512

    def cloads(itx):
        b, hh = itx // 4, itx % 4
        t0 = b * 512
        if hh == 0:
            ht = hts[b % 2]
            P.dma(ht[:], hT_v[:, :, t0:t0 + 512], [E["hT_d"].b()], [ht.b()])
        n_, d_ = nn[itx % 2], dd[itx % 2]
        P.dma(n_[:], num_d.t.ap()[:, hh, :, t0:t0 + 512].rearrange("g p t -> p g t"), [num_d.b()], [n_.b()])
        P.dma(d_[:], den_d.t.ap()[:, hh, :, t0:t0 + 512].rearrange("g p t -> p g t"), [den_d.b()], [d_.b()])
    cloads(0)
    for b in range(NBC):
        t0 = b * 512
        ht = hts[b % 2]
        for hh in range(4):
            n_, d_, y_, p_ = nn[it % 2], dd[it % 2], yb[it % 2], pz[it % 2]
            if it + 1 < NBC * 4:
                cloads(it + 1)
            it += 1
            for k in range(8):
                P.mm(p_[:], wzb[:, k, hh * 64:(hh + 1) * 64], ht[:, k, :], [wzb.b(), ht.b()], [p_.b()], start=(k == 0), stop=(k == 7))
            P.act(zb[:], p_[:], AF.Silu, [p_.b()], [zb.b()])
            P.tt(n_[:, 0, :], n_[:, 0, :], n_[:, 1, :], ALU.add, [n_.b()], [n_.b()])
            P.tt(n_[:, 0, :], n_[:, 0, :], n_[:, 2, :], ALU.add, [n_.b()], [n_.b()])
            P.tt(d_[:, 0, :], d_[:, 0, :], d_[:, 1, :], ALU.add, [d_.b()], [d_.b()], eng="pool")
            P.tt(d_[:, 0, :], d_[:, 0, :], d_[:, 2, :], ALU.add, [d_.b()], [d_.b()], eng="pool")
            P.act(d_[:, 1, :], d_[:, 0, :], AF.Ln, [d_.b()], [d_.b()])
            P.act(d_[:, 1, :], d_[:, 1, :], AF.Exp, [d_.b()], [d_.b()], scale=-1.0)
            P.tt(n_[:, 1, :], n_[:, 0, :], d_[:, 1, :], ALU.mult, [n_.b(), d_.b()], [n_.b()])
            P.tt(y_[:], n_[:, 1, :], zb[:], ALU.mult, [n_.b(), zb.b()], [y_.b()])
            P.dma(E["yTb_d"].t.ap()[hh, :, t0:t0 + 512], y_[:], [y_.b()], [E["yTb_d"].b()])
    P.end_phase()


def phase_s5(P, l, S, E):
    SEG = 512
    NSEG = S // SEG
    NCH = SEG // 8
    identf, onesb = E["identf"], E["onesb"]
    P.begin_phase()
    D0b = P.sb("D0b", [128, 32, 128], BF16)
    Bst = P.sb("Bst", [128, 32, 2, 64], BF16)
    Yb = P.sb("Yb", [64, 32, 2, 128], BF16)
    L8 = P.sb("L8", [64, 2, 32])
    P.begin_phase()
    tab = Buf("tab")
    R, W = [tab], [tab]
    par = P.sb("s5par", [64, 3, 32])
    bb = P.sb("s5bb", [64, 2, 32, 16])
    cc = P.sb("s5cc", [64, 2, 32, 16])
    bmask = P.sb("bmask", [128, 128])
    d0t = P.sb("d0t", [128, 128])
    dS5 = P.sb("dS5", [128, 32])
    P.dma(dS5[:], E["s5d_d"][l], [], W)
    P.dma(par[:], E["s5par_d"][l], [], W)
    P.dma(bb[:], E["s5b_d"][l], [], W)
    P.dma(cc[:], E["s5c_d"][l], [], W)
    P.dma(bmask[:], E["bmask_d"][:], [], W)
    tm = P.sb("s5tm", [64, 12, 32])
    LP = P.sb("LP", [64, 9, 2, 32])
    LN = P.sb("LN", [64, 9, 2, 32])
    BB = P.sb("BBar", [64, 2, 32, 16])
    XX = P.sb("XX", [64, 2, 32, 8, 16])
    YY = P.sb("YY", [64, 2, 32, 8, 16])
    X7 = P.sb("X7", [64, 2, 32, 8, 16])
    t16 = P.sb("t16", [64, 2, 32, 16])
    pD = P.ps("pD", [128, 128])
    pTr = P.ps("pTr", [128, 2, 64])
    are, aim, lst = par[:, 0, :], par[:, 1, :], par[:, 2, :]
    T_ = lambda i: tm[:, i, :]
    mul = lambda o, a, b: P.tt(o, a, b, ALU.mult, R, W)
    add = lambda o, a, b: P.tt(o, a, b, ALU.add, R, W)
    sub = lambda o, a, b: P.tt(o, a, b, ALU.subtract, R, W)
    P.act(T_(0), lst, AF.Exp, R, W)
    mul(T_(1), are, T_(0))
    mul(T_(2), aim, T_(0))
    P.act(T_(3), T_(1), AF.Exp, R, W, scale=1.0 / 16)
    hp = P.sb("halfpi", [64, 1])
    P.memset(hp[:], math.pi / 2, W)
    P.act(T_(4), T_(2), AF.Sin, R, W, scale=1.0 / 16, bias=hp[:, 0:1])
    P.act(T_(5), T_(2), AF.Sin, R, W, scale=1.0 / 16)
    lr, li = LP[:, 1, 0, :], LP[:, 1, 1, :]
    mul(lr, T_(3), T_(4))
    mul(li, T_(3), T_(5))
    for _ in range(4):
        mul(T_(6), lr, lr)
        mul(T_(7), li, li)
        mul(T_(8), lr, li)
        sub(lr, T_(6), T_(7))
        P.ts(li, T_(8), 2.0, None, ALU.mult, None, R, W)
    P.ts(T_(0), lr, -1.0, None, ALU.add, None, R, W)
    mul(T_(1), are, are)
    mul(T_(2), aim, aim)
    add(T_(1), T_(1), T_(2))
    P.recip(T_(1), T_(1), R, W)
    mul(T_(2), T_(0), are)
    mul(T_(3), li, aim)
    add(T_(2), T_(2), T_(3))
    mul(T_(9), T_(2), T_(1))
    mul(T_(2), li, are)
    mul(T_(3), T_(0), aim)
    sub(T_(2), T_(2), T_(3))
    mul(T_(10), T_(2), T_(1))
    mul(T_(0), lr, lr)
    mul(T_(1), li, li)
    add(T_(0), T_(0), T_(1))
    P.recip(T_(0), T_(0), R, W)
    mul(LN[:, 1, 0, :], lr, T_(0))
    mul(T_(1), li, T_(0))
    P.ts(LN[:, 1, 1, :], T_(1), -1.0, None, ALU.mult, None, R, W)
    P.memset(LP[:, 0, 0, :], 1.0, W)
    P.memset(LP[:, 0, 1, :], 0.0, W)

    def cmul(o_re, o_im, a_re, a_im, b_re, b_im, t1, t2):
        mul(t1, a_re, b_re)
        mul(t2, a_im, b_im)
        sub(o_re, t1, t2)
        mul(t1, a_re, b_im)
        mul(t2, a_im, b_re)
        add(o_im, t1, t2)
    for k in range(1, 8):
        cmul(LP[:, k + 1, 0, :], LP[:, k + 1, 1, :], LP[:, k, 0, :], LP[:, k, 1, :], LP[:, 1, 0, :], LP[:, 1, 1, :], T_(6), T_(7))
        cmul(LN[:, k + 1, 0, :], LN[:, k + 1, 1, :], LN[:, k, 0, :], LN[:, k, 1, :], LN[:, 1, 0, :], LN[:, 1, 1, :], T_(6), T_(7))
    bc = lambda ap: ap[:, :, None].to_broadcast([64, 32, 16])
    t1, t2 = t16[:, 0, :, :], t16[:, 1, :, :]
    cmul(BB[:, 0, :, :], BB[:, 1, :, :], bc(T_(9)), bc(T_(10)), bb[:, 0, :, :], bb[:, 1, :, :], t1, t2)
    for j in range(8):
        cmul(XX[:, 0, :, j, :], XX[:, 1, :, j, :], bc(LN[:, j + 1, 0, :]), bc(LN[:, j + 1, 1, :]), BB[:, 0, :, :], BB[:, 1, :, :], t1, t2)
        cmul(YY[:, 0, :, j, :], YY[:, 1, :, j, :], bc(LP[:, j + 1, 0, :]), bc(LP[:, j + 1, 1, :]), cc[:, 0, :, :], cc[:, 1, :, :], t1, t2)
        cmul(X7[:, 0, :, j, :], X7[:, 1, :, j, :], bc(LP[:, 7 - j, 0, :]), bc(LP[:, 7 - j, 1, :]), BB[:, 0, :, :], BB[:, 1, :, :], t1, t2)
    P.ts(YY[:, 1, :, :, :].rearrange("p g j o -> p (g j o)"), YY[:, 1, :, :, :].rearrange("p g j o -> p (g j o)"), -1.0, None, ALU.mult, None, R, W)
    P.cp(Yb[:, :, 0, :], YY[:, 0, :, :, :].rearrange("p g j o -> p g (j o)"), R, W)
    P.cp(Yb[:, :, 1, :], YY[:, 1, :, :, :].rearrange("p g j o -> p g (j o)"), R, W)
    P.cp(L8[:], LP[:, 8, :, :], R, W)
    for g in range(32):
        P.mm(pD[:], XX[:, 0, g, :, :].rearrange("p j i -> p (j i)"), YY[:, 0, g, :, :].rearrange("p j o -> p (j o)"), R, W, start=True, stop=False)
        P.mm(pD[:], XX[:, 1, g, :, :].rearrange("p j i -> p (j i)"), YY[:, 1, g, :, :].rearrange("p j o -> p (j o)"), R, W, start=False, stop=True)
        P.tt(d0t[:], pD[:], bmask[:], ALU.mult, R, W)
        P.stt(D0b[:, g, :], identf[:], dS5[:, g:g + 1], d0t[:], ALU.mult, ALU.add, R, W)
        for c in range(2):
            P.tr(pTr[:, c, :], X7[:, c, g, :, :].rearrange("p j i -> p (j i)"), identf[0:64, 0:64], R, W)
        P.cp(Bst[:, g, :, :], pTr[:], R, W)
    P.end_phase()
    wu = P.sb("wu", [128, 8, 512], BF16)
    wza = P.sb("wza", [128, 8, 512], BF16)
    glw = P.sb("glw", [128, 4, 512], BF16)
    P.begin_phase()
    stg = [P.sb("stg%d" % i, [128, 8, 512]) for i in range(2)]
    load_w(P, E["w_in_cols"](l, C_UA, 512), wu, 512, 8, stg, "wu")
    load_w(P, E["w_in_cols"](l, C_ZA, 512), wza, 512, 8, stg, "wza")
    load_w(P, E["gluw_d"].t.ap()[l], glw, 512, 4, stg, "glw")
    P.end_phase()

    def ld(name, src, shape, dt=F32):
        t = P.sb(name, shape, dt)
        P.dma(t[:], src, [], [t.b()])
        return t
    glub = ld("glub", E["glub_d"][l], [128, 4])
    selT = ld("selT", E["selT_d"][:], [128, 8, 8, 128], BF16)
    sel = ld("sel", E["sel_d"][:], [128, 8, 8, 128], BF16)
    A2 = P.sb("A2", [64, 2, 32])
    B2 = P.sb("B2", [64, 2, 32])
    P.cp(A2[:, 0, :], L8[:, 0, :], [tab], [A2.b()])
    P.cp(A2[:, 1, :], L8[:, 0, :], [tab], [A2.b()])
    P.cp(B2[:, 1, :], L8[:, 1, :], [tab], [B2.b()])
    P.ts(B2[:, 0, :], L8[:, 1, :], -1.0, None, ALU.mult, None, [tab], [B2.b()])
    hts = [P.sb("s5ht%d" % i, [128, 8, SEG], BF16) for i in range(2)]
    uT = P.sb("uT", [128, 4, SEG], BF16)
    Ugs = [P.sb("Ug%d" % i, [128, 32, NCH], BF16) for i in range(2)]
    HHs = [P.sb("HH%d" % i, [64, 2, 32, NCH]) for i in range(2)]
    Hbs = [P.sb("Hb%d" % i, [64, 2, 32, NCH], BF16) for i in range(1)] * 2
    carry = P.sb("carry", [64, 2, 32])
    P.memset(carry[:], 0.0, [carry.b()])
    sct = P.sb("sct", [64, 2, 2, 32])
    glg = P.sb("glg", [128, 32, NCH], BF16)
    gT = P.sb("gT", [128, 4, SEG], BF16)
    y1 = P.sb("y1", [128, 8, NCH])
    y2 = P.sb("y2", [128, 8, NCH])
    sgs = P.sb("sgs", [128, 512])
    zsa = P.sb("zsa", [128, 512])
    yaS = [P.sb("yaS%d" % i, [128, 4, 512], BF16) for i in range(1)] * 2
    pA = [P.ps("pA%d" % i, [128, 512]) for i in range(2)]
    pU = [P.ps("pU%d" % i, [128, 8, NCH]) for i in range(2)]
    pY = [P.ps("pY%d" % i, [128, NCH]) for i in range(2)]
    yp = [P.sb("yp%d" % i, [128, NCH]) for i in range(6)]
    yq = [P.sb("yq%d" % i, [128, NCH]) for i in range(6)]
    pSr = P.ps("pSr", [64, 8, NCH])
    pSi = P.ps("pSi", [64, 8, NCH])
    hT_v = E["hT_v"]
    yTa_v = E["yTa_d"].t.ap().rearrange("c p t -> p c t")
    import os
    SCAN_ENG = os.environ.get("SCAN_ENG", "pool")
    lvl = int(os.environ.get("S5DBG", "9"))
    cnt = {"a": 0, "u": 0}

    def pass1(sg):
        T0 = sg * SEG
        ht, Ug, HH = hts[sg % 2], Ugs[sg % 2], HHs[sg % 2]
        P.dma(ht[:], hT_v[:, :, T0:T0 + SEG], [E["hT_d"].b()], [ht.b()])
        for ct in range(4):
            for tb in range(SEG // 512):
                p_ = pA[cnt["a"] % 2]
                cnt["a"] += 1
                for k in range(8):
                    P.mm(p_[:], wu[:, k, ct * 128:(ct + 1) * 128], ht[:, k, tb * 512:(tb + 1) * 512], [wu.b(), ht.b()], [p_.b()], start=(k == 0), stop=(k == 7))
                P.cp(uT[:, ct, tb * 512:(tb + 1) * 512], p_[:], [p_.b()], [uT.b()], eng=("act" if cnt["a"] % 2 else "dve"))
        for bq in range(4):
            p_u = pU[cnt["u"] % 2]
            cnt["u"] += 1
            for gl in range(8):
                for j in range(8):
                    P.mm(p_u[:, gl, :], selT[:, gl, j, :], uT[:, bq, j:SEG:8], [selT.b(), uT.b()], [p_u.b()], start=(j == 0), stop=(j == 7))
            P.cp(Ug[:, 8 * bq:8 * bq + 8, :], p_u[:], [p_u.b()], [Ug.b(bq)], eng="act")
            for gl in range(8):
                g = 8 * bq + gl
                P.mm(pSr[:, gl, :], Bst[:, g, 0, :], Ug[:, g, :], [tab, Ug.b(bq)], [pSr.b()])
                P.mm(pSi[:, gl, :], Bst[:, g, 1, :], Ug[:, g, :], [tab, Ug.b(bq)], [pSi.b()])
            P.cp(HH[:, 0, 8 * bq:8 * bq + 8, :], pSr[:], [pSr.b()], [HH.b()])
            P.cp(HH[:, 1, 8 * bq:8 * bq + 8, :], pSi[:], [pSi.b()], [HH.b()], eng="act")

    def scan(sg):
        HH, Hb = HHs[sg % 2], Hbs[sg % 2]
        hb = [HH.b()]
        for c in range(NCH):
            if c == 0:
                hall, hsw = carry[:], carry[:, ::-1, :]
                rb = [carry.b(), HH.b()]
            else:
                hall, hsw = HH[:, :, :, c - 1], HH[:, ::-1, :, c - 1]
                rb = [HH.b()]
            sb_ = [sct.b()]
            P.tt(sct[:, 0, :, :], A2[:], hall, ALU.mult, rb + [A2.b()], sb_, eng=SCAN_ENG)
            P.tt(sct[:, 1, :, :], B2[:], hsw, ALU.mult, rb + [B2.b()], sb_, eng=SCAN_ENG)
            P.tt(HH[:, :, :, c], HH[:, :, :, c], sct[:, 0, :, :], ALU.add, sb_ + hb, hb, eng=SCAN_ENG)
            P.tt(HH[:, :, :, c], HH[:, :, :, c], sct[:, 1, :, :], ALU.add, sb_ + hb, hb, eng=SCAN_ENG)
        P.cp(Hb[:, :, :, 0], carry[:], [carry.b()], [Hb.b()], eng=SCAN_ENG)
        P.cp(Hb[:, :, :, 1:NCH], HH[:, :, :, 0:NCH - 1], [HH.b()], [Hb.b()], eng=SCAN_ENG)
        P.cp(carry[:], HH[:, :, :, NCH - 1], [HH.b(), Hb.b()], [carry.b()], eng=SCAN_ENG)

    def pass2(sg):
        T0 = sg * SEG
        ht, Ug, Hb = hts[sg % 2], Ugs[sg % 2], Hbs[sg % 2]
        NBUF = 6

        def st1(g):
            bq = g // 8
            p_y = pY[g % 2]
            ya1 = yp[g % NBUF]
            P.mm(p_y[:], D0b[:, g, :], Ug[:, g, :], [tab, Ug.b(bq)], [p_y.b()], start=True, stop=False)
            P.mm(p_y[:], Yb[:, g, 0, :], Hb[:, 0, g, :], [Hb.b()], [p_y.b()], start=False, stop=False)
            P.mm(p_y[:], Yb[:, g, 1, :], Hb[:, 1, g, :], [Hb.b()], [p_y.b()], start=False, stop=True)
            P.cp(ya1[:], p_y[:], [p_y.b()], [ya1.b()])

        def st2(g):
            ya1, ya2 = yp[g % NBUF], yq[g % NBUF]
            P.act(ya2[:], ya1[:], AF.Square, [ya1.b()], [ya2.b()], scale=0.21145921592590868)

        def st3(g):
            ya1, ya2 = yp[g % NBUF], yq[g % NBUF]
            P.stt(ya2[:], ya2[:], 1.0, ya1[:], ALU.add, ALU.mult, [ya2.b(), ya1.b()], [ya2.b()])

        def st4(g):
            ya2 = yq[g % NBUF]
            P.act(ya2[:], ya2[:], AF.Sigmoid, [ya2.b()], [ya2.b()], scale=1.5957691216)

        def st5(g):
            ya1, ya2 = yp[g % NBUF], yq[g % NBUF]
            P.tt(glg[:, g, :], ya2[:], ya1[:], ALU.mult, [ya2.b(), ya1.b()], [glg.b(g // 8)])
        stages = [st1, st2, st3, st4, st5]
        for it in range(32 + len(stages) - 1):
            for si, st in enumerate(stages):
                g = it - si
                if 0 <= g < 32:
                    st(g)
        if lvl < 4:
            return
        for ct in range(4):
            p_f = pU[cnt["u"] % 2]
            cnt["u"] += 1
            for jp in range(8):
                for gl in range(8):
                    P.mm(p_f[:, jp, :], sel[:, gl, jp, :], glg[:, ct * 8 + gl, :], [sel.b(), glg.b(ct)], [p_f.b()], start=(gl == 0), stop=(gl == 7))
            P.cp(gT[:, ct, :].rearrange("p (c j) -> p j c", j=8), p_f[:], [p_f.b()], [gT.b()], eng=("act" if ct % 2 else "dve"))
        if lvl < 5:
            return
        for tb in range(SEG // 512):
            ya = yaS[sg % 2]
            ts_ = slice(tb * 512, (tb + 1) * 512)
            for co in range(4):
                p_ = pA[cnt["a"] % 2]
                pZ = pA[(cnt["a"] + 1) % 2]
                cnt["a"] += 1
                for ct in range(4):
                    P.mm(p_[:], glw[:, ct, co * 128:(co + 1) * 128], gT[:, ct, ts_], [glw.b(), gT.b()], [p_.b()], start=(ct == 0), stop=(ct == 3))
                P.act(sgs[:], p_[:], AF.Sigmoid, [p_.b(), glub.b()], [sgs.b()], bias=glub[:, co:co + 1])
                for k in range(8):
                    P.mm(pZ[:], wza[:, k, co * 128:(co + 1) * 128], ht[:, k, ts_], [wza.b(), ht.b()], [pZ.b()], start=(k == 0), stop=(k == 7))
                P.act(zsa[:], pZ[:], AF.Silu, [pZ.b()], [zsa.b()])
                P.tt(sgs[:], sgs[:], gT[:, co, ts_], ALU.mult, [sgs.b(), gT.b()], [sgs.b()])
                P.tt(ya[:, co, :], sgs[:], zsa[:], ALU.mult, [sgs.b(), zsa.b()], [ya.b()])
            P.dma(yTa_v[:, :, T0 + tb * 512:T0 + (tb + 1) * 512], ya[:], [ya.b()], [E["yTa_d"].b()])

    if lvl >= 1:
        pass1(0)
    for sg in range(NSEG if lvl >= 9 else 1):
        if lvl >= 2:
            scan(sg)
        if sg + 1 < NSEG and lvl >= 9:
            pass1(sg + 1)
        if lvl >= 3:
            pass2(sg)
    P.end_phase()


def host_consts():
    bf = ml_dtypes.bfloat16
    c = {}
    c["c_identb"] = np.eye(128).astype(bf)
    c["c_identf"] = np.eye(128, dtype=np.float32)
    c["c_onesb"] = np.ones((128, 128)).astype(bf)
    c["c_onesf"] = np.ones((128, 128), np.float32)
    k = np.arange(128)
    bd = np.zeros((128, 128), np.float32)
    bd[:64, :64] = 1.0
    bd[64:, 64:] = 1.0
    c["c_bdones"] = bd.astype(bf)
    c["c_utri"] = (k[:, None] <= k[None, :]).astype(np.float32)
    neg = np.where(k[:, None] > k[None, :], NEGV, 0.0).astype(np.float32)
    c["c_neg4"] = np.ascontiguousarray(np.broadcast_to(neg[:, None, :], (128, 4, 128)))
    am = np.zeros((128, 2, 128), np.float32)
    am[:, 0, :] = (k[:, None] >= k[None, :])
    am[:, 1, :] = (k[None, :] >= k[:, None])
    c["c_amask"] = am.astype(bf)
    c["c_bmask"] = ((k[None, :] // 16) >= (k[:, None] // 16)).astype(np.float32)
    selT = np.zeros((128, 8, 8, 128), np.float32)
    sel = np.zeros((128, 8, 8, 128), np.float32)
    for gl in range(8):
        for j in range(8):
            for i in range(16):
                selT[16 * gl + i, gl, j, 16 * j + i] = 1.0
                sel[16 * j + i, gl, j, 16 * gl + i] = 1.0
    c["c_selT"] = selT.astype(bf)
    c["c_sel"] = sel.astype(bf)
    return c


def host_params(inp):
    f = lambda a: np.ascontiguousarray(np.asarray(a, dtype=np.float32))
    L = inp["norm_w"].shape[0]
    o = {}
    o["w_in"] = f(inp["w_in"])
    o["nw"] = f(np.asarray(inp["norm_w"]).reshape(L, 8, 128).transpose(0, 2, 1))
    are = np.asarray(inp["s5_a_re"]).transpose(0, 2, 1)
    aim = np.asarray(inp["s5_a_im"]).transpose(0, 2, 1)
    lst = np.broadcast_to(np.asarray(inp["s5_log_step"])[:, None, :], (L, 64, 32))
    o["s5par"] = f(np.stack([are, aim, lst], axis=2))
    bre = np.asarray(inp["s5_b_re"]).transpose(0, 2, 1, 3)
    bim = np.asarray(inp["s5_b_im"]).transpose(0, 2, 1, 3)
    o["s5b"] = f(np.stack([bre, bim], axis=2))
    cre = np.asarray(inp["s5_c_re"]).transpose(0, 3, 1, 2)
    cim = np.asarray(inp["s5_c_im"]).transpose(0, 3, 1, 2)
    o["s5c"] = f(np.stack([cre, cim], axis=2))
    d = np.asarray(inp["s5_d"]).reshape(L, 32, 16)
    o["s5d"] = f(np.broadcast_to(d.transpose(0, 2, 1)[:, None, :, :], (L, 8, 16, 32)).reshape(L, 128, 32))
    o["gluw"] = f(inp["s5_glu_w"])
    o["glub"] = f(np.asarray(inp["s5_glu_b"]).reshape(L, 4, 128).transpose(0, 2, 1))
    qk = np.stack([np.asarray(inp["q_norm_w"]), np.asarray(inp["k_norm_w"])], axis=2)
    o["qkw"] = f(np.concatenate([qk, qk], axis=1))
    cw = np.asarray(inp["conv_w"]).reshape(L, 4, 10, 128)
    o["convw"] = f(cw.transpose(0, 3, 2, 1))
    o["convb"] = f(np.asarray(inp["conv_b"]).reshape(L, 10, 128).transpose(0, 2, 1))
    sv = np.stack([np.asarray(inp["dt_bias"]), np.asarray(inp["ssd_a_log"]), np.asarray(inp["ssd_d"])], axis=1)
    o["ssdv"] = f(np.broadcast_to(sv[:, None, :, :], (L, 128, 3, 12)))
    dd = np.repeat(np.asarray(inp["ssd_d"]), 64, axis=1)
    o["ssdd"] = f(dd.reshape(L, 6, 128).transpose(0, 2, 1))
    o["ssdnw"] = f(np.asarray(inp["ssd_norm_w"]).reshape(L, 6, 128).transpose(0, 2, 1))
    o["proj_a"] = f(inp["proj_a"])
    o["proj_b"] = f(inp["proj_b"])
    o["proj_c"] = f(inp["proj_c"])
    o["w_out"] = f(inp["w_out"])
    return o


_NC_CACHE = {}


def kernel(**inputs):
    x = np.asarray(inputs["x"], dtype=np.float32)
    B, S, _ = x.shape
    L = np.asarray(inputs["norm_w"]).shape[0]
    key = (S, L)
    if key not in _NC_CACHE:
        _NC_CACHE[key] = build(S, L)
    nc = _NC_CACHE[key]
    shared = host_params(inputs)
    shared.update(host_consts())
    in_maps = []
    for b in range(B):
        m = dict(shared)
        m["x"] = np.ascontiguousarray(x[b])
        in_maps.append(m)
    res = run_bass_kernel_spmd(nc, in_maps, core_ids=list(range(B)))
    out = np.stack([np.asarray(r["out"], dtype=np.float32) for r in res.results], axis=0)
    return out
```

```python
from contextlib import ExitStack
import numpy as np
import concourse.bass as bass
import concourse.mybir as mybir

F32 = mybir.dt.float32
BF16 = mybir.dt.bfloat16
AF = mybir.ActivationFunctionType
ALU = mybir.AluOpType
AX = mybir.AxisListType

ENGS = ["pe", "act", "dve", "pool", "sp"]
NDMA = 40
SAME_SYNC = {"act", "dve", "pool"}


class Buf:
    __slots__ = ("name", "w", "r")

    def __init__(self, name):
        self.name = name
        self.w = None
        self.r = {}


class Tl:
    def __init__(self, t, name):
        self.t = t
        self.name = name
        self.bufs = {}

    def b(self, key=None):
        if key not in self.bufs:
            self.bufs[key] = Buf("%s/%s" % (self.name, key))
        return self.bufs[key]

    def __getitem__(self, idx):
        return self.t[idx]


class Prog:
    def __init__(self, nc):
        self.nc = nc
        self.ops = {e: [] for e in ENGS}
        self.cnt = {e: 0 for e in ENGS}
        self.waited = {e: {} for e in ENGS}
        self.dma_uses = [0] * NDMA
        self.dma_rr = 0
        self.stack = ExitStack()
        self.pstacks = []
        self.n_ops = 0

    def _ctx(self, persistent):
        return self.stack if (persistent or not self.pstacks) else self.pstacks[-1]

    def sb(self, name, shape, dt=F32, persistent=False):
        self.n_ops += 1
        name = "sb%d_%s" % (self.n_ops, name)
        t = self._ctx(persistent).enter_context(self.nc.sbuf_tensor(name, list(shape), dt))
        return Tl(t, name)

    def ps(self, name, shape, dt=F32, persistent=False):
        self.n_ops += 1
        name = "ps%d_%s" % (self.n_ops, name)
        t = self._ctx(persistent).enter_context(self.nc.psum_tensor(name, list(shape), dt))
        return Tl(t, name)

    def dram(self, name, shape, dt, kind="Internal"):
        t = self.nc.dram_tensor(name, list(shape), dt, kind=kind)
        return Tl(t, name)

    def begin_phase(self):
        self.pstacks.append(ExitStack())

    def end_phase(self):
        self.barrier()
        self.pstacks.pop().close()

    def capture(self, f):
        self._cap = []
        try:
            f()
        finally:
            lst, self._cap = self._cap, None
        return lst

    def replay_interleaved(self, a, b):
        na, nb = len(a), len(b)
        ia = ib = 0
        while ia < na or ib < nb:
            if ib >= nb or (ia < na and ia * nb <= ib * na):
                self.op(*a[ia])
                ia += 1
            else:
                self.op(*b[ib])
                ib += 1

    def op(self, eng, fn, reads=(), writes=(), dma=False):
        if getattr(self, "_cap", None) is not None:
            self._cap.append((eng, fn, tuple(reads), tuple(writes), dma))
            return None
        deps = {}
        for b in reads:
            if b.w is not None:
                k, v = b.w
                deps[k] = max(deps.get(k, 0), v)
        for b in writes:
            if b.w is not None:
                k, v = b.w
                deps[k] = max(deps.get(k, 0), v)
            for k, v in b.r.items():
                deps[k] = max(deps.get(k, 0), v)
        waits = []
        wd = self.waited[eng]
        for k, v in deps.items():
            if k == ("eng", eng) and eng not in SAME_SYNC:
                continue
            if wd.get(k, 0) < v:
                wd[k] = v
                waits.append((k, v))
        if dma:
            j = self.dma_rr
            self.dma_rr = (j + 1) % NDMA
            prev = self.dma_uses[j] * 16
            k = ("dma", j)
            if prev and wd.get(k, 0) < prev:
                wd[k] = prev
                waits.append((k, prev))
            self.dma_uses[j] += 1
            tok = (k, self.dma_uses[j] * 16)
        else:
            self.cnt[eng] += 1
            tok = (("eng", eng), self.cnt[eng])
        self.ops[eng].append((waits, fn, tok))
        self.n_ops += 1
        for b in reads:
            k, v = tok
            b.r[k] = max(b.r.get(k, 0), v)
        for b in writes:
            b.w = tok
            b.r = {}
        return tok

    def barrier(self):
        targets = {}
        for e in ENGS:
            if self.cnt[e]:
                targets[("eng", e)] = self.cnt[e]
        for j in range(NDMA):
            if self.dma_uses[j]:
                targets[("dma", j)] = self.dma_uses[j] * 16
        for e in ENGS:
            waits = []
            for k, v in targets.items():
                if k == ("eng", e):
                    continue
                if self.waited[e].get(k, 0) < v:
                    self.waited[e][k] = v
                    waits.append((k, v))
            if waits:
                self.ops[e].append((waits, None, None))

    def emit(self):
        nc = self.nc
        with ExitStack() as es:
            sems = {}
            for e in ENGS:
                sems[("eng", e)] = es.enter_context(nc.semaphore("s_" + e))
            for j in range(NDMA):
                sems[("dma", j)] = es.enter_context(nc.semaphore("d_%d" % j))
            block = es.enter_context(nc.Block())

            def run(engname, eng):
                for waits, fn, tok in self.ops[engname]:
                    for k, v in waits:
                        eng.wait_ge(sems[k], v)
                    if fn is None:
                        continue
                    ins = fn(eng)
                    k, v = tok
                    if k[0] == "dma":
                        ins.then_inc(sems[k], 16)
                    else:
                        ins.then_inc(sems[k], 1)

            @block.tensor
            def _(eng):
                run("pe", eng)

            @block.scalar
            def _(eng):
                run("act", eng)

            @block.vector
            def _(eng):
                run("dve", eng)

            @block.gpsimd
            def _(eng):
                run("pool", eng)

            @block.sync
            def _(eng):
                run("sp", eng)

    def dma(self, out, in_, reads, writes, eng="sp", **kw):
        return self.op(eng, lambda e: e.dma_start(out=out, in_=in_, **kw), reads, writes, dma=True)

    def mm(self, out, lhsT, rhs, reads, writes, start=True, stop=True, **kw):
        return self.op("pe", lambda e: e.matmul(out, lhsT, rhs, start=start, stop=stop, **kw), reads, writes)

    def tr(self, out, in_, ident, reads, writes):
        return self.op("pe", lambda e: e.transpose(out, in_, ident), reads, writes)

    def act(self, out, in_, func, reads, writes, eng="act", **kw):
        return self.op(eng, lambda e: e.activation(out, in_, func, **kw), reads, writes)

    def tt(self, out, in0, in1, op, reads, writes, eng="dve"):
        return self.op(eng, lambda e: e.tensor_tensor(out, in0, in1, op), reads, writes)

    def ts(self, out, in0, s1, s2, op0, op1, reads, writes, eng="dve", **kw):
        if op1 is None:
            return self.op(eng, lambda e: e.tensor_scalar(out, in0, s1, None, op0, **kw), reads, writes)
        return self.op(eng, lambda e: e.tensor_scalar(out, in0, s1, s2, op0, op1, **kw), reads, writes)

    def stt(self, out, in0, scalar, in1, op0, op1, reads, writes, eng="dve"):
        return self.op(eng, lambda e: e.scalar_tensor_tensor(out, in0, scalar, in1, op0, op1), reads, writes)

    def cp(self, out, in_, reads, writes, eng="dve"):
        if eng == "act":
            return self.op(eng, lambda e: e.copy(out, in_), reads, writes)
        return self.op(eng, lambda e: e.tensor_copy(out, in_), reads, writes)

    def memset(self, ap, val, writes, eng="dve"):
        return self.op(eng, lambda e: e.memset(ap, val), (), writes)

    def recip(self, out, in_, reads, writes):
        return self.op("dve", lambda e: e.reciprocal(out, in_), reads, writes)


import math
import numpy as np, ml_dtypes
from concourse.bass_utils import run_bass_kernel_spmd

D = 1024
INW = 8716
C_UA, C_ZA, C_Q, C_K, C_V, C_ZB, C_XBC, C_DT, C_ZC, C_GATE = 0, 512, 1024, 1792, 2560, 3328, 3584, 4864, 4876, 5644
EPS = 1e-6
DIL = (1, 4, 16)
NEGV = -1.0e5


def load_w(P, src_ap, dst, ncols, kt, stg, tag):
    v = src_ap.rearrange("(k p) c -> p k c", p=128)
    i = 0
    for c0 in range(0, ncols, 512):
        c1 = min(ncols, c0 + 512)
        st = stg[i % 2]
        P.dma(st[:, 0:kt, 0:c1 - c0], v[:, :, c0:c1], [], [st.b()])
        P.cp(dst[:, :, c0:c1], st[:, 0:kt, 0:c1 - c0], [st.b()], [dst.b()], eng=("act" if i % 2 else "dve"))
        i += 1


def build(S, L=2, debug=None):
    nc = bass.Bass("TRN2", target_bir_lowering=False)
    P = Prog(nc)
    NT = S // 128
    NB = S // 512
    NSB = S // 2048
    ein = lambda n, sh, dt=F32: P.dram(n, sh, dt, kind="ExternalInput")
    x_d = ein("x", [S, D])
    out_d = P.dram("out", [S, D], F32, kind="ExternalOutput")
    w_in_d = ein("w_in", [L, D, INW])
    nw_d = ein("nw", [L, 128, 8])
    s5par_d = ein("s5par", [L, 64, 3, 32])
    s5b_d = ein("s5b", [L, 64, 2, 32, 16])
    s5c_d = ein("s5c", [L, 64, 2, 32, 16])
    s5d_d = ein("s5d", [L, 128, 32])
    gluw_d = ein("gluw", [L, 512, 512])
    glub_d = ein("glub", [L, 128, 4])
    qkw_d = ein("qkw", [L, 128, 2])
    bd_d = ein("c_bdones", [128, 128], BF16)
    convw_d = ein("convw", [L, 128, 10, 4])
    convb_d = ein("convb", [L, 128, 10])
    ssdv_d = ein("ssdv", [L, 128, 3, 12])
    ssdd_d = ein("ssdd", [L, 128, 6])
    ssdnw_d = ein("ssdnw", [L, 128, 6])
    pa_d = ein("proj_a", [L, 512, D])
    pb_d = ein("proj_b", [L, 256, D])
    pc_d = ein("proj_c", [L, 768, D])
    wo_d = ein("w_out", [L, D, D])
    idb_d = ein("c_identb", [128, 128], BF16)
    idf_d = ein("c_identf", [128, 128])
    onesb_d = ein("c_onesb", [128, 128], BF16)
    onesf_d = ein("c_onesf", [128, 128])
    utri_d = ein("c_utri", [128, 128])
    neg4_d = ein("c_neg4", [128, 4, 128])
    amask_d = ein("c_amask", [128, 2, 128], BF16)
    bmask_d = ein("c_bmask", [128, 128])
    selT_d = ein("c_selT", [128, 8, 8, 128], BF16)
    sel_d = ein("c_sel", [128, 8, 8, 128], BF16)
    xmid_d = P.dram("xmid", [S, D], F32)
    hT_d = P.dram("hT", [8, 128, S], BF16)
    yTa_d = P.dram("yTa", [4, 128, S], BF16)
    yTb_d = P.dram("yTb", [4, 64, S], BF16)
    yTc_d = P.dram("yTc", [6, 128, S], BF16)
    num_d = P.dram("attnum", [3, 4, 64, S], F32)
    den_d = P.dram("attden", [3, 4, 64, S], F32)

    def cload(name, d, shape, dt=F32):
        t = P.sb(name, shape, dt, persistent=True)
        P.dma(t[:], d[:], [], [t.b()])
        return t

    identb = cload("identb", idb_d, [128, 128], BF16)
    identf = cload("identf", idf_d, [128, 128])
    onesb = cload("onesb", onesb_d, [128, 128], BF16)
    onesf = cload("onesf", onesf_d, [128, 128])

    def w_in_cols(l, c0, n):
        return w_in_d.t.ap()[l, :, c0:c0 + n]

    hT_v = hT_d.t.ap().rearrange("c p t -> p c t")

    for l in range(L):
        xin = x_d if l == 0 else xmid_d
        xout = out_d if l == L - 1 else xmid_d
        P.begin_phase()
        nw = P.sb("nw_sb", [128, 8])
        P.dma(nw[:], nw_d[l], [], [nw.b()])
        xts = [P.sb("xt%d" % i, [128, D]) for i in range(2)]
        junk = P.sb("junk", [128, D], BF16)
        xn = [P.sb("xn%d" % i, [128, D], BF16) for i in range(2)]
        ss = P.sb("ss", [128, 4])
        pts = [P.ps("pt%d" % i, [128, 8, 128], BF16) for i in range(2)]
        hts = [P.sb("hts%d" % i, [128, 8, 512], BF16) for i in range(2)]
        P.dma(xts[0][:], xin[0:128, :], [xin.b()], [xts[0].b()])
        for i in range(NT):
            xt = xts[i % 2]
            if i + 1 < NT:
                xn_ = xts[(i + 1) % 2]
                P.dma(xn_[:], xin[(i + 1) * 128:(i + 2) * 128, :], [xin.b()], [xn_.b()])
            P.act(junk[:], xt[:], AF.Square, [xt.b()], [junk.b(), ss.b()], accum_out=ss[:, 0:1])
            P.act(ss[:, 1:2], ss[:, 0:1], AF.Sqrt, [ss.b()], [ss.b()], scale=1.0 / D, bias=EPS)
            P.recip(ss[:, 2:3], ss[:, 1:2], [ss.b()], [ss.b()])
            x_n = xn[i % 2]
            P.act(x_n[:], xt[:], AF.Copy, [xt.b(), ss.b()], [x_n.b()], scale=ss[:, 2:3])
            pt = pts[i % 2]
            for c in range(8):
                P.tr(pt[:, c, :], x_n[:, c * 128:(c + 1) * 128], identb[:], [x_n.b()], [pt.b()])
            ht = hts[(i // 4) % 2]
            q = i % 4
            P.tt(ht[:, :, q * 128:(q + 1) * 128], pt[:], nw[:, :, None].to_broadcast([128, 8, 128]), ALU.mult,
                 [pt.b(), nw.b()], [ht.b()])
            if q == 3:
                t0 = (i // 4) * 512
                P.dma(hT_v[:, :, t0:t0 + 512], ht[:], [ht.b()], [hT_d.b()])
        P.end_phase()

        if debug == "A":
            break
        phase_s5(P, l, S, locals())
        if debug == "B":
            break
        phase_att(P, l, S, locals())
        if debug == "C":
            break
        phase_ssd(P, l, S, locals())
        if debug == "D":
            break
        phase_merge(P, l, S, locals(), xin, xout)

    P.barrier()
    P.emit()
    P.stack.close()
    return nc


def phase_merge(P, l, S, E, xin, xout):
    NB = S // 512
    identb = E["identb"]
    P.begin_phase()
    wg = P.sb("wg", [128, 8, 3072], BF16)
    wpa = P.sb("wpa", [128, 4, D], BF16)
    wpb = P.sb("wpb", [64, 4, D], BF16)
    wpc = P.sb("wpc", [128, 6, D], BF16)
    wo = P.sb("wo", [128, 8, D], BF16)
    P.begin_phase()
    stg = [P.sb("stg%d" % i, [128, 8, 512]) for i in range(2)]
    load_w(P, E["w_in_cols"](l, C_GATE, 3072), wg, 3072, 8, stg, "wg")
    load_w(P, E["pa_d"].t.ap()[l], wpa, D, 4, stg, "pa")
    vpb = E["pb_d"].t.ap()[l].rearrange("(k p) c -> p k c", p=64)
    for i, c0 in enumerate(range(0, D, 512)):
        st = stg[i % 2]
        P.dma(st[0:64, 0:4, :], vpb[:, :, c0:c0 + 512], [], [st.b()])
        P.cp(wpb[:, :, c0:c0 + 512], st[0:64, 0:4, :], [st.b()], [wpb.b()])
    load_w(P, E["pc_d"].t.ap()[l], wpc, D, 6, stg, "pc")
    load_w(P, E["wo_d"].t.ap()[l], wo, D, 8, stg, "wo")
    P.end_phase()
    hT_v = E["hT_v"]
    yTa_v = E["yTa_d"].t.ap().rearrange("c p t -> p c t")
    yTb_v = E["yTb_d"].t.ap().rearrange("c p t -> p c t")
    yTc_v = E["yTc_d"].t.ap().rearrange("c p t -> p c t")
    hts = [P.sb("mh%d" % i, [128, 8, 512], BF16) for i in range(2)]
    yas = [P.sb("mya%d" % i, [128, 4, 512], BF16) for i in range(2)]
    ybs = [P.sb("myb%d" % i, [64, 4, 512], BF16) for i in range(2)]
    ycs = [P.sb("myc%d" % i, [128, 6, 512], BF16) for i in range(2)]
    mg = [P.sb("mg%d" % i, [128, 8, 512], BF16) for i in range(2)]
    gt = [P.sb("gt%d" % i, [128, 512]) for i in range(2)]
    acc = [P.sb("macc%d" % i, [128, 512]) for i in range(2)]
    pg = [P.ps("pg%d" % i, [128, 512]) for i in range(2)]
    pp = [P.ps("pp%d" % i, [128, 512]) for i in range(2)]
    po = [P.ps("po%d" % i, [128, 512]) for i in range(2)]
    xr = [P.sb("xr%d" % i, [128, D]) for i in range(2)]
    xo = [P.sb("xo%d" % i, [128, D]) for i in range(2)]
    cnt = 0
    def mloads(b):
        t0 = b * 512
        ht, ya, yb, yc = hts[b % 2], yas[b % 2], ybs[b % 2], ycs[b % 2]
        P.dma(ht[:], hT_v[:, :, t0:t0 + 512], [E["hT_d"].b()], [ht.b()])
        P.dma(ya[:], yTa_v[:, :, t0:t0 + 512], [E["yTa_d"].b()], [ya.b()])
        P.dma(yb[:], yTb_v[:, :, t0:t0 + 512], [E["yTb_d"].b()], [yb.b()])
        P.dma(yc[:], yTc_v[:, :, t0:t0 + 512], [E["yTc_d"].b()], [yc.b()])

    def xload(xi):
        P.dma(xr[xi % 2][:], xin[xi * 128:(xi + 1) * 128, :], [xin.b()], [xr[xi % 2].b()])
    mloads(0)
    xload(0)
    for b in range(NB):
        t0 = b * 512
        ht, ya, yb, yc, m = hts[b % 2], yas[b % 2], ybs[b % 2], ycs[b % 2], mg[b % 2]
        if b + 1 < NB:
            mloads(b + 1)
        for dm in range(8):
            a = acc[dm % 2]
            for br in range(3):
                g_ps, p_ps, g_sb = pg[cnt % 2], pp[cnt % 2], gt[cnt % 2]
                cnt += 1
                col = br * D + dm * 128
                for k in range(8):
                    P.mm(g_ps[:], wg[:, k, col:col + 128], ht[:, k, :], [wg.b(), ht.b()], [g_ps.b()], start=(k == 0), stop=(k == 7))
                P.act(g_sb[:], g_ps[:], AF.Sigmoid, [g_ps.b()], [g_sb.b()])
                if br == 0:
                    for k in range(4):
                        P.mm(p_ps[:], wpa[:, k, dm * 128:(dm + 1) * 128], ya[:, k, :], [wpa.b(), ya.b()], [p_ps.b()], start=(k == 0), stop=(k == 3))
                elif br == 1:
                    for k in range(4):
                        P.mm(p_ps[:], wpb[:, k, dm * 128:(dm + 1) * 128], yb[:, k, :], [wpb.b(), yb.b()], [p_ps.b()], start=(k == 0), stop=(k == 3))
                else:
                    for k in range(6):
                        P.mm(p_ps[:], wpc[:, k, dm * 128:(dm + 1) * 128], yc[:, k, :], [wpc.b(), yc.b()], [p_ps.b()], start=(k == 0), stop=(k == 5))
                if br == 0:
                    P.tt(a[:], g_sb[:], p_ps[:], ALU.mult, [g_sb.b(), p_ps.b()], [a.b()])
                elif br == 1:
                    P.tt(g_sb[:], g_sb[:], p_ps[:], ALU.mult, [g_sb.b(), p_ps.b()], [g_sb.b()])
                    P.tt(a[:], a[:], g_sb[:], ALU.add, [a.b(), g_sb.b()], [a.b()], eng="pool")
                else:
                    P.tt(g_sb[:], g_sb[:], p_ps[:], ALU.mult, [g_sb.b(), p_ps.b()], [g_sb.b()])
                    P.tt(m[:, dm, :], a[:], g_sb[:], ALU.add, [a.b(), g_sb.b()], [m.b()], eng="pool")
        for tt_ in range(4):
            xi = b * 4 + tt_
            xr_, xo_ = xr[xi % 2], xo[xi % 2]
            if xi + 1 < NB * 4:
                xload(xi + 1)
            for hf in range(2):
                o_ps = po[hf]
                for k in range(8):
                    P.mm(o_ps[:], m[:, k, tt_ * 128:(tt_ + 1) * 128], wo[:, k, hf * 512:(hf + 1) * 512], [m.b(), wo.b()], [o_ps.b()], start=(k == 0), stop=(k == 7))
                P.tt(xo_[:, hf * 512:(hf + 1) * 512], xr_[:, hf * 512:(hf + 1) * 512], o_ps[:], ALU.add, [xr_.b(), o_ps.b()], [xo_.b()])
            P.dma(xout[xi * 128:(xi + 1) * 128, :], xo_[:], [xo_.b()], [xout.b()])
    P.end_phase()


def phase_ssd(P, l, S, E):
    NB = S // 512
    identb, identf, onesf = E["identb"], E["identf"], E["onesf"]
    P.begin_phase()
    wx = P.sb("wx", [128, 8, 1280], BF16)
    wdt = P.sb("wdt", [128, 8, 12], BF16)
    wzc = P.sb("wzc", [128, 8, 768], BF16)
    P.begin_phase()
    stg = [P.sb("stg%d" % i, [128, 8, 512]) for i in range(2)]
    load_w(P, E["w_in_cols"](l, C_XBC, 1280), wx, 1280, 8, stg, "wx")
    load_w(P, E["w_in_cols"](l, C_DT, 12), wdt, 12, 8, stg, "wdt")
    load_w(P, E["w_in_cols"](l, C_ZC, 768), wzc, 768, 8, stg, "wzc")
    P.end_phase()

    def ld(name, src, shape, dt=F32):
        t = P.sb(name, shape, dt)
        P.dma(t[:], src, [], [t.b()])
        return t
    convw = ld("convw", E["convw_d"][l], [128, 10, 4])
    convb = ld("convb", E["convb_d"][l], [128, 10])
    ssdv = ld("ssdv", E["ssdv_d"][l], [128, 3, 12])
    dcol = ld("dcol", E["ssdd_d"][l], [128, 6])
    nwcol = ld("nwcol", E["ssdnw_d"][l], [128, 6])
    utri = ld("utri", E["utri_d"][:], [128, 128])
    neg4 = ld("neg4", E["neg4_d"][:], [128, 4, 128])
    nega = P.sb("nega", [128, 12])
    P.act(nega[:], ssdv[:, 1, :], AF.Exp, [ssdv.b()], [nega.b()])
    P.ts(nega[:], nega[:], -1.0, None, ALU.mult, None, [nega.b()], [nega.b()])
    H = P.sb("H", [128, 12, 64])
    Hpad = P.sb("Hpad", [128, 12, 128], BF16)
    xdtpad = P.sb("xdtpad", [128, 12, 128], BF16)
    P.memset(H[:], 0.0, [H.b()])
    P.memset(Hpad[:], 0.0, [Hpad.b()])
    P.memset(xdtpad[:], 0.0, [xdtpad.b()])
    rawx = P.sb("rawx", [128, 10, 515])
    P.memset(rawx[:, :, 0:3], 0.0, [rawx.b()])
    cacc = P.sb("cacc", [128, 10, 512])
    xc = P.sb("xc", [128, 10, 512], BF16)
    zs = P.sb("zs", [128, 6, 512], BF16)
    ycb = [P.sb("ycb%d" % i, [128, 6, 512], BF16) for i in range(2)]
    hts = [P.sb("sh%d" % i, [128, 8, 512], BF16) for i in range(2)]
    dtb = P.sb("dtb", [128, 4, 12])
    nacs = P.sb("nacs", [128, 12])
    wdec = P.sb("wdec", [128, 12])
    uadt = P.sb("uadt", [128, 12, 128])
    xtok = P.sb("xtok", [128, 12, 64], BF16)
    btok = P.sb("btok", [128, 256], BF16)
    Eall = P.sb("Eall", [128, 12, 128])
    Lm = P.sb("Lm", [128, 12, 128], BF16)
    MT = P.sb("MT", [128, 12, 128], BF16)
    CTs = P.sb("CTs", [128, 12, 128], BF16)
    xdd = P.sb("xdd", [128, 12, 64], BF16)
    yf = P.sb("yf", [128, 6, 128])
    gg = P.sb("gg", [128, 6, 128])
    sq = P.sb("sq", [128, 6, 128])
    rstd = P.sb("rstd", [128, 128])
    htmp = P.sb("htmp", [128, 6, 64])
    px = P.ps("px", [128, 512])
    pmisc = P.ps("pmisc", [128, 512])
    ptr = P.ps("ptr", [128, 8, 128], BF16)
    prep = [P.ps("prep%d" % i, [128, 4, 128]) for i in range(2)]
    pyA = P.ps("pyA", [128, 4, 128])
    pyB = P.ps("pyB", [128, 4, 128])
    pS = P.ps("pS", [128, 512])
    pgt = pmisc[:, 0:256].rearrange("p (g l) -> p g l", g=2)
    hT_v = E["hT_v"]
    yTc_v = E["yTc_d"].t.ap().rearrange("c p t -> p c t")
    ev = 0
    P.dma(hts[0][:], hT_v[:, :, 0:512], [E["hT_d"].b()], [hts[0].b()])
    for b in range(NB):
        t0 = b * 512
        ht = hts[b % 2]
        if b + 1 < NB:
            P.dma(hts[(b + 1) % 2][:], hT_v[:, :, t0 + 512:t0 + 1024], [E["hT_d"].b()], [hts[(b + 1) % 2].b()])
        for tile in range(10):
            for k in range(8):
                P.mm(px[:], wx[:, k, tile * 128:(tile + 1) * 128], ht[:, k, :], [wx.b(), ht.b()], [px.b()], start=(k == 0), stop=(k == 7))
            P.cp(rawx[:, tile, 3:515], px[:], [px.b()], [rawx.b(tile)], eng="act")
        for tile in range(10):
            rb = rawx.b(tile)
            cb = cacc.b(tile)
            P.ts(cacc[:, tile, :], rawx[:, tile, 0:512], convw[:, tile, 0:1], None, ALU.mult, None, [rb, rawx.b(), convw.b()], [cb])
            for kk in range(1, 4):
                P.stt(cacc[:, tile, :], rawx[:, tile, kk:kk + 512], convw[:, tile, kk:kk + 1], cacc[:, tile, :], ALU.mult, ALU.add,
                      [rb, rawx.b(), cb], [cb])
            P.act(xc[:, tile, :], cacc[:, tile, :], AF.Silu, [cb, convb.b()], [xc.b(tile), xc.b()], bias=convb[:, tile:tile + 1])
            P.cp(rawx[:, tile, 0:3], rawx[:, tile, 512:515], [rb, rawx.b()], [rb, rawx.b()], eng="pool")
        for tile in range(6):
            for k in range(8):
                P.mm(px[:], wzc[:, k, tile * 128:(tile + 1) * 128], ht[:, k, :], [wzc.b(), ht.b()], [px.b()], start=(k == 0), stop=(k == 7))
            P.act(zs[:, tile, :], px[:], AF.Silu, [px.b()], [zs.b()])
        yc = ycb[b % 2]
        for cc in range(4):
            o = cc * 128
            pd = pmisc[:, 256:268]
            pcs = pmisc[:, 272:284]
            for k in range(8):
                P.mm(pd, ht[:, k, o:o + 128], wdt[:, k, :], [ht.b(), wdt.b()], [pmisc.b("pd")], start=(k == 0), stop=(k == 7))
            P.tt(dtb[:, 0, :], pd, ssdv[:, 0, :], ALU.add, [pmisc.b("pd"), ssdv.b()], [dtb.b()])
            P.act(dtb[:, 1, :], dtb[:, 0, :], AF.Exp, [dtb.b()], [dtb.b()])
            P.act(dtb[:, 2, :], dtb[:, 1, :], AF.Ln, [dtb.b()], [dtb.b()], bias=1.0)
            P.tt(dtb[:, 3, :], dtb[:, 2, :], nega[:], ALU.mult, [dtb.b(), nega.b()], [dtb.b()])
            P.mm(pcs, utri[:], dtb[:, 3, :], [utri.b(), dtb.b()], [pmisc.b("pcs")])
            P.ts(nacs[:], pcs, -1.0, None, ALU.mult, None, [pmisc.b("pcs")], [nacs.b()])
            P.tt(uadt[:], utri[:, None, :].to_broadcast([128, 12, 128]), dtb[:, 3, :, None].to_broadcast([128, 12, 128]), ALU.mult,
                 [utri.b(), dtb.b()], [uadt.b()], eng="pool")
            for tile in range(8):
                P.tr(ptr[:, tile, :], xc[:, tile, o:o + 128], identb[:], [xc.b(tile), xc.b()], [ptr.b()])
            P.cp(xtok[:].rearrange("p h d -> p (h d)"), ptr[:, 0:6, :].rearrange("p a b -> p (a b)"), [ptr.b()], [xtok.b()], eng="act")
            P.cp(btok[:], ptr[:, 6:8, :].rearrange("p a b -> p (a b)"), [ptr.b()], [btok.b()], eng="act")
            for g in range(2):
                P.mm(pgt[:, g, :], xc[:, 6 + g, o:o + 128], xc[:, 8 + g, o:o + 128], [xc.b()], [pmisc.b("pgt")])
            for q in range(3):
                pr = prep[q % 2]
                P.mm(pr[:].rearrange("p a b -> p (a b)"), onesf[:], uadt[:, 4 * q:4 * q + 4, :].rearrange("p a b -> p (a b)"),
                     [uadt.b()], [pr.b()])
                P.act(Eall[:, 4 * q:4 * q + 4, :], pr[:], AF.Exp, [pr.b()], [Eall.b()])
                P.mm(pr[:].rearrange("p a b -> p (a b)"), identf[:], neg4[:].rearrange("p a b -> p (a b)"),
                     [neg4.b(), Eall.b()], [pr.b()], start=False, stop=True)
                for hh in range(4):
                    h = 4 * q + hh
                    P.act(Lm[:, h, :], pr[:, hh, :], AF.Exp, [pr.b(), nacs.b()], [Lm.b()], bias=nacs[:, h:h + 1])
            for g in range(2):
                P.tt(MT[:, 6 * g:6 * g + 6, :], Lm[:, 6 * g:6 * g + 6, :], pgt[:, g:g + 1, :].to_broadcast([128, 6, 128]), ALU.mult,
                     [Lm.b(), pmisc.b("pgt")], [MT.b()])
                P.tt(CTs[:, 6 * g:6 * g + 6, :], Eall[:, 6 * g:6 * g + 6, :], xc[:, 8 + g:9 + g, o:o + 128].to_broadcast([128, 6, 128]), ALU.mult,
                     [Eall.b(), xc.b()], [CTs.b()], eng="pool")
            for r in range(2):
                P.tt(xdtpad[:, r::2, r * 64:r * 64 + 64], xtok[:, r::2, :], dtb[:, 2, r::2, None].to_broadcast([128, 6, 64]), ALU.mult,
                     [xtok.b(), dtb.b()], [xdtpad.b()])
            for pr6 in range(6):
                pyt = pyA[:, pr6, :] if pr6 < 4 else pyB[:, pr6 - 4, :]
                pyb = pyA.b() if pr6 < 4 else pyB.b("y")
                for r in range(2):
                    h = 2 * pr6 + r
                    P.mm(pyt, xdtpad[:, h, :], MT[:, h, :], [xdtpad.b(), MT.b()], [pyb], start=(r == 0), stop=False)
                    P.mm(pyt, Hpad[:, h, :], CTs[:, h, :], [Hpad.b(), CTs.b()], [pyb], start=False, stop=(r == 1))
                P.stt(yf[:, pr6, :], xc[:, pr6, o:o + 128], dcol[:, pr6:pr6 + 1], pyt, ALU.mult, ALU.add, [xc.b(), dcol.b(), pyb], [yf.b()])
            P.tt(gg[:], yf[:], zs[:, :, o:o + 128], ALU.mult, [yf.b(), zs.b()], [gg.b()])
            P.act(sq[:], gg[:], AF.Square, [gg.b()], [sq.b()])
            pss = pyB[:, 2, :]
            for tile in range(6):
                P.mm(pss, onesf[:], sq[:, tile, :], [sq.b()], [pyB.b("ss")], start=(tile == 0), stop=(tile == 5))
            P.act(rstd[:], pss, AF.Sqrt, [pyB.b("ss")], [rstd.b()], scale=1.0 / 768.0, bias=EPS)
            P.recip(rstd[:], rstd[:], [rstd.b()], [rstd.b()])
            for tile in range(6):
                P.stt(yc[:, tile, o:o + 128], gg[:, tile, :], nwcol[:, tile:tile + 1], rstd[:], ALU.mult, ALU.mult,
                      [gg.b(), nwcol.b(), rstd.b()], [yc.b()])
            P.tt(wdec[:], dtb[:, 2, :], Lm[:, :, 127], ALU.mult, [dtb.b(), Lm.b()], [wdec.b()])
            P.tt(xdd[:], xtok[:], wdec[:, :, None].to_broadcast([128, 12, 64]), ALU.mult, [xtok.b(), wdec.b()], [xdd.b()])
            for g in range(2):
                P.mm(pS[:, 0:384], btok[:, g * 128:(g + 1) * 128], xdd[:, 6 * g:6 * g + 6, :].rearrange("p a b -> p (a b)"),
                     [btok.b(), xdd.b()], [pS.b()])
                P.tt(htmp[:], H[:, 6 * g:6 * g + 6, :], Eall[:, 6 * g:6 * g + 6, 127:128].to_broadcast([128, 6, 64]), ALU.mult,
                     [H.b(), Eall.b()], [htmp.b()])
                P.tt(H[:, 6 * g:6 * g + 6, :], htmp[:], pS[:, 0:384].rearrange("p (a b) -> p a b", a=6), ALU.add,
                     [htmp.b(), pS.b()], [H.b()])
            for r in range(2):
                P.cp(Hpad[:, r::2, r * 64:r * 64 + 64], H[:, r::2, :], [H.b()], [Hpad.b()], eng="act")
        P.dma(yTc_v[:, :, t0:t0 + 512], yc[:], [yc.b()], [E["yTc_d"].b()])
    P.end_phase()


def phase_att(P, l, S, E):
    NSB = S // 2048
    onesb = E["onesb"]
    P.begin_phase()
    stg = [P.sb("stg%d" % i, [128, 8, 512]) for i in range(2)]
    qkw = P.sb("qkw", [128, 2])
    P.dma(qkw[:], E["qkw_d"][l], [], [qkw.b()])
    amask = P.sb("amask", [128, 2, 128], BF16)
    P.dma(amask[:], E["amask_d"][:], [], [amask.b()])
    bdones = P.sb("bdones", [128, 128], BF16)
    P.dma(bdones[:], E["bd_d"][:], [], [bdones.b()])
    wq = P.sb("wq", [128, 8, 256], BF16)
    wk = P.sb("wk", [128, 8, 256], BF16)
    wv = P.sb("wv", [128, 8, 256], BF16)
    hts = [P.sb("ah%d" % i, [128, 8, 2048], BF16) for i in range(1)]
    qT = [P.sb("qT%d" % h, [64, 2048], BF16) for h in range(4)]
    kT = [[P.sb("kT%d_%d" % (h, p), [64, 2048], BF16) for p in range(2)] for h in range(4)]
    V = [P.sb("V%d" % p, [128, 16, 4, 64], BF16) for p in range(2)]
    sqb = [P.sb("sqb%d" % i, [64, 512], BF16) for i in range(2)]
    rt = [P.sb("rt%d" % i, [64, 512]) for i in range(2)]
    pT = [P.sb("pT%d" % i, [128, 2, 128], BF16) for i in range(2)]
    ndS = [P.sb("ndS%d" % i, [64, 2, 2048]) for i in range(1)] * 2
    pq = [P.ps("pq%d" % i, [64, 512]) for i in range(2)]
    pn = P.ps("pn", [64, 512])
    pv = P.ps("pv", [128, 256])
    ps = [P.ps("ps%d" % i, [128, 2, 128]) for i in range(2)]
    pnd = [P.ps("pnd%d" % i, [64, 2, 128]) for i in range(2)]
    hT_v = E["hT_v"]
    num_d, den_d = E["num_d"], E["den_d"]
    bi = 0
    hi = 0
    qi = 0
    def wloads(g):
        load_w(P, E["w_in_cols"](l, C_Q + g * 256, 256), wq, 256, 8, stg, "wq")
        load_w(P, E["w_in_cols"](l, C_K + g * 256, 256), wk, 256, 8, stg, "wk")
        load_w(P, E["w_in_cols"](l, C_V + g * 256, 256), wv, 256, 8, stg, "wv")

    def htload(sb_):
        P.dma(hts[0][:], hT_v[:, :, sb_ * 2048:(sb_ + 1) * 2048], [E["hT_d"].b()], [hts[0].b()])
    wloads(0)
    htload(0)
    for g in range(3):
        d = DIL[g]
        nun = 2048 // (128 * d)
        for sb_ in range(NSB):
            par = sb_ % 2
            T0 = sb_ * 2048
            ht = hts[0]
            for pp in range(4):
                for (W, dst, wc) in ((wq, qT[pp], 0), (wk, kT[pp][par], 1)):
                    for tb in range(4):
                        p_q, s_q, r_t = pq[qi % 2], sqb[qi % 2], rt[qi % 2]
                        qi += 1
                        for k in range(8):
                            P.mm(p_q[:], W[:, k, pp * 64:(pp + 1) * 64], ht[:, k, tb * 512:(tb + 1) * 512], [W.b(), ht.b()], [p_q.b()],
                                 start=(k == 0), stop=(k == 7))
                        P.act(s_q[:], p_q[:], AF.Square, [p_q.b()], [s_q.b()])
                        P.mm(pn[:], onesb[0:64, 0:64], s_q[:], [s_q.b()], [pn.b()])
                        P.act(r_t[:], pn[:], AF.Ln, [pn.b()], [r_t.b()], scale=1.0 / 64.0, bias=EPS)
                        P.act(r_t[:], r_t[:], AF.Exp, [r_t.b()], [r_t.b()], scale=-0.5)
                        P.stt(dst[:, tb * 512:(tb + 1) * 512], p_q[:], qkw[0:64, wc:wc + 1], r_t[:], ALU.mult, ALU.mult,
                              [p_q.b(), qkw.b(), r_t.b()], [dst.b()])
            Vc = V[par]
            for blk in range(16):
                m, r = blk // d, blk % d
                off = m * 128 * d + r
                for k in range(8):
                    P.mm(pv[:], ht[:, k, off:off + 127 * d + 1:d], wv[:, k, :], [ht.b(), wv.b()], [pv.b()], start=(k == 0), stop=(k == 7))
                P.cp(Vc[:, blk, :, :].rearrange("p a b -> p (a b)"), pv[:], [pv.b()], [Vc.b()], eng="act")
            if sb_ + 1 < NSB:
                htload(sb_ + 1)
            elif g + 1 < 3:
                wloads(g + 1)
                htload(0)
            for hh in range(4):
                pp, ro = hh, 0
                qTh = qT[pp]
                kTc = kT[pp][par]
                nd = ndS[hi % 2]
                hi += 1
                def blkinfo(blk):
                    m, r = blk // d, blk % d
                    off = m * 128 * d + r
                    sl = slice(off, off + 127 * d + 1, d)
                    has_prev = (m > 0) or (sb_ > 0)
                    kprev = vprev = pb = None
                    if m > 0:
                        kprev = kTc[ro:ro + 64, off - 128 * d:off - d + 1:d]
                        vprev = Vc[:, blk - d, hh, :]
                        pb = [kTc.b(), Vc.b()]
                    elif sb_ > 0:
                        offp = (nun - 1) * 128 * d + r
                        kprev = kT[pp][1 - par][ro:ro + 64, offp:offp + 127 * d + 1:d]
                        vprev = V[1 - par][:, (nun - 1) * d + r, hh, :]
                        pb = [kT[pp][1 - par].b(), V[1 - par].b()]
                    return sl, has_prev, kprev, vprev, pb

                def qk(blk, bidx):
                    sl, has_prev, kprev, vprev, pb = blkinfo(blk)
                    p_s = ps[bidx % 2]
                    P.mm(p_s[:, 1, :], kTc[ro:ro + 64, sl], qTh[ro:ro + 64, sl], [kTc.b(), qTh.b()], [p_s.b()])
                    if has_prev:
                        P.mm(p_s[:, 0, :], kprev, qTh[ro:ro + 64, sl], [pb[0], qTh.b()], [p_s.b()])

                def rest(blk, bidx):
                    sl, has_prev, kprev, vprev, pb = blkinfo(blk)
                    p_s, p_T, p_nd = ps[bidx % 2], pT[bidx % 2], pnd[bidx % 2]
                    lo = 0 if has_prev else 1
                    P.act(p_T[:, lo:2, :], p_s[:, lo:2, :], AF.Exp, [p_s.b()], [p_T.b()], scale=0.125)
                    P.tt(p_T[:, lo:2, :], p_T[:, lo:2, :], amask[:, lo:2, :], ALU.mult, [p_T.b(), amask.b()], [p_T.b()], eng="pool")
                    P.mm(p_nd[:, 0, :], Vc[:, blk, hh, :], p_T[:, 1, :], [Vc.b(), p_T.b()], [p_nd.b()], start=True, stop=not has_prev)
                    if has_prev:
                        P.mm(p_nd[:, 0, :], vprev, p_T[:, 0, :], [pb[1], p_T.b()], [p_nd.b()], start=False, stop=True)
                    P.mm(p_nd[:, 1, :], onesb[:, 0:64], p_T[:, 1, :], [p_T.b()], [p_nd.b()], start=True, stop=not has_prev)
                    if has_prev:
                        P.mm(p_nd[:, 1, :], onesb[:, 0:64], p_T[:, 0, :], [p_T.b()], [p_nd.b()], start=False, stop=True)
                    P.cp(nd[:, :, sl], p_nd[:], [p_nd.b()], [nd.b()])
                qk(0, bi)
                for blk in range(16):
                    if blk + 1 < 16:
                        qk(blk + 1, bi + 1)
                    rest(blk, bi)
                    bi += 1
                P.dma(num_d.t.ap()[g, hh, :, T0:T0 + 2048], nd[:, 0, :], [nd.b()], [num_d.b()])
                P.dma(den_d.t.ap()[g, hh, :, T0:T0 + 2048], nd[:, 1, :], [nd.b()], [den_d.b()])
    P.end_phase()
    P.begin_phase()
    stg = [P.sb("stg%d" % i, [128, 8, 512]) for i in range(2)]
    wzb = P.sb("wzb", [128, 8, 256], BF16)
    load_w(P, E["w_in_cols"](l, C_ZB, 256), wzb, 256, 8, stg, "wzb")
    hts = [P.sb("ch%d" % i, [128, 8, 512], BF16) for i in range(2)]
    nn = [P.sb("nn%d" % i, [64, 3, 512]) for i in range(2)]
    dd = [P.sb("dd%d" % i, [64, 3, 512]) for i in range(2)]
    zb = P.sb("zb", [64, 512])
    yb = [P.sb("yb%d" % i, [64, 512], BF16) for i in range(2)]
    pz = [P.ps("pz%d" % i, [64, 512]) for i in range(2)]
    it = 0
    NBC = S // 512

    def cloads(itx):
        b, hh = itx // 4, itx % 4
        t0 = b * 512
        if hh == 0:
            ht = hts[b % 2]
            P.dma(ht[:], hT_v[:, :, t0:t0 + 512], [E["hT_d"].b()], [ht.b()])
        n_, d_ = nn[itx % 2], dd[itx % 2]
        P.dma(n_[:], num_d.t.ap()[:, hh, :, t0:t0 + 512].rearrange("g p t -> p g t"), [num_d.b()], [n_.b()])
        P.dma(d_[:], den_d.t.ap()[:, hh, :, t0:t0 + 512].rearrange("g p t -> p g t"), [den_d.b()], [d_.b()])
    cloads(0)
    for b in range(NBC):
        t0 = b * 512
        ht = hts[b % 2]
        for hh in range(4):
            n_, d_, y_, p_ = nn[it % 2], dd[it % 2], yb[it % 2], pz[it % 2]
            if it + 1 < NBC * 4:
                cloads(it + 1)
            it += 1
            for k in range(8):
                P.mm(p_[:], wzb[:, k, hh * 64:(hh + 1) * 64], ht[:, k, :], [wzb.b(), ht.b()], [p_.b()], start=(k == 0), stop=(k == 7))
            P.act(zb[:], p_[:], AF.Silu, [p_.b()], [zb.b()])
            P.tt(n_[:, 0, :], n_[:, 0, :], n_[:, 1, :], ALU.add, [n_.b()], [n_.b()])
            P.tt(n_[:, 0, :], n_[:, 0, :], n_[:, 2, :], ALU.add, [n_.b()], [n_.b()])
            P.tt(d_[:, 0, :], d_[:, 0, :], d_[:, 1, :], ALU.add, [d_.b()], [d_.b()], eng="pool")
            P.tt(d_[:, 0, :], d_[:, 0, :], d_[:, 2, :], ALU.add, [d_.b()], [d_.b()], eng="pool")
            P.act(d_[:, 1, :], d_[:, 0, :], AF.Ln, [d_.b()], [d_.b()])
            P.act(d_[:, 1, :], d_[:, 1, :], AF.Exp, [d_.b()], [d_.b()], scale=-1.0)
            P.tt(n_[:, 1, :], n_[:, 0, :], d_[:, 1, :], ALU.mult, [n_.b(), d_.b()], [n_.b()])
            P.tt(y_[:], n_[:, 1, :], zb[:], ALU.mult, [n_.b(), zb.b()], [y_.b()])
            P.dma(E["yTb_d"].t.ap()[hh, :, t0:t0 + 512], y_[:], [y_.b()], [E["yTb_d"].b()])
    P.end_phase()


def phase_s5(P, l, S, E):
    SEG = 512
    NSEG = S // SEG
    NCH = SEG // 8
    identf, onesb = E["identf"], E["onesb"]
    P.begin_phase()
    D0b = P.sb("D0b", [128, 32, 128], BF16)
    Bst = P.sb("Bst", [128, 32, 2, 64], BF16)
    Yb = P.sb("Yb", [64, 32, 2, 128], BF16)
    L8 = P.sb("L8", [64, 2, 32])
    P.begin_phase()
    tab = Buf("tab")
    R, W = [tab], [tab]
    par = P.sb("s5par", [64, 3, 32])
    bb = P.sb("s5bb", [64, 2, 32, 16])
    cc = P.sb("s5cc", [64, 2, 32, 16])
    bmask = P.sb("bmask", [128, 128])
    d0t = P.sb("d0t", [128, 128])
    dS5 = P.sb("dS5", [128, 32])
    P.dma(dS5[:], E["s5d_d"][l], [], W)
    P.dma(par[:], E["s5par_d"][l], [], W)
    P.dma(bb[:], E["s5b_d"][l], [], W)
    P.dma(cc[:], E["s5c_d"][l], [], W)
    P.dma(bmask[:], E["bmask_d"][:], [], W)
    tm = P.sb("s5tm", [64, 12, 32])
    LP = P.sb("LP", [64, 9, 2, 32])
    LN = P.sb("LN", [64, 9, 2, 32])
    BB = P.sb("BBar", [64, 2, 32, 16])
    XX = P.sb("XX", [64, 2, 32, 8, 16])
    YY = P.sb("YY", [64, 2, 32, 8, 16])
    X7 = P.sb("X7", [64, 2, 32, 8, 16])
    t16 = P.sb("t16", [64, 2, 32, 16])
    pD = P.ps("pD", [128, 128])
    pTr = P.ps("pTr", [128, 2, 64])
    are, aim, lst = par[:, 0, :], par[:, 1, :], par[:, 2, :]
    T_ = lambda i: tm[:, i, :]
    mul = lambda o, a, b: P.tt(o, a, b, ALU.mult, R, W)
    add = lambda o, a, b: P.tt(o, a, b, ALU.add, R, W)
    sub = lambda o, a, b: P.tt(o, a, b, ALU.subtract, R, W)
    P.act(T_(0), lst, AF.Exp, R, W)
    mul(T_(1), are, T_(0))
    mul(T_(2), aim, T_(0))
    P.act(T_(3), T_(1), AF.Exp, R, W, scale=1.0 / 16)
    hp = P.sb("halfpi", [64, 1])
    P.memset(hp[:], math.pi / 2, W)
    P.act(T_(4), T_(2), AF.Sin, R, W, scale=1.0 / 16, bias=hp[:, 0:1])
    P.act(T_(5), T_(2), AF.Sin, R, W, scale=1.0 / 16)
    lr, li = LP[:, 1, 0, :], LP[:, 1, 1, :]
    mul(lr, T_(3), T_(4))
    mul(li, T_(3), T_(5))
    for _ in range(4):
        mul(T_(6), lr, lr)
        mul(T_(7), li, li)
        mul(T_(8), lr, li)
        sub(lr, T_(6), T_(7))
        P.ts(li, T_(8), 2.0, None, ALU.mult, None, R, W)
    P.ts(T_(0), lr, -1.0, None, ALU.add, None, R, W)
    mul(T_(1), are, are)
    mul(T_(2), aim, aim)
    add(T_(1), T_(1), T_(2))
    P.recip(T_(1), T_(1), R, W)
    mul(T_(2), T_(0), are)
    mul(T_(3), li, aim)
    add(T_(2), T_(2), T_(3))
    mul(T_(9), T_(2), T_(1))
    mul(T_(2), li, are)
    mul(T_(3), T_(0), aim)
    sub(T_(2), T_(2), T_(3))
    mul(T_(10), T_(2), T_(1))
    mul(T_(0), lr, lr)
    mul(T_(1), li, li)
    add(T_(0), T_(0), T_(1))
    P.recip(T_(0), T_(0), R, W)
    mul(LN[:, 1, 0, :], lr, T_(0))
    mul(T_(1), li, T_(0))
    P.ts(LN[:, 1, 1, :], T_(1), -1.0, None, ALU.mult, None, R, W)
    P.memset(LP[:, 0, 0, :], 1.0, W)
    P.memset(LP[:, 0, 1, :], 0.0, W)

    def cmul(o_re, o_im, a_re, a_im, b_re, b_im, t1, t2):
        mul(t1, a_re, b_re)
        mul(t2, a_im, b_im)
        sub(o_re, t1, t2)
        mul(t1, a_re, b_im)
        mul(t2, a_im, b_re)
        add(o_im, t1, t2)
    for k in range(1, 8):
        cmul(LP[:, k + 1, 0, :], LP[:, k + 1, 1, :], LP[:, k, 0, :], LP[:, k, 1, :], LP[:, 1, 0, :], LP[:, 1, 1, :], T_(6), T_(7))
        cmul(LN[:, k + 1, 0, :], LN[:, k + 1, 1, :], LN[:, k, 0, :], LN[:, k, 1, :], LN[:, 1, 0, :], LN[:, 1, 1, :], T_(6), T_(7))
    bc = lambda ap: ap[:, :, None].to_broadcast([64, 32, 16])
    t1, t2 = t16[:, 0, :, :], t16[:, 1, :, :]
    cmul(BB[:, 0, :, :], BB[:, 1, :, :], bc(T_(9)), bc(T_(10)), bb[:, 0, :, :], bb[:, 1, :, :], t1, t2)
    for j in range(8):
        cmul(XX[:, 0, :, j, :], XX[:, 1, :, j, :], bc(LN[:, j + 1, 0, :]), bc(LN[:, j + 1, 1, :]), BB[:, 0, :, :], BB[:, 1, :, :], t1, t2)
        cmul(YY[:, 0, :, j, :], YY[:, 1, :, j, :], bc(LP[:, j + 1, 0, :]), bc(LP[:, j + 1, 1, :]), cc[:, 0, :, :], cc[:, 1, :, :], t1, t2)
        cmul(X7[:, 0, :, j, :], X7[:, 1, :, j, :], bc(LP[:, 7 - j, 0, :]), bc(LP[:, 7 - j, 1, :]), BB[:, 0, :, :], BB[:, 1, :, :], t1, t2)
    P.ts(YY[:, 1, :, :, :].rearrange("p g j o -> p (g j o)"), YY[:, 1, :, :, :].rearrange("p g j o -> p (g j o)"), -1.0, None, ALU.mult, None, R, W)
    P.cp(Yb[:, :, 0, :], YY[:, 0, :, :, :].rearrange("p g j o -> p g (j o)"), R, W)
    P.cp(Yb[:, :, 1, :], YY[:, 1, :, :, :].rearrange("p g j o -> p g (j o)"), R, W)
    P.cp(L8[:], LP[:, 8, :, :], R, W)
    for g in range(32):
        P.mm(pD[:], XX[:, 0, g, :, :].rearrange("p j i -> p (j i)"), YY[:, 0, g, :, :].rearrange("p j o -> p (j o)"), R, W, start=True, stop=False)
        P.mm(pD[:], XX[:, 1, g, :, :].rearrange("p j i -> p (j i)"), YY[:, 1, g, :, :].rearrange("p j o -> p (j o)"), R, W, start=False, stop=True)
        P.tt(d0t[:], pD[:], bmask[:], ALU.mult, R, W)
        P.stt(D0b[:, g, :], identf[:], dS5[:, g:g + 1], d0t[:], ALU.mult, ALU.add, R, W)
        for c in range(2):
            P.tr(pTr[:, c, :], X7[:, c, g, :, :].rearrange("p j i -> p (j i)"), identf[0:64, 0:64], R, W)
        P.cp(Bst[:, g, :, :], pTr[:], R, W)
    P.end_phase()
    wu = P.sb("wu", [128, 8, 512], BF16)
    wza = P.sb("wza", [128, 8, 512], BF16)
    glw = P.sb("glw", [128, 4, 512], BF16)
    P.begin_phase()
    stg = [P.sb("stg%d" % i, [128, 8, 512]) for i in range(2)]
    load_w(P, E["w_in_cols"](l, C_UA, 512), wu, 512, 8, stg, "wu")
    load_w(P, E["w_in_cols"](l, C_ZA, 512), wza, 512, 8, stg, "wza")
    load_w(P, E["gluw_d"].t.ap()[l], glw, 512, 4, stg, "glw")
    P.end_phase()

    def ld(name, src, shape, dt=F32):
        t = P.sb(name, shape, dt)
        P.dma(t[:], src, [], [t.b()])
        return t
    glub = ld("glub", E["glub_d"][l], [128, 4])
    selT = ld("selT", E["selT_d"][:], [128, 8, 8, 128], BF16)
    sel = ld("sel", E["sel_d"][:], [128, 8, 8, 128], BF16)
    A2 = P.sb("A2", [64, 2, 32])
    B2 = P.sb("B2", [64, 2, 32])
    P.cp(A2[:, 0, :], L8[:, 0, :], [tab], [A2.b()])
    P.cp(A2[:, 1, :], L8[:, 0, :], [tab], [A2.b()])
    P.cp(B2[:, 1, :], L8[:, 1, :], [tab], [B2.b()])
    P.ts(B2[:, 0, :], L8[:, 1, :], -1.0, None, ALU.mult, None, [tab], [B2.b()])
    hts = [P.sb("s5ht%d" % i, [128, 8, SEG], BF16) for i in range(2)]
    uT = P.sb("uT", [128, 4, SEG], BF16)
    Ugs = [P.sb("Ug%d" % i, [128, 32, NCH], BF16) for i in range(2)]
    HHs = [P.sb("HH%d" % i, [64, 2, 32, NCH]) for i in range(2)]
    Hbs = [P.sb("Hb%d" % i, [64, 2, 32, NCH], BF16) for i in range(1)] * 2
    carry = P.sb("carry", [64, 2, 32])
    P.memset(carry[:], 0.0, [carry.b()])
    sct = P.sb("sct", [64, 2, 2, 32])
    glg = P.sb("glg", [128, 32, NCH], BF16)
    gT = P.sb("gT", [128, 4, SEG], BF16)
    y1 = P.sb("y1", [128, 8, NCH])
    y2 = P.sb("y2", [128, 8, NCH])
    sgs = P.sb("sgs", [128, 512])
    zsa = P.sb("zsa", [128, 512])
    yaS = [P.sb("yaS%d" % i, [128, 4, 512], BF16) for i in range(1)] * 2
    pA = [P.ps("pA%d" % i, [128, 512]) for i in range(2)]
    pU = [P.ps("pU%d" % i, [128, 8, NCH]) for i in range(2)]
    pY = [P.ps("pY%d" % i, [128, NCH]) for i in range(2)]
    yp = [P.sb("yp%d" % i, [128, NCH]) for i in range(6)]
    yq = [P.sb("yq%d" % i, [128, NCH]) for i in range(6)]
    pSr = P.ps("pSr", [64, 8, NCH])
    pSi = P.ps("pSi", [64, 8, NCH])
    hT_v = E["hT_v"]
    yTa_v = E["yTa_d"].t.ap().rearrange("c p t -> p c t")
    import os
    SCAN_ENG = os.environ.get("SCAN_ENG", "pool")
    lvl = int(os.environ.get("S5DBG", "9"))
    cnt = {"a": 0, "u": 0}

    def pass1(sg):
        T0 = sg * SEG
        ht, Ug, HH = hts[sg % 2], Ugs[sg % 2], HHs[sg % 2]
        P.dma(ht[:], hT_v[:, :, T0:T0 + SEG], [E["hT_d"].b()], [ht.b()])
        for ct in range(4):
            for tb in range(SEG // 512):
                p_ = pA[cnt["a"] % 2]
                cnt["a"] += 1
                for k in range(8):
                    P.mm(p_[:], wu[:, k, ct * 128:(ct + 1) * 128], ht[:, k, tb * 512:(tb + 1) * 512], [wu.b(), ht.b()], [p_.b()], start=(k == 0), stop=(k == 7))
                P.cp(uT[:, ct, tb * 512:(tb + 1) * 512], p_[:], [p_.b()], [uT.b()], eng=("act" if cnt["a"] % 2 else "dve"))
        for bq in range(4):
            p_u = pU[cnt["u"] % 2]
            cnt["u"] += 1
            for gl in range(8):
                for j in range(8):
                    P.mm(p_u[:, gl, :], selT[:, gl, j, :], uT[:, bq, j:SEG:8], [selT.b(), uT.b()], [p_u.b()], start=(j == 0), stop=(j == 7))
            P.cp(Ug[:, 8 * bq:8 * bq + 8, :], p_u[:], [p_u.b()], [Ug.b(bq)], eng="act")
            for gl in range(8):
                g = 8 * bq + gl
                P.mm(pSr[:, gl, :], Bst[:, g, 0, :], Ug[:, g, :], [tab, Ug.b(bq)], [pSr.b()])
                P.mm(pSi[:, gl, :], Bst[:, g, 1, :], Ug[:, g, :], [tab, Ug.b(bq)], [pSi.b()])
            P.cp(HH[:, 0, 8 * bq:8 * bq + 8, :], pSr[:], [pSr.b()], [HH.b()])
            P.cp(HH[:, 1, 8 * bq:8 * bq + 8, :], pSi[:], [pSi.b()], [HH.b()], eng="act")

    def scan(sg):
        HH, Hb = HHs[sg % 2], Hbs[sg % 2]
        hb = [HH.b()]
        for c in range(NCH):
            if c == 0:
                hall, hsw = carry[:], carry[:, ::-1, :]
                rb = [carry.b(), HH.b()]
            else:
                hall, hsw = HH[:, :, :, c - 1], HH[:, ::-1, :, c - 1]
                rb = [HH.b()]
            sb_ = [sct.b()]
            P.tt(sct[:, 0, :, :], A2[:], hall, ALU.mult, rb + [A2.b()], sb_, eng=SCAN_ENG)
            P.tt(sct[:, 1, :, :], B2[:], hsw, ALU.mult, rb + [B2.b()], sb_, eng=SCAN_ENG)
            P.tt(HH[:, :, :, c], HH[:, :, :, c], sct[:, 0, :, :], ALU.add, sb_ + hb, hb, eng=SCAN_ENG)
            P.tt(HH[:, :, :, c], HH[:, :, :, c], sct[:, 1, :, :], ALU.add, sb_ + hb, hb, eng=SCAN_ENG)
        P.cp(Hb[:, :, :, 0], carry[:], [carry.b()], [Hb.b()], eng=SCAN_ENG)
        P.cp(Hb[:, :, :, 1:NCH], HH[:, :, :, 0:NCH - 1], [HH.b()], [Hb.b()], eng=SCAN_ENG)
        P.cp(carry[:], HH[:, :, :, NCH - 1], [HH.b(), Hb.b()], [carry.b()], eng=SCAN_ENG)

    def pass2(sg):
        T0 = sg * SEG
        ht, Ug, Hb = hts[sg % 2], Ugs[sg % 2], Hbs[sg % 2]
        NBUF = 6

        def st1(g):
            bq = g // 8
            p_y = pY[g % 2]
            ya1 = yp[g % NBUF]
            P.mm(p_y[:], D0b[:, g, :], Ug[:, g, :], [tab, Ug.b(bq)], [p_y.b()], start=True, stop=False)
            P.mm(p_y[:], Yb[:, g, 0, :], Hb[:, 0, g, :], [Hb.b()], [p_y.b()], start=False, stop=False)
            P.mm(p_y[:], Yb[:, g, 1, :], Hb[:, 1, g, :], [Hb.b()], [p_y.b()], start=False, stop=True)
            P.cp(ya1[:], p_y[:], [p_y.b()], [ya1.b()])

        def st2(g):
            ya1, ya2 = yp[g % NBUF], yq[g % NBUF]
            P.act(ya2[:], ya1[:], AF.Square, [ya1.b()], [ya2.b()], scale=0.21145921592590868)

        def st3(g):
            ya1, ya2 = yp[g % NBUF], yq[g % NBUF]
            P.stt(ya2[:], ya2[:], 1.0, ya1[:], ALU.add, ALU.mult, [ya2.b(), ya1.b()], [ya2.b()])

        def st4(g):
            ya2 = yq[g % NBUF]
            P.act(ya2[:], ya2[:], AF.Sigmoid, [ya2.b()], [ya2.b()], scale=1.5957691216)

        def st5(g):
            ya1, ya2 = yp[g % NBUF], yq[g % NBUF]
            P.tt(glg[:, g, :], ya2[:], ya1[:], ALU.mult, [ya2.b(), ya1.b()], [glg.b(g // 8)])
        stages = [st1, st2, st3, st4, st5]
        for it in range(32 + len(stages) - 1):
            for si, st in enumerate(stages):
                g = it - si
                if 0 <= g < 32:
                    st(g)
        if lvl < 4:
            return
        for ct in range(4):
            p_f = pU[cnt["u"] % 2]
            cnt["u"] += 1
            for jp in range(8):
                for gl in range(8):
                    P.mm(p_f[:, jp, :], sel[:, gl, jp, :], glg[:, ct * 8 + gl, :], [sel.b(), glg.b(ct)], [p_f.b()], start=(gl == 0), stop=(gl == 7))
            P.cp(gT[:, ct, :].rearrange("p (c j) -> p j c", j=8), p_f[:], [p_f.b()], [gT.b()], eng=("act" if ct % 2 else "dve"))
        if lvl < 5:
            return
        for tb in range(SEG // 512):
            ya = yaS[sg % 2]
            ts_ = slice(tb * 512, (tb + 1) * 512)
            for co in range(4):
                p_ = pA[cnt["a"] % 2]
                pZ = pA[(cnt["a"] + 1) % 2]
                cnt["a"] += 1
                for ct in range(4):
                    P.mm(p_[:], glw[:, ct, co * 128:(co + 1) * 128], gT[:, ct, ts_], [glw.b(), gT.b()], [p_.b()], start=(ct == 0), stop=(ct == 3))
                P.act(sgs[:], p_[:], AF.Sigmoid, [p_.b(), glub.b()], [sgs.b()], bias=glub[:, co:co + 1])
                for k in range(8):
                    P.mm(pZ[:], wza[:, k, co * 128:(co + 1) * 128], ht[:, k, ts_], [wza.b(), ht.b()], [pZ.b()], start=(k == 0), stop=(k == 7))
                P.act(zsa[:], pZ[:], AF.Silu, [pZ.b()], [zsa.b()])
                P.tt(sgs[:], sgs[:], gT[:, co, ts_], ALU.mult, [sgs.b(), gT.b()], [sgs.b()])
                P.tt(ya[:, co, :], sgs[:], zsa[:], ALU.mult, [sgs.b(), zsa.b()], [ya.b()])
            P.dma(yTa_v[:, :, T0 + tb * 512:T0 + (tb + 1) * 512], ya[:], [ya.b()], [E["yTa_d"].b()])

    if lvl >= 1:
        pass1(0)
    for sg in range(NSEG if lvl >= 9 else 1):
        if lvl >= 2:
            scan(sg)
        if sg + 1 < NSEG and lvl >= 9:
            pass1(sg + 1)
        if lvl >= 3:
            pass2(sg)
    P.end_phase()


def host_consts():
    bf = ml_dtypes.bfloat16
    c = {}
    c["c_identb"] = np.eye(128).astype(bf)
    c["c_identf"] = np.eye(128, dtype=np.float32)
    c["c_onesb"] = np.ones((128, 128)).astype(bf)
    c["c_onesf"] = np.ones((128, 128), np.float32)
    k = np.arange(128)
    bd = np.zeros((128, 128), np.float32)
    bd[:64, :64] = 1.0
    bd[64:, 64:] = 1.0
    c["c_bdones"] = bd.astype(bf)
    c["c_utri"] = (k[:, None] <= k[None, :]).astype(np.float32)
    neg = np.where(k[:, None] > k[None, :], NEGV, 0.0).astype(np.float32)
    c["c_neg4"] = np.ascontiguousarray(np.broadcast_to(neg[:, None, :], (128, 4, 128)))
    am = np.zeros((128, 2, 128), np.float32)
    am[:, 0, :] = (k[:, None] >= k[None, :])
    am[:, 1, :] = (k[None, :] >= k[:, None])
    c["c_amask"] = am.astype(bf)
    c["c_bmask"] = ((k[None, :] // 16) >= (k[:, None] // 16)).astype(np.float32)
    selT = np.zeros((128, 8, 8, 128), np.float32)
    sel = np.zeros((128, 8, 8, 128), np.float32)
    for gl in range(8):
        for j in range(8):
            for i in range(16):
                selT[16 * gl + i, gl, j, 16 * j + i] = 1.0
                sel[16 * j + i, gl, j, 16 * gl + i] = 1.0
    c["c_selT"] = selT.astype(bf)
    c["c_sel"] = sel.astype(bf)
    return c


def host_params(inp):
    f = lambda a: np.ascontiguousarray(np.asarray(a, dtype=np.float32))
    L = inp["norm_w"].shape[0]
    o = {}
    o["w_in"] = f(inp["w_in"])
    o["nw"] = f(np.asarray(inp["norm_w"]).reshape(L, 8, 128).transpose(0, 2, 1))
    are = np.asarray(inp["s5_a_re"]).transpose(0, 2, 1)
    aim = np.asarray(inp["s5_a_im"]).transpose(0, 2, 1)
    lst = np.broadcast_to(np.asarray(inp["s5_log_step"])[:, None, :], (L, 64, 32))
    o["s5par"] = f(np.stack([are, aim, lst], axis=2))
    bre = np.asarray(inp["s5_b_re"]).transpose(0, 2, 1, 3)
    bim = np.asarray(inp["s5_b_im"]).transpose(0, 2, 1, 3)
    o["s5b"] = f(np.stack([bre, bim], axis=2))
    cre = np.asarray(inp["s5_c_re"]).transpose(0, 3, 1, 2)
    cim = np.asarray(inp["s5_c_im"]).transpose(0, 3, 1, 2)
    o["s5c"] = f(np.stack([cre, cim], axis=2))
    d = np.asarray(inp["s5_d"]).reshape(L, 32, 16)
    o["s5d"] = f(np.broadcast_to(d.transpose(0, 2, 1)[:, None, :, :], (L, 8, 16, 32)).reshape(L, 128, 32))
    o["gluw"] = f(inp["s5_glu_w"])
    o["glub"] = f(np.asarray(inp["s5_glu_b"]).reshape(L, 4, 128).transpose(0, 2, 1))
    qk = np.stack([np.asarray(inp["q_norm_w"]), np.asarray(inp["k_norm_w"])], axis=2)
    o["qkw"] = f(np.concatenate([qk, qk], axis=1))
    cw = np.asarray(inp["conv_w"]).reshape(L, 4, 10, 128)
    o["convw"] = f(cw.transpose(0, 3, 2, 1))
    o["convb"] = f(np.asarray(inp["conv_b"]).reshape(L, 10, 128).transpose(0, 2, 1))
    sv = np.stack([np.asarray(inp["dt_bias"]), np.asarray(inp["ssd_a_log"]), np.asarray(inp["ssd_d"])], axis=1)
    o["ssdv"] = f(np.broadcast_to(sv[:, None, :, :], (L, 128, 3, 12)))
    dd = np.repeat(np.asarray(inp["ssd_d"]), 64, axis=1)
    o["ssdd"] = f(dd.reshape(L, 6, 128).transpose(0, 2, 1))
    o["ssdnw"] = f(np.asarray(inp["ssd_norm_w"]).reshape(L, 6, 128).transpose(0, 2, 1))
    o["proj_a"] = f(inp["proj_a"])
    o["proj_b"] = f(inp["proj_b"])
    o["proj_c"] = f(inp["proj_c"])
    o["w_out"] = f(inp["w_out"])
    return o


_NC_CACHE = {}


def kernel(**inputs):
    x = np.asarray(inputs["x"], dtype=np.float32)
    B, S, _ = x.shape
    L = np.asarray(inputs["norm_w"]).shape[0]
    key = (S, L)
    if key not in _NC_CACHE:
        _NC_CACHE[key] = build(S, L)
    nc = _NC_CACHE[key]
    shared = host_params(inputs)
    shared.update(host_consts())
    in_maps = []
    for b in range(B):
        m = dict(shared)
        m["x"] = np.ascontiguousarray(x[b])
        in_maps.append(m)
    res = run_bass_kernel_spmd(nc, in_maps, core_ids=list(range(B)))
    out = np.stack([np.asarray(r["out"], dtype=np.float32) for r in res.results], axis=0)
    return out
```
